# Optimizing a Trainium2 kernel written in Bass

```python
import math
import jax
import jax.numpy as jnp
from jax import lax
import numpy as np

D_MODEL = 2048
BATCH = 8
SEQ = 2048
DEPTH = 2

MEM_LEN = 256
EPS = 1e-6
ROPE_THETA = 10000.0
Q_BLOCK = 128
NEG_INF = -1e30
HEAD_DIM = 64

SWA_HEADS = 16
SWA_KV_HEADS = 4
SWA_WINDOW = 128
SWA_Q_W = SWA_HEADS * HEAD_DIM
SWA_KV_W = SWA_KV_HEADS * HEAD_DIM

RWKV_HEADS = 16
RWKV_HEAD = 64
RWKV_DIM = RWKV_HEADS * RWKV_HEAD
DECAY_LORA = 64
AAA_LORA = 64
GATE_LORA = 160
RWKV_GN_EPS = 64e-5
RWKV_W = 3 * RWKV_DIM + DECAY_LORA + AAA_LORA + GATE_LORA

MLA_HEADS = 16
MLA_Q_RANK = 512
MLA_KV_RANK = 256
MLA_NOPE = 64
MLA_ROPE = 32
MLA_V = 64

DIFF_HEADS = 8
DIFF_QK = 64
DIFF_V = 2 * DIFF_QK

MEM_HEADS = 4
MEM_HEAD_DIM = 128
MEM_W = MEM_HEADS * MEM_HEAD_DIM

D_FF = 5632

AB_IN = SWA_Q_W + 2 * SWA_KV_W + RWKV_W
AB_OUT = SWA_Q_W + RWKV_DIM
CD_IN = MLA_Q_RANK + MLA_KV_RANK + MLA_ROPE + 2 * (DIFF_HEADS * 2 * DIFF_QK) + DIFF_HEADS * DIFF_V
CD_OUT = MLA_HEADS * MLA_V + DIFF_HEADS * DIFF_V
N_EVEN = (DEPTH + 1) // 2
N_ODD = DEPTH // 2

kernel_name = 'hybrid_swa_rwkv7_mla_diff_macaron'


def rms_norm(x, g):
    xf = x.astype(jnp.float32)
    y = xf * lax.rsqrt(jnp.mean(xf * xf, axis=-1, keepdims=True) + EPS)
    return (y * g.astype(jnp.float32)).astype(x.dtype)


def rope_tables(positions, dim):
    inv = 1.0 / (ROPE_THETA ** (jnp.arange(0, dim, 2, dtype=jnp.float32) / dim))
    ang = positions.astype(jnp.float32)[..., None] * inv
    return jnp.cos(ang), jnp.sin(ang)


def apply_rope(x, cos, sin):
    x1, x2 = jnp.split(x.astype(jnp.float32), 2, axis=-1)
    c = cos[:, :, None, :]
    s = sin[:, :, None, :]
    return jnp.concatenate([x1 * c - x2 * s, x1 * s + x2 * c], axis=-1).astype(x.dtype)


def swiglu(h, w_gate, w_up, w_down):
    return (jax.nn.silu(h @ w_gate) * (h @ w_up)) @ w_down


def sliding_window_sink_attention(q, k, v, sinks):
    B, S, H, d = q.shape
    KV = k.shape[2]
    G = H // KV
    nb = S // Q_BLOCK
    qb = q.reshape(B, nb, Q_BLOCK, KV, G, d)

    def with_prev(t):
        t = t.reshape(B, nb, Q_BLOCK, KV, d)
        prev = jnp.concatenate([jnp.zeros_like(t[:, :1]), t[:, :-1]], axis=1)
        return jnp.concatenate([prev, t], axis=2)

    kk, vv = with_prev(k), with_prev(v)
    s = jnp.einsum('bnqkgd,bnskd->bnkgqs', qb, kk).astype(jnp.float32) * (d ** -0.5)
    k_idx = jnp.arange(2 * Q_BLOCK)[None, :]
    rel = (jnp.arange(Q_BLOCK)[:, None] + Q_BLOCK) - k_idx
    band = (rel >= 0) & (rel < SWA_WINDOW)
    blk_ok = (jnp.arange(nb)[:, None, None] > 0) | (k_idx[None] >= Q_BLOCK)
    valid = (band[None] & blk_ok)[None, :, None, None]
    s = jnp.where(valid, s, NEG_INF)
    sink = sinks.astype(jnp.float32).reshape(1, 1, KV, G, 1, 1)
    m = jnp.maximum(jnp.max(s, axis=-1, keepdims=True), sink)
    p = jnp.exp(s - m)
    p = p / (jnp.sum(p, axis=-1, keepdims=True) + jnp.exp(sink - m))
    o = jnp.einsum('bnkgqs,bnskd->bnqkgd', p.astype(v.dtype), vv)
    return o.reshape(B, S, H, d)


def rwkv7_time_mix(proj, mu, w0, w2, a0, a2, g2, k_k, k_a, r_k, gn_g, gn_b):
    B, S, _ = proj.shape
    f32 = jnp.float32
    p = proj.astype(f32)
    prev = jnp.concatenate([jnp.zeros_like(p[:, :1]), p[:, :-1]], axis=1)
    xs = p + (prev - p) * mu.astype(f32)
    c3 = 3 * RWKV_DIM
    r, k, v, w_lo, a_lo, g_lo = jnp.split(
        xs, [RWKV_DIM, 2 * RWKV_DIM, c3, c3 + DECAY_LORA, c3 + DECAY_LORA + AAA_LORA], axis=-1)
    w = -jax.nn.softplus(-(w0.astype(f32) + jnp.tanh(w_lo) @ w2.astype(f32))) - 0.5
    a = jax.nn.sigmoid(a0.astype(f32) + a_lo @ a2.astype(f32))
    g = jax.nn.sigmoid(g_lo) @ g2.astype(f32)
    hs = (B, S, RWKV_HEADS, RWKV_HEAD)
    r, k, v, w, a = r.reshape(hs), k.reshape(hs), v.reshape(hs), w.reshape(hs), a.reshape(hs)
    kk = k * k_k.astype(f32).reshape(RWKV_HEADS, RWKV_HEAD)
    kk = kk / jnp.maximum(jnp.sqrt(jnp.sum(kk * kk, axis=-1, keepdims=True)), 1e-12)
    k = k * (1.0 + (a - 1.0) * k_a.astype(f32).reshape(RWKV_HEADS, RWKV_HEAD))
    decay = jnp.exp(-jnp.exp(w))

    def step(state, inp):
        r_t, k_t, v_t, d_t, kk_t, a_t = inp
        sa = jnp.einsum('bhij,bhj->bhi', state, -kk_t)
        state = (state * d_t[:, :, None, :] + sa[..., None] * (kk_t * a_t)[:, :, None, :]
                 + v_t[..., None] * k_t[:, :, None, :])
        return state, jnp.einsum('bhij,bhj->bhi', state, r_t)

    sf = lambda t: jnp.swapaxes(t, 0, 1)
    init = jnp.zeros((B, RWKV_HEADS, RWKV_HEAD, RWKV_HEAD), f32)
    _, y = lax.scan(step, init, (sf(r), sf(k), sf(v), sf(decay), sf(kk), sf(a)))
    y = sf(y)
    mean = jnp.mean(y, axis=-1, keepdims=True)
    var = jnp.mean(jnp.square(y - mean), axis=-1, keepdims=True)
    y = ((y - mean) * lax.rsqrt(var + RWKV_GN_EPS)).reshape(B, S, RWKV_DIM) * gn_g.astype(f32) + gn_b.astype(f32)
    bonus = jnp.sum(r * k * r_k.astype(f32).reshape(RWKV_HEADS, RWKV_HEAD), axis=-1, keepdims=True) * v
    out = (y + bonus.reshape(B, S, RWKV_DIM)) * g
    return out.astype(proj.dtype)


def causal_block_probs(q_blk, k, q_start):
    s = jnp.einsum('bqhd,bshd->bhqs', q_blk, k).astype(jnp.float32)
    q_pos = q_start + jnp.arange(Q_BLOCK)
    mask = jnp.arange(k.shape[1])[None, :] <= q_pos[:, None]
    return jax.nn.softmax(jnp.where(mask, s, NEG_INF), axis=-1)


def causal_block_attention(q, k, v, scale):
    B, S, H, d = q.shape
    nb = S // Q_BLOCK
    qb = jnp.swapaxes((q * scale).reshape(B, nb, Q_BLOCK, H, d), 0, 1)

    def one(args):
        i, q_blk = args
        p = causal_block_probs(q_blk, k, i * Q_BLOCK)
        return jnp.einsum('bhqs,bshd->bqhd', p.astype(v.dtype), v)

    o = lax.map(one, (jnp.arange(nb), qb))
    return jnp.swapaxes(o, 0, 1).reshape(B, S, H, v.shape[-1])


def mla_attention(c_q, c_kv, k_pe, cq_norm, ckv_norm, w_uq, w_ukv,
                  q_nope_norm, k_nope_norm, q_rope_norm, k_rope_norm, cos, sin):
    B, S, _ = c_q.shape
    q = (rms_norm(c_q, cq_norm) @ w_uq).reshape(B, S, MLA_HEADS, MLA_NOPE + MLA_ROPE)
    kv = (rms_norm(c_kv, ckv_norm) @ w_ukv).reshape(B, S, MLA_HEADS, MLA_NOPE + MLA_V)
    q_nope, q_pe = jnp.split(q, [MLA_NOPE], axis=-1)
    k_nope, v = jnp.split(kv, [MLA_NOPE], axis=-1)
    q_nope = rms_norm(q_nope, q_nope_norm)
    k_nope = rms_norm(k_nope, k_nope_norm)
    q_pe = apply_rope(rms_norm(q_pe, q_rope_norm), cos, sin)
    k_pe = apply_rope(rms_norm(k_pe.reshape(B, S, 1, MLA_ROPE), k_rope_norm), cos, sin)
    q = jnp.concatenate([q_nope, q_pe], axis=-1)
    k = jnp.concatenate([k_nope, jnp.broadcast_to(k_pe, (B, S, MLA_HEADS, MLA_ROPE))], axis=-1)
    o = causal_block_attention(q, k, v, (MLA_NOPE + MLA_ROPE) ** -0.5)
    return o.reshape(B, S, MLA_HEADS * MLA_V)


def differential_attention(dq, dk, dv, q_norm, k_norm, lq1, lk1, lq2, lk2, subln, lambda_init, cos, sin):
    B, S, _ = dq.shape
    f32 = jnp.float32
    q = apply_rope(rms_norm(dq.reshape(B, S, 2 * DIFF_HEADS, DIFF_QK), q_norm), cos, sin) * (DIFF_QK ** -0.5)
    k = apply_rope(rms_norm(dk.reshape(B, S, 2 * DIFF_HEADS, DIFF_QK), k_norm), cos, sin)
    q = q.reshape(B, S, DIFF_HEADS, 2, DIFF_QK)
    k = k.reshape(B, S, DIFF_HEADS, 2, DIFF_QK)
    k1, k2 = k[:, :, :, 0], k[:, :, :, 1]
    v = dv.reshape(B, S, DIFF_HEADS, DIFF_V)
    lam = (jnp.exp(jnp.sum(lq1.astype(f32) * lk1.astype(f32)))
           - jnp.exp(jnp.sum(lq2.astype(f32) * lk2.astype(f32))) + lambda_init)
    nb = S // Q_BLOCK
    qb = jnp.swapaxes(q.reshape(B, nb, Q_BLOCK, DIFF_HEADS, 2, DIFF_QK), 0, 1)

    def one(args):
        i, q_blk = args
        p1 = causal_block_probs(q_blk[:, :, :, 0], k1, i * Q_BLOCK)
        p2 = causal_block_probs(q_blk[:, :, :, 1], k2, i * Q_BLOCK)
        return jnp.einsum('bhqs,bshd->bqhd', (p1 - lam * p2).astype(v.dtype), v)

    o = jnp.swapaxes(lax.map(one, (jnp.arange(nb), qb)), 0, 1).reshape(B, S, DIFF_HEADS, DIFF_V)
    o = rms_norm(o, subln) * (1.0 - lambda_init)
    return o.reshape(B, S, DIFF_HEADS * DIFF_V)


def memory_cross_attention(h, mem_k, mem_v, w_q, q_norm, w_o):
    B, S, _ = h.shape
    q = rms_norm((h @ w_q).reshape(B, S, MEM_HEADS, MEM_HEAD_DIM), q_norm)
    s = jnp.einsum('bshd,bmhd->bhsm', q, mem_k).astype(jnp.float32) * (MEM_HEAD_DIM ** -0.5)
    p = jax.nn.softmax(s, axis=-1)
    o = jnp.einsum('bhsm,bmhd->bshd', p.astype(mem_v.dtype), mem_v).reshape(B, S, MEM_W)
    return o @ w_o


def setup_inputs(seed: int = 0) -> dict:
    key = jax.random.key(seed)
    ks = iter(jax.random.split(key, 64))
    f32 = jnp.float32

    def normal(shape, scale):
        return jax.random.normal(next(ks), shape, f32) * scale

    def gain(shape):
        return 1.0 + 0.02 * jax.random.normal(next(ks), shape, f32)

    D, F, E, O = D_MODEL, D_FF, N_EVEN, N_ODD
    x = normal((BATCH, SEQ, D), 1.0)
    mem = normal((BATCH, MEM_LEN, D), 1.0)
    positions = (jax.random.randint(next(ks), (BATCH, 1), 0, 4096) + jnp.arange(SEQ)[None, :]).astype(jnp.int32)
    return {
        'x': x, 'mem': mem, 'positions': positions,
        'ffn1_norm': gain((DEPTH, D)),
        'ffn1_w_gate': normal((DEPTH, D, F), D ** -0.5),
        'ffn1_w_up': normal((DEPTH, D, F), D ** -0.5),
        'ffn1_w_down': normal((DEPTH, F, D), F ** -0.5),
        'mix_norm': gain((DEPTH, D)),
        'ab_w_in': normal((E, D, AB_IN), D ** -0.5),
        'ab_w_out': normal((E, AB_OUT, D), AB_OUT ** -0.5),
        'swa_q_norm': gain((E, HEAD_DIM)),
        'swa_k_norm': gain((E, HEAD_DIM)),
        'swa_sinks': normal((E, SWA_HEADS), 1.0),
        'rwkv_mu': jax.random.uniform(next(ks), (E, RWKV_W), f32),
        'rwkv_w0': jax.random.uniform(next(ks), (E, RWKV_DIM), f32, -6.0, -1.0),
        'rwkv_w2': normal((E, DECAY_LORA, RWKV_DIM), 0.5 * DECAY_LORA ** -0.5),
        'rwkv_a0': normal((E, RWKV_DIM), 0.1),
        'rwkv_a2': normal((E, AAA_LORA, RWKV_DIM), AAA_LORA ** -0.5),
        'rwkv_g2': normal((E, GATE_LORA, RWKV_DIM), GATE_LORA ** -0.5),
        'rwkv_k_k': 0.85 + normal((E, RWKV_DIM), 0.02),
        'rwkv_k_a': gain((E, RWKV_DIM)),
        'rwkv_r_k': normal((E, RWKV_DIM), 0.1),
        'rwkv_gn_g': gain((E, RWKV_DIM)),
        'rwkv_gn_b': normal((E, RWKV_DIM), 0.01),
        'cd_w_in': normal((O, D, CD_IN), D ** -0.5),
        'cd_w_out': normal((O, CD_OUT, D), CD_OUT ** -0.5),
        'mla_cq_norm': gain((O, MLA_Q_RANK)),
        'mla_ckv_norm': gain((O, MLA_KV_RANK)),
        'mla_w_uq': normal((O, MLA_Q_RANK, MLA_HEADS * (MLA_NOPE + MLA_ROPE)), MLA_Q_RANK ** -0.5),
        'mla_w_ukv': normal((O, MLA_KV_RANK, MLA_HEADS * (MLA_NOPE + MLA_V)), MLA_KV_RANK ** -0.5),
        'mla_q_nope_norm': gain((O, MLA_NOPE)),
        'mla_k_nope_norm': gain((O, MLA_NOPE)),
        'mla_q_rope_norm': gain((O, MLA_ROPE)),
        'mla_k_rope_norm': gain((O, MLA_ROPE)),
        'diff_q_norm': gain((O, DIFF_QK)),
        'diff_k_norm': gain((O, DIFF_QK)),
        'diff_lq1': normal((O, DIFF_QK), 0.1),
        'diff_lk1': normal((O, DIFF_QK), 0.1),
        'diff_lq2': normal((O, DIFF_QK), 0.1),
        'diff_lk2': normal((O, DIFF_QK), 0.1),
        'diff_subln': gain((O, DIFF_V)),
        'memx_norm': gain((DEPTH, D)),
        'memx_w_q': normal((DEPTH, D, MEM_W), D ** -0.5),
        'memx_q_norm': gain((DEPTH, MEM_HEAD_DIM)),
        'memx_w_o': normal((DEPTH, MEM_W, D), MEM_W ** -0.5),
        'mem_norm': gain((D,)),
        'mem_w_kv': normal((D, 2 * MEM_W), D ** -0.5),
        'mem_k_norm': gain((MEM_HEAD_DIM,)),
        'ffn2_norm': gain((DEPTH, D)),
        'ffn2_w_gate': normal((DEPTH, D, F), D ** -0.5),
        'ffn2_w_up': normal((DEPTH, D, F), D ** -0.5),
        'ffn2_w_down': normal((DEPTH, F, D), F ** -0.5),
    }


def reference(x, mem, positions, ffn1_norm, ffn1_w_gate, ffn1_w_up, ffn1_w_down, mix_norm,
              ab_w_in, ab_w_out, swa_q_norm, swa_k_norm, swa_sinks,
              rwkv_mu, rwkv_w0, rwkv_w2, rwkv_a0, rwkv_a2, rwkv_g2, rwkv_k_k, rwkv_k_a, rwkv_r_k,
              rwkv_gn_g, rwkv_gn_b,
              cd_w_in, cd_w_out, mla_cq_norm, mla_ckv_norm, mla_w_uq, mla_w_ukv,
              mla_q_nope_norm, mla_k_nope_norm, mla_q_rope_norm, mla_k_rope_norm,
              diff_q_norm, diff_k_norm, diff_lq1, diff_lk1, diff_lq2, diff_lk2, diff_subln,
              memx_norm, memx_w_q, memx_q_norm, memx_w_o, mem_norm, mem_w_kv, mem_k_norm,
              ffn2_norm, ffn2_w_gate, ffn2_w_up, ffn2_w_down):
    B, S, _ = x.shape
    M = mem.shape[1]
    cos64, sin64 = rope_tables(positions, HEAD_DIM)
    cos32, sin32 = rope_tables(positions, MLA_ROPE)

    mem_k, mem_v = jnp.split(rms_norm(mem, mem_norm) @ mem_w_kv, 2, axis=-1)
    mem_k = rms_norm(mem_k.reshape(B, M, MEM_HEADS, MEM_HEAD_DIM), mem_k_norm)
    mem_v = mem_v.reshape(B, M, MEM_HEADS, MEM_HEAD_DIM)

    ab_split = [SWA_Q_W, SWA_Q_W + SWA_KV_W, SWA_Q_W + 2 * SWA_KV_W]
    c1 = MLA_Q_RANK
    c2 = c1 + MLA_KV_RANK
    c3 = c2 + MLA_ROPE
    c4 = c3 + DIFF_HEADS * 2 * DIFF_QK
    c5 = c4 + DIFF_HEADS * 2 * DIFF_QK
    cd_split = [c1, c2, c3, c4, c5]

    for layer in range(DEPTH):
        x = x + 0.5 * swiglu(rms_norm(x, ffn1_norm[layer]), ffn1_w_gate[layer], ffn1_w_up[layer], ffn1_w_down[layer])
        h = rms_norm(x, mix_norm[layer])
        j = layer // 2
        if layer % 2 == 0:
            u = h @ ab_w_in[j]
            qa, ka, va, rw = jnp.split(u, ab_split, axis=-1)
            q = apply_rope(rms_norm(qa.reshape(B, S, SWA_HEADS, HEAD_DIM), swa_q_norm[j]), cos64, sin64)
            k = apply_rope(rms_norm(ka.reshape(B, S, SWA_KV_HEADS, HEAD_DIM), swa_k_norm[j]), cos64, sin64)
            v = va.reshape(B, S, SWA_KV_HEADS, HEAD_DIM)
            y_a = sliding_window_sink_attention(q, k, v, swa_sinks[j]).reshape(B, S, SWA_Q_W)
            y_b = rwkv7_time_mix(rw, rwkv_mu[j], rwkv_w0[j], rwkv_w2[j], rwkv_a0[j], rwkv_a2[j], rwkv_g2[j],
                                 rwkv_k_k[j], rwkv_k_a[j], rwkv_r_k[j], rwkv_gn_g[j], rwkv_gn_b[j])
            mixed = jnp.concatenate([y_a, y_b], axis=-1) @ ab_w_out[j]
        else:
            u = h @ cd_w_in[j]
            c_q, c_kv, k_pe, dq, dk, dv = jnp.split(u, cd_split, axis=-1)
            y_c = mla_attention(c_q, c_kv, k_pe, mla_cq_norm[j], mla_ckv_norm[j], mla_w_uq[j], mla_w_ukv[j],
                                mla_q_nope_norm[j], mla_k_nope_norm[j], mla_q_rope_norm[j], mla_k_rope_norm[j],
                                cos32, sin32)
            lambda_init = 0.8 - 0.6 * math.exp(-0.3 * layer)
            y_d = differential_attention(dq, dk, dv, diff_q_norm[j], diff_k_norm[j], diff_lq1[j], diff_lk1[j],
                                         diff_lq2[j], diff_lk2[j], diff_subln[j], lambda_init, cos64, sin64)
            mixed = jnp.concatenate([y_c, y_d], axis=-1) @ cd_w_out[j]
        x = x + mixed
        x = x + memory_cross_attention(rms_norm(x, memx_norm[layer]), mem_k, mem_v,
                                       memx_w_q[layer], memx_q_norm[layer], memx_w_o[layer])
        x = x + 0.5 * swiglu(rms_norm(x, ffn2_norm[layer]), ffn2_w_gate[layer], ffn2_w_up[layer], ffn2_w_down[layer])
    return x
```

```python
import math
from contextlib import ExitStack
import numpy as np
import ml_dtypes
import concourse.bass as bass
import concourse.mybir as mybir
from concourse.bass_utils import run_bass_kernel_spmd

F32 = mybir.dt.float32
BF16 = mybir.dt.bfloat16
I32 = mybir.dt.int32
ALU = mybir.AluOpType
AF = mybir.ActivationFunctionType
AX = mybir.AxisListType

S = 2048
D = 2048
NB = S // 128
FF = 5632
EPS = 1e-6
PI = math.pi


class Buf:
    __slots__ = ("name", "w", "r", "excl")

    def __init__(self, name, excl=False):
        self.name = name
        self.w = None
        self.r = []
        self.excl = excl


class V:
    __slots__ = ("buf", "ap")

    def __init__(self, buf, ap):
        self.buf = buf
        self.ap = ap

    def __getitem__(self, k):
        return V(self.buf, self.ap[k])

    def re(self, s, **kw):
        return V(self.buf, self.ap.rearrange(s, **kw))

    def bc(self, shape):
        return V(self.buf, self.ap.to_broadcast(list(shape)))

    def un(self, axis):
        return V(self.buf, self.ap.unsqueeze(axis))

    def pb(self, n):
        return V(self.buf, self.ap.partition_broadcast(n))


class FW:
    NDMA = 6

    def __init__(self, nc):
        self.nc = nc
        self.E = {"pe": nc.tensor, "act": nc.scalar, "dve": nc.vector, "pool": nc.gpsimd, "sp": nc.sync}
        self.sems = {}
        self.cnt = {}
        for e in self.E:
            self.sems[e] = nc.alloc_semaphore("s_" + e)
            self.cnt[e] = 0
        self.dq = {}
        for q in ("sp", "act", "pool"):
            lst = []
            for i in range(self.NDMA):
                k = "d_%s%d" % (q, i)
                self.sems[k] = nc.alloc_semaphore(k)
                self.cnt[k] = 0
                lst.append(k)
            self.dq[q] = [lst, 0]
        self.waited = {e: {} for e in self.E}
        self.stack = None
        self.cache = {}
        self.uid = 0

    def phase(self):
        fw = self

        class P:
            def __enter__(s):
                fw.stack = ExitStack()
                fw.stack.__enter__()
                fw.cache = {}
                return fw

            def __exit__(s, *a):
                fw.barrier()
                fw.stack.close()
                fw.stack = None
                return False
        return P()

    def _nm(self, name):
        self.uid += 1
        return "%s_%d" % (name, self.uid)

    def sb(self, name, shape, dt, persistent=False):
        nm = self._nm(name)
        if persistent or self.stack is None:
            t = self.nc.alloc_sbuf_tensor(nm, list(shape), dt)
        else:
            t = self.stack.enter_context(self.nc.sbuf_tensor(nm, list(shape), dt))
        return V(Buf(nm), t.ap())

    def tmp(self, name, shape, dt):
        key = (name, tuple(shape), str(dt))
        if key not in self.cache:
            self.cache[key] = self.sb(name, shape, dt)
        return self.cache[key]

    def ps(self, name, shape, dt):
        nm = self._nm(name)
        t = self.stack.enter_context(self.nc.psum_tensor(nm, list(shape), dt))
        return V(Buf(nm), t.ap())

    def dram(self, name, shape, dt, kind="Internal"):
        t = self.nc.dram_tensor(name, list(shape), dt, kind=kind)
        return V(Buf(name), t.ap())

    def _wait(self, eng, tok):
        if tok is None:
            return
        k, v = tok
        w = self.waited[eng]
        if w.get(k, 0) >= v:
            return
        if k == eng and eng == "pe":
            return
        self.E[eng].wait_ge(self.sems[k], v)
        w[k] = v

    def _deps(self, eng, reads, writes):
        for x in reads:
            self._wait(eng, x.buf.w)
            if x.buf.excl:
                for t in x.buf.r:
                    self._wait(eng, t)
        for x in writes:
            b = x.buf
            self._wait(eng, b.w)
            for t in b.r:
                self._wait(eng, t)

    def _commit(self, tok, reads, writes):
        for x in reads:
            if x.buf.excl:
                x.buf.w = tok
                x.buf.r = []
                continue
            r = x.buf.r
            r.append(tok)
            if len(r) > 16:
                m = {}
                for k, v in r:
                    if m.get(k, 0) < v:
                        m[k] = v
                x.buf.r = list(m.items())
        for x in writes:
            x.buf.w = tok
            x.buf.r = []

    def op(self, eng, fn, reads, writes):
        self._deps(eng, reads, writes)
        ins = fn()
        self.cnt[eng] += 1
        ins.then_inc(self.sems[eng], 1)
        tok = (eng, self.cnt[eng])
        self._commit(tok, reads, writes)
        return tok

    def dma(self, q, out, in_, **kw):
        lst, i = self.dq[q]
        k = lst[i % len(lst)]
        self.dq[q][1] = i + 1
        self._wait(q, (k, self.cnt[k]))
        self._deps(q, [in_], [out])
        ins = self.E[q].dma_start(out=out.ap, in_=in_.ap, **kw)
        self.cnt[k] += 16
        ins.then_inc(self.sems[k], 16)
        tok = (k, self.cnt[k])
        self._commit(tok, [in_], [out])
        return tok

    def barrier(self):
        for e in self.E:
            for k in self.sems:
                if self.cnt[k] > 0:
                    self._wait(e, (k, self.cnt[k]))

    def mm(self, out, lhsT, rhs, start=True, stop=True):
        return self.op("pe", lambda: self.nc.tensor.matmul(out.ap, lhsT.ap, rhs.ap, start=start, stop=stop),
                       [lhsT, rhs], [out])

    def tr(self, out, in_, ident):
        return self.op("pe", lambda: self.nc.tensor.transpose(out.ap, in_.ap, ident.ap), [in_, ident], [out])

    def act(self, out, in_, func, bias=None, scale=1.0, accum=None):
        reads = [in_]
        kw = {}
        if isinstance(bias, V):
            reads.append(bias)
            kw["bias"] = bias.ap
        elif bias is not None:
            kw["bias"] = bias
        if isinstance(scale, V):
            reads.append(scale)
            kw["scale"] = scale.ap
        else:
            kw["scale"] = scale
        writes = [out]
        if accum is not None:
            kw["accum_out"] = accum.ap
            writes.append(accum)
        return self.op("act", lambda: self.nc.scalar.activation(out.ap, in_.ap, func, **kw), reads, writes)

    def tt(self, eng, out, a, b, op):
        e = self.E[eng]
        return self.op(eng, lambda: e.tensor_tensor(out.ap, a.ap, b.ap, op), [a, b], [out])

    def ts(self, eng, out, a, s1, op0, s2=None, op1=None):
        e = self.E[eng]
        reads = [a]
        if isinstance(s1, V):
            reads.append(s1)
            s1 = s1.ap
        if isinstance(s2, V):
            reads.append(s2)
            s2 = s2.ap
        if op1 is None:
            s2 = 0.0
            op1 = ALU.add
        return self.op(eng, lambda: e.tensor_scalar(out.ap, a.ap, s1, s2, op0, op1), reads, [out])

    def stt(self, eng, out, a, s, b, op0, op1):
        e = self.E[eng]
        reads = [a, b]
        if isinstance(s, V):
            reads.append(s)
            s = s.ap
        return self.op(eng, lambda: e.scalar_tensor_tensor(out.ap, a.ap, s, b.ap, op0, op1), reads, [out])

    def copy(self, eng, out, in_):
        if eng == "act":
            return self.op("act", lambda: self.nc.scalar.copy(out.ap, in_.ap), [in_], [out])
        e = self.E[eng]
        return self.op(eng, lambda: e.tensor_copy(out.ap, in_.ap), [in_], [out])

    def red(self, eng, out, in_, op=ALU.add, axis=AX.X):
        e = self.E[eng]
        return self.op(eng, lambda: e.tensor_reduce(out.ap, in_.ap, axis, op), [in_], [out])

    def memset(self, eng, out, val):
        e = self.E[eng]
        return self.op(eng, lambda: e.memset(out.ap, val), [], [out])

    def recip(self, out, in_):
        return self.op("dve", lambda: self.nc.vector.reciprocal(out.ap, in_.ap), [in_], [out])


class Rot:
    def __init__(self, items):
        self.items = items
        self.i = 0

    def next(self):
        x = self.items[self.i % len(self.items)]
        self.i += 1
        return x


def wview(W, c0, c1, r0=0, r1=None):
    ap = W.ap
    if r1 is None:
        r1 = ap.shape[0]
    return V(W.buf, ap[r0:r1, c0:c1].rearrange("(k p) c -> p k c", p=128))


def rstd_from_ss(fw, ss, n, eps, tmp, out):
    fw.act(tmp, ss, AF.Sqrt, bias=fw.eps_tiles[eps], scale=1.0 / n)
    fw.recip(out, tmp)


def norm_T(fw, C, src_blks, K, gain_bc, dstT, psT, nblk=None, dst_off=0, cast_only=False, src_bf16=False):
    nblk = len(src_blks) if nblk is None else nblk
    KC = K // 128
    xt = Rot([fw.sb("nt_x", [128, K], BF16 if src_bf16 else F32) for _ in range(2)])
    hb = Rot([fw.sb("nt_h", [128, K], BF16) for _ in range(2)])
    junk = fw.sb("nt_j", [128, K], BF16)
    st = Rot([fw.sb("nt_s", [128, 4], F32) for _ in range(2)])
    ev = Rot(["dve", "act"])
    for i in range(nblk):
        x = xt.next()
        fw.dma("sp", x, src_blks[i])
        if cast_only:
            h = x if src_bf16 else hb.next()
            if not src_bf16:
                fw.copy("dve", h, x)
        else:
            s = st.next()
            h = hb.next()
            fw.act(junk, x, AF.Square, accum=s[:, 0:1])
            rstd_from_ss(fw, s[:, 0:1], K, EPS, s[:, 1:2], s[:, 2:3])
            fw.stt("dve", h, x, s[:, 2:3], gain_bc, ALU.mult, ALU.mult)
        for k0 in range(0, KC, 8):
            n = min(8, KC - k0)
            p = psT.next()
            for j in range(n):
                fw.tr(p[:, j, :], h[:, (k0 + j) * 128:(k0 + j + 1) * 128], C.ident)
            fw.copy(ev.next(), dstT[:, k0:k0 + n, dst_off + i * 128: dst_off + (i + 1) * 128], p[:, 0:n, :])


def load_bc(fw, name, vec, n, q="sp", mul=None):
    t = fw.sb(name, [128, n], F32)
    fw.dma(q, t, vec.pb(128))
    if mul is not None:
        fw.ts("dve", t, t, float(mul), ALU.mult)
    return t


def head_norm(fw, src, gain_bc, out, H, d, eng="dve", eps=EPS):
    sq = fw.tmp("hn_sq", [128, H, d], F32)
    st = fw.tmp("hn_st", [128, 3, H], F32)
    fw.act(sq, src, AF.Square)
    fw.red("dve", st[:, 0, :], sq)
    rstd_from_ss(fw, st[:, 0, :], d, eps, st[:, 1, :], st[:, 2, :])
    fw.tt(eng, sq, src, st[:, 2, :].un(2).bc([128, H, d]), ALU.mult)
    fw.tt(eng, out, sq, gain_bc.un(1).bc([128, H, d]), ALU.mult)


def rope(fw, x, out, cos, sin, H, d, eng="pool"):
    h = d // 2
    t1 = fw.tmp("rp_1", [128, H, h], F32)
    t2 = fw.tmp("rp_2", [128, H, h], F32)
    cb = cos.un(1).bc([128, H, h])
    sbb = sin.un(1).bc([128, H, h])
    x1 = x[:, :, 0:h]
    x2 = x[:, :, h:d]
    fw.tt(eng, t1, x1, cb, ALU.mult)
    fw.tt(eng, t2, x2, sbb, ALU.mult)
    fw.tt(eng, out[:, :, 0:h], t1, t2, ALU.subtract)
    fw.tt(eng, t1, x1, sbb, ALU.mult)
    fw.tt(eng, t2, x2, cb, ALU.mult)
    fw.tt(eng, out[:, :, h:d], t1, t2, ALU.add)


def ffn_phase(fw, C, xs_blk, norm_g, wg, wu, wd, T=1024, NF=4):
    FC = FF // 128
    FG = FC // NF
    with fw.phase():
        gain = load_bc(fw, "ffn_g", norm_g, D)
        hT = fw.sb("hT", [128, 16, T], BF16)
        actT = fw.sb("actT", [128, FG, T], BF16)
        wgb = Rot([fw.sb("wgb", [128, 16, 256], BF16) for _ in range(2)])
        wub = Rot([fw.sb("wub", [128, 16, 256], BF16) for _ in range(2)])
        wdb = Rot([fw.sb("wdb", [128, FG, 512], BF16) for _ in range(2)])
        sg = Rot([fw.sb("sg", [128, 512], F32) for _ in range(2)])
        xo = Rot([fw.sb("xo", [128, 512], F32) for _ in range(3)])
        xn = Rot([fw.sb("xn", [128, 512], F32) for _ in range(3)])
        psT = Rot([fw.ps("psT", [128, 8, 128], BF16) for _ in range(1)])
        psg = Rot([fw.ps("psg", [128, 512], F32) for _ in range(2)])
        psu = Rot([fw.ps("psu", [128, 512], F32) for _ in range(2)])
        pso = Rot([fw.ps("pso", [128, 512], F32) for _ in range(2)])
        for tt in range(S // T):
            nsub = T // 128
            norm_T(fw, C, xs_blk[tt * nsub:(tt + 1) * nsub], D, gain, hT, psT)
            for fg in range(NF):
                for fp in range(FG // 2 + FG % 2):
                    nj = min(2, FG - fp * 2)
                    c0 = (fg * FG + fp * 2) * 128
                    g_ = wgb.next()
                    u_ = wub.next()
                    fw.dma("pool", g_[:, :, 0:nj * 128], wview(wg, c0, c0 + nj * 128))
                    fw.dma("pool", u_[:, :, 0:nj * 128], wview(wu, c0, c0 + nj * 128))
                    for j in range(nj):
                        fl = fp * 2 + j
                        for th in range(T // 512):
                            pg = psg.next()
                            pu = psu.next()
                            tk = slice(th * 512, (th + 1) * 512)
                            for k in range(16):
                                fw.mm(pg, g_[:, k, j * 128:(j + 1) * 128], hT[:, k, tk], start=(k == 0), stop=(k == 15))
                            for k in range(16):
                                fw.mm(pu, u_[:, k, j * 128:(j + 1) * 128], hT[:, k, tk], start=(k == 0), stop=(k == 15))
                            s_ = sg.next()
                            fw.act(s_, pg, AF.Silu)
                            fw.tt("dve", actT[:, fl, tk], s_, pu, ALU.mult)
                for db in range(4):
                    w_ = wdb.next()
                    fw.dma("pool", w_, wview(wd, db * 512, (db + 1) * 512, fg * FG * 128, (fg + 1) * FG * 128))
                    for sub in range(nsub):
                        blk = xs_blk[tt * nsub + sub]
                        po = pso.next()
                        for f in range(FG):
                            fw.mm(po, actT[:, f, sub * 128:(sub + 1) * 128], w_[:, f, :], start=(f == 0), stop=(f == FG - 1))
                        o_ = xo.next()
                        n_ = xn.next()
                        fw.dma("sp", o_, blk[:, db * 512:(db + 1) * 512])
                        fw.stt("dve", n_, po, 0.5, o_, ALU.mult, ALU.add)
                        fw.dma("sp", blk[:, db * 512:(db + 1) * 512], n_)


def linear_phase(fw, C, src_blks, K, norm_g, W, ncols, evac, cast_only=False, src_bf16=False, extra=None):
    KC = K // 128
    with fw.phase():
        gain = None if cast_only else load_bc(fw, "lin_g", norm_g, K)
        hT = fw.sb("lhT", [128, KC, S], BF16)
        psT = Rot([fw.ps("lpsT", [128, 8, 128], BF16) for _ in range(2)])
        pso = Rot([fw.ps("lpso", [128, 512], F32) for _ in range(4)])
        wb = Rot([fw.sb("lwb", [128, KC, 512], BF16) for _ in range(2)])
        ctx = extra(fw) if extra is not None else None
        norm_T(fw, C, src_blks, K, gain, hT, psT, cast_only=cast_only, src_bf16=src_bf16)
        for c0 in range(0, ncols, 512):
            c1 = min(ncols, c0 + 512)
            w_ = wb.next()
            fw.dma("pool", w_[:, :, 0:c1 - c0], wview(W, c0, c1))
            for i in range(NB):
                po = pso.next()
                for k in range(KC):
                    fw.mm(po[:, 0:c1 - c0], hT[:, k, i * 128:(i + 1) * 128], w_[:, k, 0:c1 - c0], start=(k == 0), stop=(k == KC - 1))
                evac(fw, i, c0, c1, po[:, 0:c1 - c0], ctx)


def evac_store(dst_blks):
    def mk(fw):
        return {"t": Rot([fw.sb("ev_t", [128, 512], F32) for _ in range(3)]), "e": Rot(["act", "dve"])}

    def ev(fw, i, c0, c1, po, ctx):
        t = ctx["t"].next()
        fw.copy(ctx["e"].next(), t[:, 0:c1 - c0], po)
        fw.dma("sp", dst_blks[i][:, c0:c1], t[:, 0:c1 - c0])
    return mk, ev


def evac_resid(xs_blk):
    def mk(fw):
        return {"o": Rot([fw.sb("er_o", [128, 512], F32) for _ in range(3)]),
                "n": Rot([fw.sb("er_n", [128, 512], F32) for _ in range(3)])}

    def ev(fw, i, c0, c1, po, ctx):
        o_ = ctx["o"].next()
        n_ = ctx["n"].next()
        fw.dma("sp", o_, xs_blk[i][:, c0:c1])
        fw.tt("dve", n_, po, o_, ALU.add)
        fw.dma("sp", xs_blk[i][:, c0:c1], n_)
    return mk, ev


def setup_phase(fw, C, I):
    C.ident = fw.sb("ident", [128, 128], BF16, persistent=True)
    C.maskU = fw.sb("maskU", [128, 128], BF16, persistent=True)
    C.maskL = fw.sb("maskL", [128, 128], BF16, persistent=True)
    C.cos64 = fw.sb("cos64", [128, NB, 32], F32, persistent=True)
    C.sin64 = fw.sb("sin64", [128, NB, 32], F32, persistent=True)
    C.cos32 = fw.sb("cos32", [128, NB, 16], F32, persistent=True)
    C.sin32 = fw.sb("sin32", [128, NB, 16], F32, persistent=True)
    C.memKT = fw.sb("memKT", [128, 4, 256], BF16, persistent=True)
    C.memV1 = fw.sb("memV1", [128, 2, 4, 130], BF16, persistent=True)
    C.cst = fw.sb("cst", [128, 8], F32, persistent=True)
    fw.eps_tiles = {}
    for j, e in enumerate([EPS, 64e-5, 0.0]):
        fw.memset("dve", C.cst[:, j:j + 1], e)
        fw.eps_tiles[e] = C.cst[:, j:j + 1]
    fw.dma("sp", C.ident, I["c_ident"])
    fw.dma("sp", C.maskU, I["c_masku"])
    fw.dma("sp", C.maskL, I["c_maskl"])
    with fw.phase():
        posi = fw.sb("posi", [128, NB], I32)
        posf = fw.sb("posf", [128, NB], F32)
        fw.dma("sp", posi, I["pos"])
        fw.copy("dve", posf, posi)
        for (d2, invn, ct, st_) in ((32, "c_inv64", C.cos64, C.sin64), (16, "c_inv32", C.cos32, C.sin32)):
            inv = fw.sb("inv", [128, d2], F32)
            fw.dma("sp", inv, I[invn])
            ang = fw.sb("ang", [128, NB, d2], F32)
            a2 = fw.sb("ang2", [128, NB, d2], F32)
            ni = fw.sb("angi", [128, NB, d2], I32)
            nf = fw.sb("angf", [128, NB, d2], F32)
            fw.tt("dve", ang, posf.un(2).bc([128, NB, d2]), inv.un(1).bc([128, NB, d2]), ALU.mult)
            for (off, dst) in ((0.0, st_), (PI / 2, ct)):
                fw.ts("dve", a2, ang, off, ALU.add)
                fw.ts("dve", nf, a2, 1.0 / (2 * PI), ALU.mult)
                fw.copy("dve", ni, nf)
                fw.copy("dve", nf, ni)
                fw.stt("dve", a2, nf, -2 * PI, a2, ALU.mult, ALU.add)
                fw.ts("dve", a2, a2, 3.1415925, ALU.min, -3.1415925, ALU.max)
                fw.act(dst, a2, AF.Sin)
    with fw.phase():
        gain = load_bc(fw, "mem_g", I["mem_norm"], D)
        gk = load_bc(fw, "mem_gk", I["mem_k_norm"], 128)
        memT = fw.sb("memT", [128, 16, 256], BF16)
        wkv = fw.sb("wkv", [128, 16, 1024], BF16)
        psT = Rot([fw.ps("mpsT", [128, 8, 128], BF16) for _ in range(2)])
        pso = Rot([fw.ps("mpso", [128, 512], F32) for _ in range(2)])
        kf = fw.sb("mkf", [128, 4, 128], BF16)
        mem_blks = [I["mem"][i * 128:(i + 1) * 128, :] for i in range(2)]
        fw.dma("pool", wkv, wview(I["mem_w_kv"], 0, 1024))
        fw.memset("dve", C.memV1, 1.0)
        norm_T(fw, C, mem_blks, D, gain, memT, psT)
        for mt in range(2):
            pk = pso.next()
            for k in range(16):
                fw.mm(pk, memT[:, k, mt * 128:(mt + 1) * 128], wkv[:, k, 0:512], start=(k == 0), stop=(k == 15))
            head_norm(fw, pk.re("p (h d) -> p h d", h=4), gk, kf, 4, 128)
            p = psT.next()
            for h in range(4):
                fw.tr(p[:, h, :], kf[:, h, :], C.ident)
            fw.copy("dve", C.memKT[:, :, mt * 128:(mt + 1) * 128], p[:, 0:4, :])
            pv = pso.next()
            for k in range(16):
                fw.mm(pv, memT[:, k, mt * 128:(mt + 1) * 128], wkv[:, k, 512:1024], start=(k == 0), stop=(k == 15))
            fw.copy("dve", C.memV1[:, mt, :, 0:128], pv.re("p (h d) -> p h d", h=4))


def memx_phase(fw, C, I, layer, xs_blk):
    with fw.phase():
        gain = load_bc(fw, "mx_g", I["memx_norm"][layer], D)
        gq = load_bc(fw, "mx_gq", I["memx_q_norm"][layer], 128, mul=128 ** -0.5)
        hT = fw.sb("mx_hT", [128, 16, S], BF16)
        wq = fw.sb("mx_wq", [128, 16, 512], BF16)
        wo = fw.sb("mx_wo", [128, 4, 2048], BF16)
        fw.dma("pool", wq, wview(I["memx_w_q"][layer], 0, 512))
        fw.dma("pool", wo, wview(I["memx_w_o"][layer], 0, 2048))
        psT = Rot([fw.ps("mx_psT", [128, 8, 128], BF16) for _ in range(2)])
        psq = Rot([fw.ps("mx_psq", [128, 512], F32) for _ in range(2)])
        pss = Rot([fw.ps("mx_pss", [128, 2, 256], F32) for _ in range(2)])
        pso = Rot([fw.ps("mx_pso", [128, 512], F32) for _ in range(2)])
        norm_T(fw, C, xs_blk, D, gain, hT, psT)
        qf = fw.sb("mx_qf", [128, 4, 128], BF16)
        qT = fw.sb("mx_qT", [128, 4, 128], BF16)
        E = Rot([fw.sb("mx_E", [128, 2, 128], BF16) for _ in range(2)])
        rec = fw.sb("mx_rec", [128, 4], F32)
        ob = fw.sb("mx_ob", [128, 4, 128], BF16)
        oT = fw.sb("mx_oT", [128, 4, 128], BF16)
        xo = Rot([fw.sb("mx_xo", [128, 512], F32) for _ in range(2)])
        xn = Rot([fw.sb("mx_xn", [128, 512], F32) for _ in range(2)])
        for i in range(NB):
            pq = psq.next()
            for k in range(16):
                fw.mm(pq, hT[:, k, i * 128:(i + 1) * 128], wq[:, k, :], start=(k == 0), stop=(k == 15))
            head_norm(fw, pq.re("p (h d) -> p h d", h=4), gq, qf, 4, 128)
            p = psT.next()
            for h in range(4):
                fw.tr(p[:, h, :], qf[:, h, :], C.ident)
            fw.copy("dve", qT, p[:, 0:4, :])
            pa = pss.next()
            pb = pss.next()
            for h in range(4):
                ps_ = psq.next()
                for mt in range(2):
                    fw.mm(ps_[:, mt * 128:(mt + 1) * 128], C.memKT[:, h, mt * 128:(mt + 1) * 128], qT[:, h, :])
                e_ = E.next()
                fw.act(e_.re("p a b -> p (a b)"), ps_[:, 0:256], AF.Exp)
                pacc = (pa if h < 2 else pb)[:, h % 2, 0:129]
                for mt in range(2):
                    fw.mm(pacc, e_[:, mt, :], C.memV1[:, mt, h, 0:129], start=(mt == 0), stop=(mt == 1))
            for (pp, h0) in ((pa, 0), (pb, 2)):
                fw.recip(rec[:, h0:h0 + 2], pp[:, :, 128])
                fw.tt("dve", ob[:, h0:h0 + 2, :], pp[:, :, 0:128], rec[:, h0:h0 + 2].un(2).bc([128, 2, 128]), ALU.mult)
            p = psT.next()
            for h in range(4):
                fw.tr(p[:, h, :], ob[:, h, :], C.ident)
            fw.copy("dve", oT, p[:, 0:4, :])
            for db in range(4):
                po = pso.next()
                for k in range(4):
                    fw.mm(po, oT[:, k, :], wo[:, k, db * 512:(db + 1) * 512], start=(k == 0), stop=(k == 3))
                o_ = xo.next()
                n_ = xn.next()
                fw.dma("sp", o_, xs_blk[i][:, db * 512:(db + 1) * 512])
                fw.tt("dve", n_, po, o_, ALU.add)
                fw.dma("sp", xs_blk[i][:, db * 512:(db + 1) * 512], n_)


def swa_phase(fw, C, I, u_blk, yA_blk):
    with fw.phase():
        gq = load_bc(fw, "sw_gq", I["swa_q_norm"][0], 64, mul=0.125)
        gk = load_bc(fw, "sw_gk", I["swa_k_norm"][0], 64)
        esink = load_bc(fw, "sw_sink", I["swa_sinks"][0], 16)
        fw.act(esink, esink, AF.Exp)
        ut = Rot([fw.sb("sw_u", [128, 1536], F32) for _ in range(2)])
        qn = fw.sb("sw_qn", [128, 16, 64], F32)
        kn = fw.sb("sw_kn", [128, 4, 64], F32)
        qf = fw.sb("sw_qf", [128, 16, 64], BF16)
        kf = fw.sb("sw_kf", [128, 4, 64], BF16)
        qT = fw.sb("sw_qT", [64, 16, 128], BF16)
        kT = Rot([fw.sb("sw_kT", [64, 4, 128], BF16) for _ in range(2)])
        V1 = Rot([fw.sb("sw_V1", [128, 4, 66], BF16) for _ in range(2)])
        E = Rot([fw.sb("sw_E", [128, 4, 128], BF16) for _ in range(4)])
        den = fw.sb("sw_den", [128, 4], F32)
        yt = Rot([fw.sb("sw_y", [128, 16, 64], BF16) for _ in range(2)])
        psT = Rot([fw.ps("sw_psT", [128, 8, 128], BF16) for _ in range(2)])
        pss = Rot([fw.ps("sw_pss", [128, 512], F32) for _ in range(2)])
        pso = Rot([fw.ps("sw_pso", [128, 4, 128], F32) for _ in range(2)])
        for v_ in V1.items:
            fw.memset("dve", v_, 1.0)
        kT_prev = None
        V1_prev = None
        for n in range(NB):
            u = ut.next()
            fw.dma("sp", u, u_blk[n][:, 0:1536])
            head_norm(fw, u[:, 0:1024].re("p (h d) -> p h d", h=16), gq, qn, 16, 64)
            rope(fw, qn, qf, C.cos64[:, n, :], C.sin64[:, n, :], 16, 64)
            head_norm(fw, u[:, 1024:1280].re("p (h d) -> p h d", h=4), gk, kn, 4, 64)
            rope(fw, kn, kf, C.cos64[:, n, :], C.sin64[:, n, :], 4, 64)
            V1c = V1.next()
            fw.copy("act", V1c[:, :, 0:64], u[:, 1280:1536].re("p (h d) -> p h d", h=4))
            for h0 in (0, 8):
                p = psT.next()
                for h in range(8):
                    fw.tr(p[0:64, h, :], qf[:, h0 + h, :], C.ident)
                fw.copy("dve", qT[:, h0:h0 + 8, :], p[0:64, :, :])
            kTc = kT.next()
            p = psT.next()
            for h in range(4):
                fw.tr(p[0:64, h, :], kf[:, h, :], C.ident)
            fw.copy("dve", kTc, p[0:64, 0:4, :])
            y = yt.next()
            for g in range(4):
                srcs = []
                if n > 0:
                    srcs.append((kT_prev, V1_prev, C.maskL))
                srcs.append((kTc, V1c, C.maskU))
                es = []
                for (kt_, v1_, mk_) in srcs:
                    ps_ = pss.next()
                    fw.mm(ps_, kt_[:, g, :], qT[:, 4 * g:4 * g + 4, :].re("p a b -> p (a b)"))
                    e_ = E.next()
                    fw.act(e_.re("p a b -> p (a b)"), ps_, AF.Exp)
                    fw.tt("pool", e_, e_, mk_.un(1).bc([128, 4, 128]), ALU.mult)
                    es.append((e_, v1_))
                po = pso.next()
                for hq in range(4):
                    for j, (e_, v1_) in enumerate(es):
                        fw.mm(po[:, hq, 0:65], e_[:, hq, :], v1_[:, g, 0:65], start=(j == 0), stop=(j == len(es) - 1))
                fw.tt("dve", den, po[:, :, 64], esink[:, 4 * g:4 * g + 4], ALU.add)
                fw.recip(den, den)
                fw.tt("dve", y[:, 4 * g:4 * g + 4, :], po[:, :, 0:64], den.un(2).bc([128, 4, 64]), ALU.mult)
            fw.dma("sp", yA_blk[n][:, 0:1024], y.re("p h d -> p (h d)"))
            kT_prev, V1_prev = kTc, V1c


def rwkv_prep_phase(fw, C, I, u, u_blk, rowsTM_blk, gS_blk, bonS_blk):
    RW0 = 1536
    RWW = 3360
    with fw.phase():
        mu = load_bc(fw, "rk_mu", I["rwkv_mu"][0], RWW)
        w0 = load_bc(fw, "rk_w0", I["rwkv_w0"][0], 1024)
        a0 = load_bc(fw, "rk_a0", I["rwkv_a0"][0], 1024)
        kkw = load_bc(fw, "rk_kk", I["rwkv_k_k"][0], 1024)
        kaw = load_bc(fw, "rk_ka", I["rwkv_k_a"][0], 1024)
        rkw = load_bc(fw, "rk_rk", I["rwkv_r_k"][0], 1024)
        w2 = fw.sb("rk_w2", [64, 1024], BF16)
        a2 = fw.sb("rk_a2", [64, 1024], BF16)
        g2a = fw.sb("rk_g2a", [128, 1024], BF16)
        g2b = fw.sb("rk_g2b", [32, 1024], BF16)
        fw.dma("pool", w2, I["rwkv_w2"][0])
        fw.dma("pool", a2, I["rwkv_a2"][0])
        fw.dma("pool", g2a, I["rwkv_g2"][0][0:128, :])
        fw.dma("pool", g2b, I["rwkv_g2"][0][128:160, :])
        pt = Rot([fw.sb("rk_p", [128, RWW], F32) for _ in range(2)])
        pv = Rot([fw.sb("rk_pv", [128, RWW], F32) for _ in range(1)])
        xs = fw.sb("rk_xs", [128, RWW], F32)
        lo = fw.sb("rk_lo", [128, 288], BF16)
        loT = fw.sb("rk_loT", [128, 4, 128], BF16)
        t1 = fw.sb("rk_t1", [128, 1024], F32)
        t2 = fw.sb("rk_t2", [128, 1024], F32)
        aa = fw.sb("rk_a", [128, 1024], F32)
        kk = fw.sb("rk_kkn", [128, 1024], F32)
        st = fw.sb("rk_st", [128, 4, 16], F32)
        rows = Rot([fw.sb("rk_rows", [128, 6, 1024], F32) for _ in range(1)])
        gg = Rot([fw.sb("rk_g", [128, 1024], F32) for _ in range(1)])
        bon = Rot([fw.sb("rk_bon", [128, 1024], F32) for _ in range(1)])
        psT = Rot([fw.ps("rk_psT", [128, 8, 128], BF16) for _ in range(1)])
        psm = Rot([fw.ps("rk_psm", [128, 512], F32) for _ in range(4)])
        H3 = lambda v: v.re("p (h d) -> p h d", h=16)
        for n in range(NB):
            p = pt.next()
            pr = pv.next()
            fw.dma("sp", p, u_blk[n][:, RW0:RW0 + RWW])
            if n == 0:
                fw.memset("dve", pr, 0.0)
                fw.dma("act", pr[1:128, :], u[0:127, RW0:RW0 + RWW])
            else:
                fw.dma("act", pr, u[n * 128 - 1:n * 128 + 127, RW0:RW0 + RWW])
            fw.tt("dve", pr, pr, p, ALU.subtract)
            fw.tt("pool", pr, pr, mu, ALU.mult)
            fw.tt("dve", xs, pr, p, ALU.add)
            r_ = xs[:, 0:1024]
            k_ = xs[:, 1024:2048]
            v_ = xs[:, 2048:3072]
            fw.act(lo[:, 0:64], xs[:, 3072:3136], AF.Tanh)
            fw.copy("dve", lo[:, 64:128], xs[:, 3136:3200])
            fw.act(lo[:, 128:288], xs[:, 3200:3360], AF.Sigmoid)
            ptt = psT.next()
            fw.tr(ptt[0:64, 0, :], lo[:, 0:64], C.ident)
            fw.tr(ptt[0:64, 1, :], lo[:, 64:128], C.ident)
            fw.tr(ptt[:, 2, :], lo[:, 128:256], C.ident)
            fw.tr(ptt[0:32, 3, :], lo[:, 256:288], C.ident)
            fw.copy("dve", loT[0:64, 0:2, :], ptt[0:64, 0:2, :])
            fw.copy("dve", loT[:, 2, :], ptt[:, 2, :])
            fw.copy("dve", loT[0:32, 3, :], ptt[0:32, 3, :])
            R = rows.next()
            g_ = gg.next()
            for cb in range(2):
                cs = slice(cb * 512, (cb + 1) * 512)
                pw = psm.next()
                fw.mm(pw, loT[0:64, 0, :], w2[:, cs])
                pa = psm.next()
                fw.mm(pa, loT[0:64, 1, :], a2[:, cs])
                pg = psm.next()
                fw.mm(pg, loT[:, 2, :], g2a[:, cs], start=True, stop=False)
                fw.mm(pg, loT[0:32, 3, :], g2b[:, cs], start=False, stop=True)
                fw.copy("act", g_[:, cs], pg)
                fw.tt("dve", t1[:, cs], pw, w0[:, cs], ALU.add)
                fw.tt("dve", aa[:, cs], pa, a0[:, cs], ALU.add)
            fw.act(t1, t1, AF.Exp, scale=-1.0)
            fw.act(t1, t1, AF.Ln, bias=1.0)
            fw.act(t1, t1, AF.Exp, scale=-1.0, bias=C.cst[:, 3:4])
            fw.copy("act", R[:, 1, :], t1)
            fw.act(aa, aa, AF.Sigmoid)
            fw.tt("pool", kk, k_, kkw, ALU.mult)
            fw.tt("pool", t2, kk, kk, ALU.mult)
            fw.red("dve", st[:, 0, :], H3(t2))
            fw.act(st[:, 1, :], st[:, 0, :], AF.Sqrt)
            fw.ts("dve", st[:, 1, :], st[:, 1, :], 1e-12, ALU.max)
            fw.recip(st[:, 2, :], st[:, 1, :])
            fw.tt("dve", H3(kk), H3(kk), st[:, 2, :].un(2).bc([128, 16, 64]), ALU.mult)
            fw.ts("pool", R[:, 0, :], kk, -1.0, ALU.mult)
            fw.tt("pool", R[:, 2, :], kk, aa, ALU.mult)
            fw.ts("dve", t2, aa, -1.0, ALU.add)
            fw.tt("dve", t2, t2, kaw, ALU.mult)
            fw.stt("dve", R[:, 3, :], t2, 1.0, k_, ALU.add, ALU.mult)
            fw.copy("act", R[:, 4, :], r_)
            fw.tt("pool", t2, r_, rkw, ALU.mult)
            fw.tt("pool", t2, t2, R[:, 3, :], ALU.mult)
            fw.red("dve", st[:, 3, :], H3(t2))
            b_ = bon.next()
            fw.tt("dve", H3(b_), H3(v_), st[:, 3, :].un(2).bc([128, 16, 64]), ALU.mult)
            fw.dma("sp", gS_blk[n], g_)
            fw.dma("sp", bonS_blk[n], b_)
            fw.copy("act", R[:, 5, :], v_)
            fw.dma("act", rowsTM_blk[n], R.re("p a c -> p (a c)"))


def rwkv_scan_phase(fw, C, rowS, vS, yS):
    TB = 32
    with fw.phase():
        St = [fw.sb("sc_S", [128, 8, 64], F32) for _ in range(2)]
        A = fw.sb("sc_A", [128, 8, 64], F32)
        B = fw.sb("sc_B", [128, 8, 64], F32)
        Cc = Rot([fw.sb("sc_C", [128, 8, 64], F32) for _ in range(2)])
        Ee_r = Rot([fw.sb("sc_E", [128, 8, 64], F32) for _ in range(2)])
        sa = fw.sb("sc_sa", [128, 8], F32)
        rb = Rot([fw.sb("sc_rows", [128, 5, TB, 64], F32) for _ in range(2)])
        vb = Rot([fw.sb("sc_v", [128, TB, 8], F32) for _ in range(2)])
        yb = Rot([fw.sb("sc_y", [128, TB, 8], F32) for _ in range(2)])
        fw.memset("dve", St[0], 0.0)
        cur = 0
        qs = Rot(["sp", "act"])
        for b in range(S // TB):
            t0 = b * TB
            R = rb.next()
            vv = vb.next()
            yy = yb.next()
            for ty in range(5):
                for h in range(16):
                    fw.dma(qs.next(), R[h * 8:(h + 1) * 8, ty, :, :], rowS[ty, h, t0:t0 + TB, :].pb(8))
            fw.dma(qs.next(), vv, vS[:, t0:t0 + TB, :])
            for t in range(TB):
                S0 = St[cur]
                S1 = St[1 - cur]
                rowb = lambda ty: R[:, ty, t, :].un(1).bc([128, 8, 64])
                c_ = Cc.next()
                fw.tt("dve", c_, rowb(3), vv[:, t, :].un(2).bc([128, 8, 64]), ALU.mult)
                fw.tt("dve", A, S0, rowb(0), ALU.mult)
                fw.red("dve", sa, A)
                fw.tt("dve", S1, S0, rowb(1), ALU.mult)
                fw.tt("dve", B, rowb(2), sa.un(2).bc([128, 8, 64]), ALU.mult)
                fw.tt("dve", S1, S1, B, ALU.add)
                fw.tt("dve", S1, S1, c_, ALU.add)
                Ee = Ee_r.next()
                fw.tt("dve", Ee, S1, rowb(4), ALU.mult)
                fw.red("dve", yy[:, t, :], Ee)
                cur = 1 - cur
            fw.dma(qs.next(), yS[:, t0:t0 + TB, :], yy)


def rwkv_chunk_phase(fw, C, rowsTM_blk, yTM_blk):
    HG = 4
    with fw.phase():
        triU = fw.sb("ck_tri", [128, 128], F32)
        mk2 = fw.sb("ck_mk2", [128, 2, 128], F32)
        mSL = fw.sb("ck_msl", [128, 128], F32)
        ones = fw.sb("ck_ones", [128, 1], F32)
        fw.copy("dve", triU, C.maskU)
        fw.copy("dve", mk2[:, 1, :], C.maskU)
        fw.tt("dve", mk2[:, 0, :], C.maskU, C.ident, ALU.subtract)
        fw.copy("dve", mSL, C.maskL)
        fw.memset("dve", ones, 1.0)
        H32 = [fw.sb("ck_H32", [64, 64], F32) for _ in range(16)]
        Hb = [fw.sb("ck_Hb", [64, 64], BF16) for _ in range(16)]
        for h in range(16):
            fw.memset("dve", H32[h], 0.0)
            fw.memset("dve", Hb[h], 0.0)
        inb = Rot([fw.sb("ck_in", [128, 6, 1024], F32) for _ in range(2)])
        Lt = fw.sb("ck_L", [128, 1024], F32)
        Pt = fw.sb("ck_Pt", [128, 1024], F32)
        Pinv = fw.sb("ck_Pinv", [128, 1024], F32)
        Pm1 = fw.sb("ck_Pm1", [128, 1024], F32)
        At = fw.sb("ck_At", [128, 1024], BF16)
        Bt = fw.sb("ck_Bt", [128, 1024], BF16)
        Kt = fw.sb("ck_Kt", [128, 1024], BF16)
        Rt = fw.sb("ck_Rt", [128, 1024], BF16)
        Vt = fw.sb("ck_Vt", [128, 1024], BF16)
        PC = fw.sb("ck_PC", [64, 16], F32)
        ytile = Rot([fw.sb("ck_y", [128, 1024], F32) for _ in range(2)])
        T4s = [fw.sb("ck_T4", [64, 4, 128], BF16) for _ in range(HG)]
        G1s = [fw.sb("ck_G1", [128, 2, 128], BF16) for _ in range(HG)]
        G2s = [fw.sb("ck_G2", [128, 2, 128], BF16) for _ in range(HG)]
        Ns = [[fw.sb("ck_N", [128, 128], BF16) for _ in range(HG)] for _ in range(2)]
        Ms = [[fw.sb("ck_M", [128, 128], BF16) for _ in range(HG)] for _ in range(2)]
        Zs = [[fw.sb("ck_Z", [128, 128], BF16) for _ in range(HG)] for _ in range(2)]
        Yvs = [fw.sb("ck_Yv", [128, 64], F32) for _ in range(HG)]
        Gps = [fw.sb("ck_Gp", [64, 64], F32) for _ in range(HG)]
        WTs = [fw.sb("ck_WT", [64, 128], BF16) for _ in range(HG)]
        Us = [fw.sb("ck_U", [128, 64], BF16) for _ in range(HG)]
        tHs = [fw.sb("ck_tH", [64, 64], F32) for _ in range(HG)]
        banks = [fw.ps("ck_pp", [128, 512], F32) for _ in range(7)]
        for b_ in banks:
            b_.buf.excl = True
        pp = Rot([V(b_.buf, b_.ap[:, hh * 256:(hh + 1) * 256]) for hh in range(2) for b_ in banks])
        bankb = fw.ps("ck_pb", [128, 1024], BF16)
        bankb.buf.excl = True
        pb = Rot([V(bankb.buf, bankb.ap[:, hh * 512:(hh + 1) * 512]) for hh in range(2)])
        ev = Rot(["act", "dve"])
        for n in range(NB):
            X = inb.next()
            fw.dma("sp", X.re("p a c -> p (a c)"), rowsTM_blk[n])
            nkk, ew, kka, km, r_, v_ = (X[:, j, :] for j in range(6))
            for c4 in range(4):
                cs = slice(c4 * 256, (c4 + 1) * 256)
                p = pp.next()
                fw.mm(p, triU, ew[:, cs])
                fw.act(Pinv[:, cs], p, AF.Exp)
                fw.act(Pt[:, cs], p, AF.Exp, scale=-1.0)
                fw.tt("dve", Lt[:, cs], ew[:, cs], p, ALU.subtract)
                fw.act(Pm1[:, cs], Lt[:, cs], AF.Exp)
            pc = pp.next()
            for h in range(16):
                fw.mm(pc[0:64, h:h + 1], ew[:, h * 64:(h + 1) * 64], ones)
            fw.act(PC, pc[0:64, 0:16], AF.Exp, scale=-1.0)
            fw.tt("dve", At, nkk, Pm1, ALU.mult)
            fw.tt("pool", Bt, kka, Pinv, ALU.mult)
            fw.tt("pool", Kt, km, Pinv, ALU.mult)
            fw.tt("dve", Rt, r_, Pt, ALU.mult)
            fw.copy("act", Vt, v_)
            y = ytile.next()
            for g0 in range(0, 16, HG):
                hs = list(range(g0, g0 + HG))
                for h in hs:
                    i = h % HG
                    sl = slice(h * 64, (h + 1) * 64)
                    pT = pb.next()
                    pv = pT[0:64, :].re("p (a t) -> p a t", a=4)
                    for j, src in enumerate((At, Rt, Bt, Kt)):
                        fw.tr(pv[:, j, :], src[:, sl], C.ident)
                    T4 = T4s[i]
                    fw.copy("act", T4, pv)
                    ar = T4[:, 0:2, :].re("p a t -> p (a t)")
                    p1 = pp.next()
                    fw.mm(p1, T4[:, 2, :], ar)
                    p2 = pp.next()
                    fw.mm(p2, T4[:, 3, :], ar)
                    p3 = pp.next()
                    fw.mm(p3[:, 0:128], T4[:, 0, :], T4[:, 2, :])
                    fw.tt("dve", G1s[i], p1.re("p (a t) -> p a t", a=2), mk2, ALU.mult)
                    fw.tt("dve", G2s[i], p2.re("p (a t) -> p a t", a=2), mk2, ALU.mult)
                    fw.tt("dve", Ms[0][i], p3[:, 0:128], mSL, ALU.mult)
                    p4 = pp.next()
                    fw.mm(p4[:, 0:64], G2s[i][:, 0, :], Vt[:, sl])
                    fw.mm(p4[:, 64:128], G2s[i][:, 1, :], Vt[:, sl])
                    fw.copy("act", Zs[0][i][:, 0:64], At[:, sl])
                    fw.copy("act", Zs[0][i][:, 64:128], p4[:, 0:64])
                    fw.copy("dve", Yvs[i], p4[:, 64:128])
                    p5 = pp.next()
                    fw.mm(p5[0:64, 0:64], Kt[:, sl], Vt[:, sl])
                    fw.ts("dve", Gps[i], p5[0:64, 0:64], PC[:, h:h + 1], ALU.mult)
                for k in range(7):
                    for h in hs:
                        i = h % HG
                        Nk = G1s[i][:, 0, :] if k == 0 else Ns[k % 2][i]
                        Mk = Ms[k % 2][i]
                        Zk = Zs[k % 2][i]
                        pz = pp.next()
                        fw.mm(pz[:, 0:128], Nk, Zk)
                        fw.tt("dve", Zs[(k + 1) % 2][i], Zk, pz[:, 0:128], ALU.add)
                        if k < 6:
                            pn = pp.next()
                            fw.mm(pn[:, 0:128], Mk, Nk)
                            fw.copy("act", Ns[(k + 1) % 2][i], pn[:, 0:128])
                            pm = pp.next()
                            fw.mm(pm[:, 0:128], Nk, Mk)
                            fw.copy(ev.next(), Ms[(k + 1) % 2][i], pm[:, 0:128])
                for h in hs:
                    i = h % HG
                    sl = slice(h * 64, (h + 1) * 64)
                    Zf = Zs[1][i]
                    pT = pb.next()
                    fw.tr(pT[0:64, 0:128], Zf[:, 0:64], C.ident)
                    fw.copy("act", WTs[i], pT[0:64, 0:128])
                    pu = pp.next()
                    fw.mm(pu[:, 0:64], WTs[i], Hb[h])
                    fw.tt("dve", Us[i], pu[:, 0:64], Zf[:, 64:128], ALU.add)
                    py = pp.next()
                    fw.mm(py[:, 0:64], T4s[i][:, 1, :], Hb[h], start=True, stop=False)
                    fw.mm(py[:, 0:64], G1s[i][:, 1, :], Us[i], start=False, stop=True)
                    fw.tt("dve", y[:, sl], py[:, 0:64], Yvs[i], ALU.add)
                    ph = pp.next()
                    fw.mm(ph[0:64, 0:64], Bt[:, sl], Us[i])
                    fw.tt("dve", tHs[i], ph[0:64, 0:64], H32[h], ALU.add)
                    fw.stt("dve", H32[h], tHs[i], PC[:, h:h + 1], Gps[i], ALU.mult, ALU.add)
                    fw.copy("act", Hb[h], H32[h])
            fw.dma("sp", yTM_blk[n], y)


def rwkv_post_phase(fw, C, I, yTM_blk, gS_blk, bonS_blk, yA_blk):
    with fw.phase():
        gng = load_bc(fw, "rp_g", I["rwkv_gn_g"][0], 1024)
        gnb = load_bc(fw, "rp_b", I["rwkv_gn_b"][0], 1024)
        yt = Rot([fw.sb("rp_y", [128, 16, 64], F32) for _ in range(2)])
        gt = Rot([fw.sb("rp_gt", [128, 1024], F32) for _ in range(2)])
        bt = Rot([fw.sb("rp_bt", [128, 1024], F32) for _ in range(2)])
        sq = fw.sb("rp_sq", [128, 16, 64], F32)
        st = fw.sb("rp_st", [128, 4, 16], F32)
        ob = Rot([fw.sb("rp_o", [128, 1024], BF16) for _ in range(2)])
        F2 = lambda v: v.re("p h d -> p (h d)")
        for n in range(NB):
            y = yt.next()
            g_ = gt.next()
            b_ = bt.next()
            fw.dma("sp", y.re("p h d -> p (h d)"), yTM_blk[n])
            fw.dma("act", g_, gS_blk[n])
            fw.dma("act", b_, bonS_blk[n])
            fw.red("dve", st[:, 0, :], y)
            fw.ts("dve", st[:, 0, :], st[:, 0, :], 1.0 / 64, ALU.mult)
            fw.tt("dve", y, y, st[:, 0, :].un(2).bc([128, 16, 64]), ALU.subtract)
            fw.tt("pool", sq, y, y, ALU.mult)
            fw.red("dve", st[:, 1, :], sq)
            rstd_from_ss(fw, st[:, 1, :], 64, 64e-5, st[:, 2, :], st[:, 3, :])
            fw.tt("dve", y, y, st[:, 3, :].un(2).bc([128, 16, 64]), ALU.mult)
            fw.tt("pool", F2(y), F2(y), gng, ALU.mult)
            fw.tt("pool", F2(y), F2(y), gnb, ALU.add)
            fw.tt("dve", F2(y), F2(y), b_, ALU.add)
            o_ = ob.next()
            fw.tt("dve", o_, F2(y), g_, ALU.mult)
            fw.dma("sp", yA_blk[n][:, 1024:2048], o_)


def causal_attn(fw, C, qT, kT, V1, dv, pss, pso, E, masks_eng, on_out):
    for qi in range(NB):
        po = pso.next()
        for sg in range(0, qi + 1, 4):
            n = min(4, qi + 1 - sg)
            ps_ = pss.next()
            for j in range(n):
                fw.mm(ps_[:, j * 128:(j + 1) * 128], kT[:, (sg + j) * 128:(sg + j + 1) * 128], qT[:, qi * 128:(qi + 1) * 128])
            e_ = E.next()
            fw.act(e_[:, 0:n * 128], ps_[:, 0:n * 128], AF.Exp)
            if sg + n - 1 == qi:
                j = n - 1
                fw.tt(masks_eng, e_[:, j * 128:(j + 1) * 128], e_[:, j * 128:(j + 1) * 128], C.maskU, ALU.mult)
            for j in range(n):
                fw.mm(po[:, 0:dv + 1], e_[:, j * 128:(j + 1) * 128], V1[:, sg + j, 0:dv + 1], start=(sg + j == 0), stop=(sg + j == qi))
        on_out(qi, po)


def mla_prep_phase(fw, C, I, u_blk, qTS, kTS, V1S_blk):
    sc = 96 ** -0.5
    with fw.phase():
        gcq = load_bc(fw, "ml_gcq", I["mla_cq_norm"][0], 512)
        gckv = load_bc(fw, "ml_gckv", I["mla_ckv_norm"][0], 256)
        gqn = load_bc(fw, "ml_gqn", I["mla_q_nope_norm"][0], 64, mul=sc)
        gkn = load_bc(fw, "ml_gkn", I["mla_k_nope_norm"][0], 64)
        gqr = load_bc(fw, "ml_gqr", I["mla_q_rope_norm"][0], 32, mul=sc)
        gkr = load_bc(fw, "ml_gkr", I["mla_k_rope_norm"][0], 32)
        wuq = fw.sb("ml_wuq", [128, 4, 1536], BF16)
        wukv = fw.sb("ml_wukv", [128, 2, 2048], BF16)
        fw.dma("pool", wuq, wview(I["mla_w_uq"][0], 0, 1536))
        fw.dma("pool", wukv, wview(I["mla_w_ukv"][0], 0, 2048))
        ut = Rot([fw.sb("ml_u", [128, 800], F32) for _ in range(2)])
        cb = fw.sb("ml_cb", [128, 768], BF16)
        junk = fw.sb("ml_junk", [128, 512], BF16)
        st = fw.sb("ml_st", [128, 8], F32)
        cT = fw.sb("ml_cT", [128, 6, 128], BF16)
        qfull = fw.sb("ml_qfull", [128, 16, 96], F32)
        kvfull = fw.sb("ml_kvfull", [128, 16, 128], F32)
        tq = fw.sb("ml_tq", [128, 16, 64], F32)
        tr_ = fw.sb("ml_tr", [128, 16, 32], F32)
        tkr = fw.sb("ml_tkr", [128, 1, 32], F32)
        kpe = fw.sb("ml_kpe", [128, 1, 32], BF16)
        qf = fw.sb("ml_qf", [128, 16, 96], BF16)
        kf = fw.sb("ml_kf", [128, 16, 96], BF16)
        V1 = Rot([fw.sb("ml_V1", [128, 16, 66], BF16) for _ in range(2)])
        qTt = Rot([fw.sb("ml_qTt", [96, 16, 128], BF16) for _ in range(2)])
        kTt = Rot([fw.sb("ml_kTt", [96, 16, 128], BF16) for _ in range(2)])
        for v_ in V1.items:
            fw.memset("dve", v_, 1.0)
        psT = Rot([fw.ps("ml_psT", [128, 8, 128], BF16) for _ in range(2)])
        psm = Rot([fw.ps("ml_psm", [128, 512], F32) for _ in range(4)])
        for n in range(NB):
            u = ut.next()
            fw.dma("sp", u, u_blk[n][:, 0:800])
            for (c0, c1, g_, j0) in ((0, 512, gcq, 0), (512, 768, gckv, 3)):
                w = c1 - c0
                fw.act(junk[:, 0:w], u[:, c0:c1], AF.Square, accum=st[:, j0:j0 + 1])
                rstd_from_ss(fw, st[:, j0:j0 + 1], w, EPS, st[:, j0 + 1:j0 + 2], st[:, j0 + 2:j0 + 3])
                fw.stt("dve", cb[:, c0:c1], u[:, c0:c1], st[:, j0 + 2:j0 + 3], g_, ALU.mult, ALU.mult)
            p = psT.next()
            for j in range(6):
                fw.tr(p[:, j, :], cb[:, j * 128:(j + 1) * 128], C.ident)
            fw.copy("dve", cT, p[:, 0:6, :])
            for cbk in range(3):
                pq = psm.next()
                for k in range(4):
                    fw.mm(pq, cT[:, k, :], wuq[:, k, cbk * 512:(cbk + 1) * 512], start=(k == 0), stop=(k == 3))
                fw.copy("act", qfull.re("p h d -> p (h d)")[:, cbk * 512:(cbk + 1) * 512], pq)
            for cbk in range(4):
                pk = psm.next()
                for k in range(2):
                    fw.mm(pk, cT[:, 4 + k, :], wukv[:, k, cbk * 512:(cbk + 1) * 512], start=(k == 0), stop=(k == 1))
                fw.copy("act", kvfull.re("p h d -> p (h d)")[:, cbk * 512:(cbk + 1) * 512], pk)
            cs32 = (C.cos32[:, n, :], C.sin32[:, n, :])
            head_norm(fw, qfull[:, :, 0:64], gqn, qf[:, :, 0:64], 16, 64)
            head_norm(fw, qfull[:, :, 64:96], gqr, tr_, 16, 32)
            rope(fw, tr_, qf[:, :, 64:96], cs32[0], cs32[1], 16, 32)
            head_norm(fw, kvfull[:, :, 0:64], gkn, kf[:, :, 0:64], 16, 64)
            head_norm(fw, u[:, 768:800].re("p (h d) -> p h d", h=1), gkr, tkr, 1, 32)
            rope(fw, tkr, kpe, cs32[0], cs32[1], 1, 32)
            fw.copy("dve", kf[:, :, 64:96], kpe.bc([128, 16, 32]))
            V1c = V1.next()
            fw.copy("act", V1c[:, :, 0:64], kvfull[:, :, 64:128])
            fw.dma("sp", V1S_blk[n], V1c.re("p h d -> p (h d)"))
            for (src, dstS, rot) in ((qf, qTS, qTt), (kf, kTS, kTt)):
                tt_ = rot.next()
                for h0 in (0, 8):
                    p = psT.next()
                    for h in range(8):
                        fw.tr(p[0:96, h, :], src[:, h0 + h, :], C.ident)
                    fw.copy("dve", tt_[:, h0:h0 + 8, :], p[0:96, :, :])
                fw.dma("act", dstS[:, :, n * 128:(n + 1) * 128].re("h d t -> d h t"), tt_)


def mla_attn_phase(fw, C, qTS, kTS, V1S, yA):
    with fw.phase():
        qT = Rot([fw.sb("ma_qT", [96, S], BF16) for _ in range(2)])
        kT = Rot([fw.sb("ma_kT", [96, S], BF16) for _ in range(2)])
        V1 = Rot([fw.sb("ma_V1", [128, NB, 66], BF16) for _ in range(2)])
        E = Rot([fw.sb("ma_E", [128, 512], BF16) for _ in range(3)])
        yo = Rot([fw.sb("ma_yo", [128, NB, 64], BF16) for _ in range(2)])
        rec = fw.sb("ma_rec", [128, 2], F32)
        pss = Rot([fw.ps("ma_pss", [128, 512], F32) for _ in range(3)])
        pso = Rot([fw.ps("ma_pso", [128, 512], F32) for _ in range(2)])
        for h in range(16):
            q_ = qT.next()
            k_ = kT.next()
            v_ = V1.next()
            y_ = yo.next()
            fw.dma("sp", q_, qTS[h])
            fw.dma("sp", k_, kTS[h])
            fw.dma("sp", v_, V1S[:, h * 66:(h + 1) * 66].re("(n p) d -> p n d", p=128))

            def on_out(qi, po, y_=y_):
                fw.recip(rec[:, 0:1], po[:, 64:65])
                fw.ts("dve", y_[:, qi, :], po[:, 0:64], rec[:, 0:1], ALU.mult)
            causal_attn(fw, C, q_, k_, v_, 64, pss, pso, E, "pool", on_out)
            fw.dma("sp", yA[:, h * 64:(h + 1) * 64].re("(n p) d -> p n d", p=128), y_)


def diff_prep_phase(fw, C, I, u_blk, qTS, kTS, V1S_blk):
    with fw.phase():
        gq = load_bc(fw, "df_gq", I["diff_q_norm"][0], 64, mul=0.125)
        gk = load_bc(fw, "df_gk", I["diff_k_norm"][0], 64)
        ut = Rot([fw.sb("df_u", [128, 3072], F32) for _ in range(2)])
        tn = fw.sb("df_tn", [128, 16, 64], F32)
        qf = fw.sb("df_qf", [128, 16, 64], BF16)
        kf = fw.sb("df_kf", [128, 16, 64], BF16)
        V1 = Rot([fw.sb("df_V1", [128, 8, 130], BF16) for _ in range(2)])
        qTt = Rot([fw.sb("df_qTt", [64, 16, 128], BF16) for _ in range(2)])
        kTt = Rot([fw.sb("df_kTt", [64, 16, 128], BF16) for _ in range(2)])
        for v_ in V1.items:
            fw.memset("dve", v_, 1.0)
        psT = Rot([fw.ps("df_psT", [128, 8, 128], BF16) for _ in range(2)])
        for n in range(NB):
            u = ut.next()
            fw.dma("sp", u, u_blk[n][:, 800:3872])
            cs = (C.cos64[:, n, :], C.sin64[:, n, :])
            head_norm(fw, u[:, 0:1024].re("p (h d) -> p h d", h=16), gq, tn, 16, 64)
            rope(fw, tn, qf, cs[0], cs[1], 16, 64)
            head_norm(fw, u[:, 1024:2048].re("p (h d) -> p h d", h=16), gk, tn, 16, 64)
            rope(fw, tn, kf, cs[0], cs[1], 16, 64)
            V1c = V1.next()
            fw.copy("act", V1c[:, :, 0:128], u[:, 2048:3072].re("p (h d) -> p h d", h=8))
            fw.dma("sp", V1S_blk[n], V1c.re("p h d -> p (h d)"))
            for (src, dstS, rot) in ((qf, qTS, qTt), (kf, kTS, kTt)):
                tt_ = rot.next()
                for h0 in (0, 8):
                    p = psT.next()
                    for h in range(8):
                        fw.tr(p[0:64, h, :], src[:, h0 + h, :], C.ident)
                    fw.copy("dve", tt_[:, h0:h0 + 8, :], p[0:64, :, :])
                fw.dma("act", dstS[:, :, n * 128:(n + 1) * 128].re("h d t -> d h t"), tt_)


def diff_attn_phase(fw, C, I, qTS, kTS, V1S, yA):
    lam_init = 0.8 - 0.6 * math.exp(-0.3 * 1)
    with fw.phase():
        lt = [load_bc(fw, "df_l%d" % j, I[nm][0], 64) for j, nm in enumerate(("diff_lq1", "diff_lk1", "diff_lq2", "diff_lk2"))]
        ls = fw.sb("df_ls", [128, 4], F32)
        fw.tt("dve", lt[0], lt[0], lt[1], ALU.mult)
        fw.tt("dve", lt[2], lt[2], lt[3], ALU.mult)
        fw.red("dve", ls[:, 0:1], lt[0])
        fw.red("dve", ls[:, 1:2], lt[2])
        fw.act(ls[:, 0:2], ls[:, 0:2], AF.Exp)
        fw.tt("dve", ls[:, 2:3], ls[:, 0:1], ls[:, 1:2], ALU.subtract)
        fw.ts("dve", ls[:, 3:4], ls[:, 2:3], lam_init, ALU.add, -1.0, ALU.mult)
        gsub = load_bc(fw, "df_gs", I["diff_subln"][0], 128, mul=(1.0 - lam_init))
        qT = Rot([fw.sb("da_qT", [64, S], BF16) for _ in range(2)])
        kT = Rot([fw.sb("da_kT", [64, S], BF16) for _ in range(2)])
        V1 = Rot([fw.sb("da_V1", [128, NB, 130], BF16) for _ in range(2)])
        E = Rot([fw.sb("da_E", [128, 512], BF16) for _ in range(3)])
        o1 = fw.sb("da_o1", [128, NB, 128], F32)
        o2 = fw.sb("da_o2", [128, 128], F32)
        sq = fw.sb("da_sq", [128, 128], F32)
        yo = Rot([fw.sb("da_yo", [128, NB, 128], BF16) for _ in range(2)])
        rec = fw.sb("da_rec", [128, 4], F32)
        pss = Rot([fw.ps("da_pss", [128, 512], F32) for _ in range(3)])
        pso = Rot([fw.ps("da_pso", [128, 512], F32) for _ in range(2)])
        for h in range(8):
            v_ = V1.next()
            y_ = yo.next()
            fw.dma("sp", v_, V1S[:, h * 130:(h + 1) * 130].re("(n p) d -> p n d", p=128))
            for m in range(2):
                q_ = qT.next()
                k_ = kT.next()
                fw.dma("sp", q_, qTS[2 * h + m])
                fw.dma("sp", k_, kTS[2 * h + m])

                def on_out(qi, po, m=m, y_=y_):
                    fw.recip(rec[:, 0:1], po[:, 128:129])
                    if m == 0:
                        fw.ts("dve", o1[:, qi, :], po[:, 0:128], rec[:, 0:1], ALU.mult)
                    else:
                        fw.tt("dve", rec[:, 1:2], rec[:, 0:1], ls[:, 3:4], ALU.mult)
                        fw.stt("dve", o2, po[:, 0:128], rec[:, 1:2], o1[:, qi, :], ALU.mult, ALU.add)
                        fw.act(sq, o2, AF.Square, accum=rec[:, 2:3])
                        rstd_from_ss(fw, rec[:, 2:3], 128, EPS, rec[:, 3:4], rec[:, 2:3])
                        fw.stt("dve", y_[:, qi, :], o2, rec[:, 2:3], gsub, ALU.mult, ALU.mult)
                causal_attn(fw, C, q_, k_, v_, 128, pss, pso, E, "pool", on_out)
            fw.dma("sp", yA[:, 1024 + h * 128:1024 + (h + 1) * 128].re("(n p) d -> p n d", p=128), y_)


WNAMES = ['ffn1_norm', 'ffn1_w_gate', 'ffn1_w_up', 'ffn1_w_down', 'mix_norm', 'ab_w_in', 'ab_w_out', 'swa_q_norm',
          'swa_k_norm', 'swa_sinks', 'rwkv_mu', 'rwkv_w0', 'rwkv_w2', 'rwkv_a0', 'rwkv_a2', 'rwkv_g2', 'rwkv_k_k',
          'rwkv_k_a', 'rwkv_r_k', 'rwkv_gn_g', 'rwkv_gn_b', 'cd_w_in', 'cd_w_out', 'mla_cq_norm', 'mla_ckv_norm',
          'mla_w_uq', 'mla_w_ukv', 'mla_q_nope_norm', 'mla_k_nope_norm', 'mla_q_rope_norm', 'mla_k_rope_norm',
          'diff_q_norm', 'diff_k_norm', 'diff_lq1', 'diff_lk1', 'diff_lq2', 'diff_lk2', 'diff_subln', 'memx_norm',
          'memx_w_q', 'memx_q_norm', 'memx_w_o', 'mem_norm', 'mem_w_kv', 'mem_k_norm', 'ffn2_norm', 'ffn2_w_gate',
          'ffn2_w_up', 'ffn2_w_down']


class Consts:
    pass


def build(shapes, phases=None, debug=()):
    nc = bass.Bass("TRN2", target_bir_lowering=False)
    fw = FW(nc)
    I = {}
    I["x"] = fw.dram("x", [S, D], F32, kind="ExternalInput")
    I["mem"] = fw.dram("mem", [256, D], F32, kind="ExternalInput")
    I["pos"] = fw.dram("pos", [128, NB], I32, kind="ExternalInput")
    for nm in WNAMES:
        I[nm] = fw.dram(nm, list(shapes[nm]), F32, kind="ExternalInput")
    I["c_ident"] = fw.dram("c_ident", [128, 128], BF16, kind="ExternalInput")
    I["c_masku"] = fw.dram("c_masku", [128, 128], BF16, kind="ExternalInput")
    I["c_maskl"] = fw.dram("c_maskl", [128, 128], BF16, kind="ExternalInput")
    I["c_inv64"] = fw.dram("c_inv64", [128, 32], F32, kind="ExternalInput")
    I["c_inv32"] = fw.dram("c_inv32", [128, 16], F32, kind="ExternalInput")
    out = fw.dram("out", [S, D], F32, kind="ExternalOutput")

    def scratch(name, shape, dt):
        return fw.dram(name, shape, dt, kind=("ExternalOutput" if name in debug else "Internal"))

    def blks(v, n=NB):
        return [V(Buf(v.buf.name + "_b%d" % i), v.ap[i * 128:(i + 1) * 128]) for i in range(n)]
    xs = scratch("xs", [S, D], F32)
    xs_blk = blks(xs)
    u = scratch("u", [S, 4896], F32)
    u_blk = blks(u)
    yA = scratch("yA", [S, D], BF16)
    yA_blk = blks(yA)
    rowsTM = scratch("rowsTM", [S, 6 * 1024], F32)
    yTM = scratch("yTM", [S, 1024], F32)
    gS = scratch("gS", [S, 1024], F32)
    bonS = scratch("bonS", [S, 1024], F32)
    qTS = scratch("qTS", [16, 96, S], BF16)
    kTS = scratch("kTS", [16, 96, S], BF16)
    mV1S = scratch("mV1S", [S, 16 * 66], BF16)
    dqTS = scratch("dqTS", [16, 64, S], BF16)
    dkTS = scratch("dkTS", [16, 64, S], BF16)
    dV1S = scratch("dV1S", [S, 8 * 130], BF16)
    C = Consts()
    ph = phases

    def on(p):
        return ph is None or p in ph

    setup_phase(fw, C, I)
    fw.memset("dve", C.cst[:, 3:4], -0.5)
    x_blk = blks(I["x"])
    for i in range(NB):
        fw.dma("sp" if i % 2 else "act", xs_blk[i], x_blk[i])
    for layer in range(2):
        L = "L%d" % layer
        if on(L + "ffn1"):
            ffn_phase(fw, C, xs_blk, I["ffn1_norm"][layer], I["ffn1_w_gate"][layer], I["ffn1_w_up"][layer], I["ffn1_w_down"][layer])
        if layer == 0:
            if on("L0win"):
                mk, ev = evac_store(u_blk)
                linear_phase(fw, C, xs_blk, D, I["mix_norm"][0], I["ab_w_in"][0], 4896, ev, extra=mk)
                fw.barrier()
            if on("L0swa"):
                swa_phase(fw, C, I, u_blk, yA_blk)
            if on("L0rwkv"):
                rwkv_prep_phase(fw, C, I, u, u_blk, blks(rowsTM), blks(gS), blks(bonS))
                rwkv_chunk_phase(fw, C, blks(rowsTM), blks(yTM))
                rwkv_post_phase(fw, C, I, blks(yTM), blks(gS), blks(bonS), yA_blk)
            if on("L0wout"):
                mk, ev = evac_resid(xs_blk)
                linear_phase(fw, C, yA_blk, D, None, I["ab_w_out"][0], D, ev, cast_only=True, src_bf16=True, extra=mk)
        else:
            if on("L1win"):
                mk, ev = evac_store(u_blk)
                linear_phase(fw, C, xs_blk, D, I["mix_norm"][1], I["cd_w_in"][0], 3872, ev, extra=mk)
            if on("L1mla"):
                mla_prep_phase(fw, C, I, u_blk, qTS, kTS, blks(mV1S))
                mla_attn_phase(fw, C, qTS, kTS, mV1S, yA)
            if on("L1diff"):
                diff_prep_phase(fw, C, I, u_blk, dqTS, dkTS, blks(dV1S))
                diff_attn_phase(fw, C, I, dqTS, dkTS, dV1S, yA)
            if on("L1wout"):
                mk, ev = evac_resid(xs_blk)
                linear_phase(fw, C, blks(yA), D, None, I["cd_w_out"][0], D, ev, cast_only=True, src_bf16=True, extra=mk)
        if on(L + "memx"):
            memx_phase(fw, C, I, layer, xs_blk)
        if on(L + "ffn2"):
            ffn_phase(fw, C, xs_blk, I["ffn2_norm"][layer], I["ffn2_w_gate"][layer], I["ffn2_w_up"][layer], I["ffn2_w_down"][layer])
    fw.barrier()
    toks = []
    for i in range(NB):
        toks.append(fw.dma("sp" if i % 2 else "act", V(Buf("out%d" % i), out.ap[i * 128:(i + 1) * 128]), xs_blk[i]))
    for t in toks:
        fw._wait("sp", t)
    fw.barrier()
    return nc


def host_consts():
    bf = ml_dtypes.bfloat16
    s = np.arange(128)[:, None]
    q = np.arange(128)[None, :]
    inv64 = (1.0 / (10000.0 ** (np.arange(0, 64, 2, dtype=np.float32) / 64))).astype(np.float32)
    inv32 = (1.0 / (10000.0 ** (np.arange(0, 32, 2, dtype=np.float32) / 32))).astype(np.float32)
    return {
        "c_ident": np.eye(128, dtype=np.float32).astype(bf),
        "c_masku": (q >= s).astype(np.float32).astype(bf),
        "c_maskl": (s > q).astype(np.float32).astype(bf),
        "c_inv64": np.ascontiguousarray(np.broadcast_to(inv64[None, :], (128, 32))),
        "c_inv32": np.ascontiguousarray(np.broadcast_to(inv32[None, :], (128, 16))),
    }


def make_in_maps(inputs, cores):
    cst = host_consts()
    maps = []
    for b in cores:
        m = {"x": np.ascontiguousarray(inputs["x"][b], dtype=np.float32),
             "mem": np.ascontiguousarray(inputs["mem"][b], dtype=np.float32),
             "pos": np.ascontiguousarray(np.asarray(inputs["positions"][b]).astype(np.int32).reshape(NB, 128).T)}
        for nm in WNAMES:
            m[nm] = np.ascontiguousarray(inputs[nm], dtype=np.float32)
        m.update(cst)
        maps.append(m)
    return maps


def kernel(**inputs):
    inputs = {k: np.asarray(v) for k, v in inputs.items()}
    shapes = {nm: inputs[nm].shape for nm in WNAMES}
    nc = build(shapes)
    maps = make_in_maps(inputs, list(range(8)))
    res = run_bass_kernel_spmd(nc, maps, core_ids=list(range(8)))
    return np.stack([np.asarray(r["out"], dtype=np.float32) for r in res.results], axis=0)
```

```python
import math
from contextlib import ExitStack
import numpy as np
import ml_dtypes
import concourse.bass as bass
import concourse.mybir as mybir
from concourse.bass_utils import run_bass_kernel_spmd

F32 = mybir.dt.float32
BF16 = mybir.dt.bfloat16
I32 = mybir.dt.int32
ALU = mybir.AluOpType
AF = mybir.ActivationFunctionType
AX = mybir.AxisListType

S = 2048
D = 2048
NB = S // 128
FF = 5632
EPS = 1e-6
PI = math.pi


class Buf:
    __slots__ = ("name", "w", "r", "excl")

    def __init__(self, name, excl=False):
        self.name = name
        self.w = None
        self.r = []
        self.excl = excl


class V:
    __slots__ = ("buf", "ap")

    def __init__(self, buf, ap):
        self.buf = buf
        self.ap = ap

    def __getitem__(self, k):
        return V(self.buf, self.ap[k])

    def re(self, s, **kw):
        return V(self.buf, self.ap.rearrange(s, **kw))

    def bc(self, shape):
        return V(self.buf, self.ap.to_broadcast(list(shape)))

    def un(self, axis):
        return V(self.buf, self.ap.unsqueeze(axis))

    def pb(self, n):
        return V(self.buf, self.ap.partition_broadcast(n))


class FW:
    NDMA = 6

    def __init__(self, nc):
        self.nc = nc
        self.E = {"pe": nc.tensor, "act": nc.scalar, "dve": nc.vector, "pool": nc.gpsimd, "sp": nc.sync}
        self.sems = {}
        self.cnt = {}
        for e in self.E:
            self.sems[e] = nc.alloc_semaphore("s_" + e)
            self.cnt[e] = 0
        self.dq = {}
        for q in ("sp", "act", "pool"):
            lst = []
            for i in range(self.NDMA):
                k = "d_%s%d" % (q, i)
                self.sems[k] = nc.alloc_semaphore(k)
                self.cnt[k] = 0
                lst.append(k)
            self.dq[q] = [lst, 0]
        self.waited = {e: {} for e in self.E}
        self.stack = None
        self.cache = {}
        self.uid = 0

    def phase(self):
        fw = self

        class P:
            def __enter__(s):
                fw.stack = ExitStack()
                fw.stack.__enter__()
                fw.cache = {}
                return fw

            def __exit__(s, *a):
                fw.barrier()
                fw.stack.close()
                fw.stack = None
                return False
        return P()

    def _nm(self, name):
        self.uid += 1
        return "%s_%d" % (name, self.uid)

    def sb(self, name, shape, dt, persistent=False):
        nm = self._nm(name)
        if persistent or self.stack is None:
            t = self.nc.alloc_sbuf_tensor(nm, list(shape), dt)
        else:
            t = self.stack.enter_context(self.nc.sbuf_tensor(nm, list(shape), dt))
        return V(Buf(nm), t.ap())

    def tmp(self, name, shape, dt):
        key = (name, tuple(shape), str(dt))
        if key not in self.cache:
            self.cache[key] = self.sb(name, shape, dt)
        return self.cache[key]

    def ps(self, name, shape, dt):
        nm = self._nm(name)
        t = self.stack.enter_context(self.nc.psum_tensor(nm, list(shape), dt))
        return V(Buf(nm), t.ap())

    def dram(self, name, shape, dt, kind="Internal"):
        t = self.nc.dram_tensor(name, list(shape), dt, kind=kind)
        return V(Buf(name), t.ap())

    def _wait(self, eng, tok):
        if tok is None:
            return
        k, v = tok
        w = self.waited[eng]
        if w.get(k, 0) >= v:
            return
        if k == eng and eng == "pe":
            return
        self.E[eng].wait_ge(self.sems[k], v)
        w[k] = v

    def _deps(self, eng, reads, writes):
        for x in reads:
            self._wait(eng, x.buf.w)
            if x.buf.excl:
                for t in x.buf.r:
                    self._wait(eng, t)
        for x in writes:
            b = x.buf
            self._wait(eng, b.w)
            for t in b.r:
                self._wait(eng, t)

    def _commit(self, tok, reads, writes):
        for x in reads:
            if x.buf.excl:
                x.buf.w = tok
                x.buf.r = []
                continue
            r = x.buf.r
            r.append(tok)
            if len(r) > 16:
                m = {}
                for k, v in r:
                    if m.get(k, 0) < v:
                        m[k] = v
                x.buf.r = list(m.items())
        for x in writes:
            x.buf.w = tok
            x.buf.r = []

    def op(self, eng, fn, reads, writes):
        self._deps(eng, reads, writes)
        ins = fn()
        self.cnt[eng] += 1
        ins.then_inc(self.sems[eng], 1)
        tok = (eng, self.cnt[eng])
        self._commit(tok, reads, writes)
        return tok

    def dma(self, q, out, in_, **kw):
        lst, i = self.dq[q]
        k = lst[i % len(lst)]
        self.dq[q][1] = i + 1
        self._wait(q, (k, self.cnt[k]))
        self._deps(q, [in_], [out])
        ins = self.E[q].dma_start(out=out.ap, in_=in_.ap, **kw)
        self.cnt[k] += 16
        ins.then_inc(self.sems[k], 16)
        tok = (k, self.cnt[k])
        self._commit(tok, [in_], [out])
        return tok

    def barrier(self):
        for e in self.E:
            for k in self.sems:
                if self.cnt[k] > 0:
                    self._wait(e, (k, self.cnt[k]))

    def mm(self, out, lhsT, rhs, start=True, stop=True):
        return self.op("pe", lambda: self.nc.tensor.matmul(out.ap, lhsT.ap, rhs.ap, start=start, stop=stop),
                       [lhsT, rhs], [out])

    def tr(self, out, in_, ident):
        return self.op("pe", lambda: self.nc.tensor.transpose(out.ap, in_.ap, ident.ap), [in_, ident], [out])

    def act(self, out, in_, func, bias=None, scale=1.0, accum=None):
        reads = [in_]
        kw = {}
        if isinstance(bias, V):
            reads.append(bias)
            kw["bias"] = bias.ap
        elif bias is not None:
            kw["bias"] = bias
        if isinstance(scale, V):
            reads.append(scale)
            kw["scale"] = scale.ap
        else:
            kw["scale"] = scale
        writes = [out]
        if accum is not None:
            kw["accum_out"] = accum.ap
            writes.append(accum)
        return self.op("act", lambda: self.nc.scalar.activation(out.ap, in_.ap, func, **kw), reads, writes)

    def tt(self, eng, out, a, b, op):
        e = self.E[eng]
        return self.op(eng, lambda: e.tensor_tensor(out.ap, a.ap, b.ap, op), [a, b], [out])

    def ts(self, eng, out, a, s1, op0, s2=None, op1=None):
        e = self.E[eng]
        reads = [a]
        if isinstance(s1, V):
            reads.append(s1)
            s1 = s1.ap
        if isinstance(s2, V):
            reads.append(s2)
            s2 = s2.ap
        if op1 is None:
            s2 = 0.0
            op1 = ALU.add
        return self.op(eng, lambda: e.tensor_scalar(out.ap, a.ap, s1, s2, op0, op1), reads, [out])

    def stt(self, eng, out, a, s, b, op0, op1):
        e = self.E[eng]
        reads = [a, b]
        if isinstance(s, V):
            reads.append(s)
            s = s.ap
        return self.op(eng, lambda: e.scalar_tensor_tensor(out.ap, a.ap, s, b.ap, op0, op1), reads, [out])

    def copy(self, eng, out, in_):
        if eng == "act":
            return self.op("act", lambda: self.nc.scalar.copy(out.ap, in_.ap), [in_], [out])
        e = self.E[eng]
        return self.op(eng, lambda: e.tensor_copy(out.ap, in_.ap), [in_], [out])

    def red(self, eng, out, in_, op=ALU.add, axis=AX.X):
        e = self.E[eng]
        return self.op(eng, lambda: e.tensor_reduce(out.ap, in_.ap, axis, op), [in_], [out])

    def memset(self, eng, out, val):
        e = self.E[eng]
        return self.op(eng, lambda: e.memset(out.ap, val), [], [out])

    def recip(self, out, in_):
        return self.op("dve", lambda: self.nc.vector.reciprocal(out.ap, in_.ap), [in_], [out])


class Rot:
    def __init__(self, items):
        self.items = items
        self.i = 0

    def next(self):
        x = self.items[self.i % len(self.items)]
        self.i += 1
        return x


def wview(W, c0, c1, r0=0, r1=None):
    ap = W.ap
    if r1 is None:
        r1 = ap.shape[0]
    return V(W.buf, ap[r0:r1, c0:c1].rearrange("(k p) c -> p k c", p=128))


def rstd_from_ss(fw, ss, n, eps, tmp, out):
    fw.act(tmp, ss, AF.Sqrt, bias=fw.eps_tiles[eps], scale=1.0 / n)
    fw.recip(out, tmp)


def norm_T(fw, C, src_blks, K, gain_bc, dstT, psT, nblk=None, dst_off=0, cast_only=False, src_bf16=False):
    nblk = len(src_blks) if nblk is None else nblk
    KC = K // 128
    xt = Rot([fw.sb("nt_x", [128, K], BF16 if src_bf16 else F32) for _ in range(2)])
    hb = Rot([fw.sb("nt_h", [128, K], BF16) for _ in range(2)])
    junk = fw.sb("nt_j", [128, K], BF16)
    st = Rot([fw.sb("nt_s", [128, 4], F32) for _ in range(2)])
    ev = Rot(["dve", "act"])
    for i in range(nblk):
        x = xt.next()
        fw.dma("sp", x, src_blks[i])
        if cast_only:
            h = x if src_bf16 else hb.next()
            if not src_bf16:
                fw.copy("dve", h, x)
        else:
            s = st.next()
            h = hb.next()
            fw.act(junk, x, AF.Square, accum=s[:, 0:1])
            rstd_from_ss(fw, s[:, 0:1], K, EPS, s[:, 1:2], s[:, 2:3])
            fw.stt("dve", h, x, s[:, 2:3], gain_bc, ALU.mult, ALU.mult)
        for k0 in range(0, KC, 8):
            n = min(8, KC - k0)
            p = psT.next()
            for j in range(n):
                fw.tr(p[:, j, :], h[:, (k0 + j) * 128:(k0 + j + 1) * 128], C.ident)
            fw.copy(ev.next(), dstT[:, k0:k0 + n, dst_off + i * 128: dst_off + (i + 1) * 128], p[:, 0:n, :])


def load_bc(fw, name, vec, n, q="sp", mul=None):
    t = fw.sb(name, [128, n], F32)
    fw.dma(q, t, vec.pb(128))
    if mul is not None:
        fw.ts("dve", t, t, float(mul), ALU.mult)
    return t


def head_norm(fw, src, gain_bc, out, H, d, eng="dve", eps=EPS):
    sq = fw.tmp("hn_sq", [128, H, d], F32)
    st = fw.tmp("hn_st", [128, 3, H], F32)
    fw.act(sq, src, AF.Square)
    fw.red("dve", st[:, 0, :], sq)
    rstd_from_ss(fw, st[:, 0, :], d, eps, st[:, 1, :], st[:, 2, :])
    fw.tt(eng, sq, src, st[:, 2, :].un(2).bc([128, H, d]), ALU.mult)
    fw.tt(eng, out, sq, gain_bc.un(1).bc([128, H, d]), ALU.mult)


def rope(fw, x, out, cos, sin, H, d, eng="pool"):
    h = d // 2
    t1 = fw.tmp("rp_1", [128, H, h], F32)
    t2 = fw.tmp("rp_2", [128, H, h], F32)
    cb = cos.un(1).bc([128, H, h])
    sbb = sin.un(1).bc([128, H, h])
    x1 = x[:, :, 0:h]
    x2 = x[:, :, h:d]
    fw.tt(eng, t1, x1, cb, ALU.mult)
    fw.tt(eng, t2, x2, sbb, ALU.mult)
    fw.tt(eng, out[:, :, 0:h], t1, t2, ALU.subtract)
    fw.tt(eng, t1, x1, sbb, ALU.mult)
    fw.tt(eng, t2, x2, cb, ALU.mult)
    fw.tt(eng, out[:, :, h:d], t1, t2, ALU.add)


def ffn_phase(fw, C, xs_blk, norm_g, wg, wu, wd, T=1024, NF=4):
    FC = FF // 128
    FG = FC // NF
    with fw.phase():
        gain = load_bc(fw, "ffn_g", norm_g, D)
        hT = fw.sb("hT", [128, 16, T], BF16)
        actT = fw.sb("actT", [128, FG, T], BF16)
        wgb = Rot([fw.sb("wgb", [128, 16, 256], BF16) for _ in range(2)])
        wub = Rot([fw.sb("wub", [128, 16, 256], BF16) for _ in range(2)])
        wdb = Rot([fw.sb("wdb", [128, FG, 512], BF16) for _ in range(2)])
        sg = Rot([fw.sb("sg", [128, 512], F32) for _ in range(2)])
        xo = Rot([fw.sb("xo", [128, 512], F32) for _ in range(4)])
        xn = Rot([fw.sb("xn", [128, 512], F32) for _ in range(4)])
        psT = Rot([fw.ps("psT", [128, 8, 128], BF16) for _ in range(1)])
        psg = Rot([fw.ps("psg", [128, 512], F32) for _ in range(2)])
        psu = Rot([fw.ps("psu", [128, 512], F32) for _ in range(2)])
        pso = Rot([fw.ps("pso", [128, 512], F32) for _ in range(3)])
        for tt in range(S // T):
            nsub = T // 128
            norm_T(fw, C, xs_blk[tt * nsub:(tt + 1) * nsub], D, gain, hT, psT)
            for fg in range(NF):
                for fp in range(FG // 2 + FG % 2):
                    nj = min(2, FG - fp * 2)
                    c0 = (fg * FG + fp * 2) * 128
                    g_ = wgb.next()
                    u_ = wub.next()
                    fw.dma("pool", g_[:, :, 0:nj * 128], wview(wg, c0, c0 + nj * 128))
                    fw.dma("pool", u_[:, :, 0:nj * 128], wview(wu, c0, c0 + nj * 128))
                    for j in range(nj):
                        fl = fp * 2 + j
                        for th in range(T // 512):
                            pg = psg.next()
                            pu = psu.next()
                            tk = slice(th * 512, (th + 1) * 512)
                            for k in range(16):
                                fw.mm(pg, g_[:, k, j * 128:(j + 1) * 128], hT[:, k, tk], start=(k == 0), stop=(k == 15))
                            for k in range(16):
                                fw.mm(pu, u_[:, k, j * 128:(j + 1) * 128], hT[:, k, tk], start=(k == 0), stop=(k == 15))
                            s_ = sg.next()
                            fw.act(s_, pg, AF.Silu)
                            fw.tt("dve", actT[:, fl, tk], s_, pu, ALU.mult)
                for db in range(4):
                    w_ = wdb.next()
                    fw.dma("pool", w_, wview(wd, db * 512, (db + 1) * 512, fg * FG * 128, (fg + 1) * FG * 128))
                    for sub in range(nsub):
                        blk = xs_blk[tt * nsub + sub]
                        po = pso.next()
                        for f in range(FG):
                            fw.mm(po, actT[:, f, sub * 128:(sub + 1) * 128], w_[:, f, :], start=(f == 0), stop=(f == FG - 1))
                        o_ = xo.next()
                        n_ = xn.next()
                        fw.dma("sp", o_, blk[:, db * 512:(db + 1) * 512])
                        fw.stt("dve", n_, po, 0.5, o_, ALU.mult, ALU.add)
                        fw.dma("act", blk[:, db * 512:(db + 1) * 512], n_)


def linear_phase(fw, C, src_blks, K, norm_g, W, ncols, evac, cast_only=False, src_bf16=False, extra=None):
    KC = K // 128
    with fw.phase():
        gain = None if cast_only else load_bc(fw, "lin_g", norm_g, K)
        hT = fw.sb("lhT", [128, KC, S], BF16)
        psT = Rot([fw.ps("lpsT", [128, 8, 128], BF16) for _ in range(2)])
        pso = Rot([fw.ps("lpso", [128, 512], F32) for _ in range(4)])
        wb = Rot([fw.sb("lwb", [128, KC, 512], BF16) for _ in range(2)])
        ctx = extra(fw) if extra is not None else None
        norm_T(fw, C, src_blks, K, gain, hT, psT, cast_only=cast_only, src_bf16=src_bf16)
        for c0 in range(0, ncols, 512):
            c1 = min(ncols, c0 + 512)
            w_ = wb.next()
            fw.dma("pool", w_[:, :, 0:c1 - c0], wview(W, c0, c1))
            for i in range(NB):
                po = pso.next()
                for k in range(KC):
                    fw.mm(po[:, 0:c1 - c0], hT[:, k, i * 128:(i + 1) * 128], w_[:, k, 0:c1 - c0], start=(k == 0), stop=(k == KC - 1))
                evac(fw, i, c0, c1, po[:, 0:c1 - c0], ctx)


def evac_store(dst_blks):
    def mk(fw):
        return {"t": Rot([fw.sb("ev_t", [128, 512], F32) for _ in range(3)]), "e": Rot(["act", "dve"])}

    def ev(fw, i, c0, c1, po, ctx):
        t = ctx["t"].next()
        fw.copy(ctx["e"].next(), t[:, 0:c1 - c0], po)
        fw.dma("sp", dst_blks[i][:, c0:c1], t[:, 0:c1 - c0])
    return mk, ev


def evac_resid(xs_blk):
    def mk(fw):
        return {"o": Rot([fw.sb("er_o", [128, 512], F32) for _ in range(3)]),
                "n": Rot([fw.sb("er_n", [128, 512], F32) for _ in range(3)])}

    def ev(fw, i, c0, c1, po, ctx):
        o_ = ctx["o"].next()
        n_ = ctx["n"].next()
        fw.dma("sp", o_, xs_blk[i][:, c0:c1])
        fw.tt("dve", n_, po, o_, ALU.add)
        fw.dma("act", xs_blk[i][:, c0:c1], n_)
    return mk, ev


def setup_phase(fw, C, I):
    C.ident = fw.sb("ident", [128, 128], BF16, persistent=True)
    C.maskU = fw.sb("maskU", [128, 128], BF16, persistent=True)
    C.maskL = fw.sb("maskL", [128, 128], BF16, persistent=True)
    C.cos64 = fw.sb("cos64", [128, NB, 32], F32, persistent=True)
    C.sin64 = fw.sb("sin64", [128, NB, 32], F32, persistent=True)
    C.cos32 = fw.sb("cos32", [128, NB, 16], F32, persistent=True)
    C.sin32 = fw.sb("sin32", [128, NB, 16], F32, persistent=True)
    C.memKT = fw.sb("memKT", [128, 4, 256], BF16, persistent=True)
    C.memV1 = fw.sb("memV1", [128, 2, 4, 130], BF16, persistent=True)
    C.cst = fw.sb("cst", [128, 8], F32, persistent=True)
    fw.eps_tiles = {}
    for j, e in enumerate([EPS, 64e-5, 0.0]):
        fw.memset("dve", C.cst[:, j:j + 1], e)
        fw.eps_tiles[e] = C.cst[:, j:j + 1]
    fw.dma("sp", C.ident, I["c_ident"])
    fw.dma("sp", C.maskU, I["c_masku"])
    fw.dma("sp", C.maskL, I["c_maskl"])
    with fw.phase():
        posi = fw.sb("posi", [128, NB], I32)
        posf = fw.sb("posf", [128, NB], F32)
        fw.dma("sp", posi, I["pos"])
        fw.copy("dve", posf, posi)
        for (d2, invn, ct, st_) in ((32, "c_inv64", C.cos64, C.sin64), (16, "c_inv32", C.cos32, C.sin32)):
            inv = fw.sb("inv", [128, d2], F32)
            fw.dma("sp", inv, I[invn])
            ang = fw.sb("ang", [128, NB, d2], F32)
            a2 = fw.sb("ang2", [128, NB, d2], F32)
            ni = fw.sb("angi", [128, NB, d2], I32)
            nf = fw.sb("angf", [128, NB, d2], F32)
            fw.tt("dve", ang, posf.un(2).bc([128, NB, d2]), inv.un(1).bc([128, NB, d2]), ALU.mult)
            for (off, dst) in ((0.0, st_), (PI / 2, ct)):
                fw.ts("dve", a2, ang, off, ALU.add)
                fw.ts("dve", nf, a2, 1.0 / (2 * PI), ALU.mult)
                fw.copy("dve", ni, nf)
                fw.copy("dve", nf, ni)
                fw.stt("dve", a2, nf, -2 * PI, a2, ALU.mult, ALU.add)
                fw.ts("dve", a2, a2, 3.1415925, ALU.min, -3.1415925, ALU.max)
                fw.act(dst, a2, AF.Sin)
    with fw.phase():
        gain = load_bc(fw, "mem_g", I["mem_norm"], D)
        gk = load_bc(fw, "mem_gk", I["mem_k_norm"], 128)
        memT = fw.sb("memT", [128, 16, 256], BF16)
        wkv = fw.sb("wkv", [128, 16, 1024], BF16)
        psT = Rot([fw.ps("mpsT", [128, 8, 128], BF16) for _ in range(2)])
        pso = Rot([fw.ps("mpso", [128, 512], F32) for _ in range(2)])
        kf = fw.sb("mkf", [128, 4, 128], BF16)
        mem_blks = [I["mem"][i * 128:(i + 1) * 128, :] for i in range(2)]
        fw.dma("pool", wkv, wview(I["mem_w_kv"], 0, 1024))
        fw.memset("dve", C.memV1, 1.0)
        norm_T(fw, C, mem_blks, D, gain, memT, psT)
        for mt in range(2):
            pk = pso.next()
            for k in range(16):
                fw.mm(pk, memT[:, k, mt * 128:(mt + 1) * 128], wkv[:, k, 0:512], start=(k == 0), stop=(k == 15))
            head_norm(fw, pk.re("p (h d) -> p h d", h=4), gk, kf, 4, 128)
            p = psT.next()
            for h in range(4):
                fw.tr(p[:, h, :], kf[:, h, :], C.ident)
            fw.copy("dve", C.memKT[:, :, mt * 128:(mt + 1) * 128], p[:, 0:4, :])
            pv = pso.next()
            for k in range(16):
                fw.mm(pv, memT[:, k, mt * 128:(mt + 1) * 128], wkv[:, k, 512:1024], start=(k == 0), stop=(k == 15))
            fw.copy("dve", C.memV1[:, mt, :, 0:128], pv.re("p (h d) -> p h d", h=4))


def memx_phase(fw, C, I, layer, xs_blk):
    with fw.phase():
        gain = load_bc(fw, "mx_g", I["memx_norm"][layer], D)
        gq = load_bc(fw, "mx_gq", I["memx_q_norm"][layer], 128, mul=128 ** -0.5)
        hT = fw.sb("mx_hT", [128, 16, S], BF16)
        wq = fw.sb("mx_wq", [128, 16, 512], BF16)
        wo = fw.sb("mx_wo", [128, 4, 2048], BF16)
        fw.dma("pool", wq, wview(I["memx_w_q"][layer], 0, 512))
        fw.dma("pool", wo, wview(I["memx_w_o"][layer], 0, 2048))
        psT = Rot([fw.ps("mx_psT", [128, 8, 128], BF16) for _ in range(2)])
        psq = Rot([fw.ps("mx_psq", [128, 512], F32) for _ in range(2)])
        pss = Rot([fw.ps("mx_pss", [128, 2, 256], F32) for _ in range(2)])
        pso = Rot([fw.ps("mx_pso", [128, 512], F32) for _ in range(2)])
        norm_T(fw, C, xs_blk, D, gain, hT, psT)
        qf = fw.sb("mx_qf", [128, 4, 128], BF16)
        qT = fw.sb("mx_qT", [128, 4, 128], BF16)
        E = Rot([fw.sb("mx_E", [128, 2, 128], BF16) for _ in range(2)])
        rec = fw.sb("mx_rec", [128, 4], F32)
        ob = fw.sb("mx_ob", [128, 4, 128], BF16)
        oT = fw.sb("mx_oT", [128, 4, 128], BF16)
        xo = Rot([fw.sb("mx_xo", [128, 512], F32) for _ in range(2)])
        xn = Rot([fw.sb("mx_xn", [128, 512], F32) for _ in range(2)])
        for i in range(NB):
            pq = psq.next()
            for k in range(16):
                fw.mm(pq, hT[:, k, i * 128:(i + 1) * 128], wq[:, k, :], start=(k == 0), stop=(k == 15))
            head_norm(fw, pq.re("p (h d) -> p h d", h=4), gq, qf, 4, 128)
            p = psT.next()
            for h in range(4):
                fw.tr(p[:, h, :], qf[:, h, :], C.ident)
            fw.copy("dve", qT, p[:, 0:4, :])
            pa = pss.next()
            pb = pss.next()
            for h in range(4):
                ps_ = psq.next()
                for mt in range(2):
                    fw.mm(ps_[:, mt * 128:(mt + 1) * 128], C.memKT[:, h, mt * 128:(mt + 1) * 128], qT[:, h, :])
                e_ = E.next()
                fw.act(e_.re("p a b -> p (a b)"), ps_[:, 0:256], AF.Exp)
                pacc = (pa if h < 2 else pb)[:, h % 2, 0:129]
                for mt in range(2):
                    fw.mm(pacc, e_[:, mt, :], C.memV1[:, mt, h, 0:129], start=(mt == 0), stop=(mt == 1))
            for (pp, h0) in ((pa, 0), (pb, 2)):
                fw.recip(rec[:, h0:h0 + 2], pp[:, :, 128])
                fw.tt("dve", ob[:, h0:h0 + 2, :], pp[:, :, 0:128], rec[:, h0:h0 + 2].un(2).bc([128, 2, 128]), ALU.mult)
            p = psT.next()
            for h in range(4):
                fw.tr(p[:, h, :], ob[:, h, :], C.ident)
            fw.copy("dve", oT, p[:, 0:4, :])
            for db in range(4):
                po = pso.next()
                for k in range(4):
                    fw.mm(po, oT[:, k, :], wo[:, k, db * 512:(db + 1) * 512], start=(k == 0), stop=(k == 3))
                o_ = xo.next()
                n_ = xn.next()
                fw.dma("sp", o_, xs_blk[i][:, db * 512:(db + 1) * 512])
                fw.tt("dve", n_, po, o_, ALU.add)
                fw.dma("act", xs_blk[i][:, db * 512:(db + 1) * 512], n_)


def swa_phase(fw, C, I, u_blk, yA_blk):
    with fw.phase():
        gq = load_bc(fw, "sw_gq", I["swa_q_norm"][0], 64, mul=0.125)
        gk = load_bc(fw, "sw_gk", I["swa_k_norm"][0], 64)
        esink = load_bc(fw, "sw_sink", I["swa_sinks"][0], 16)
        fw.act(esink, esink, AF.Exp)
        ut = Rot([fw.sb("sw_u", [128, 1536], F32) for _ in range(2)])
        qn = fw.sb("sw_qn", [128, 16, 64], F32)
        kn = fw.sb("sw_kn", [128, 4, 64], F32)
        qf = fw.sb("sw_qf", [128, 16, 64], BF16)
        kf = fw.sb("sw_kf", [128, 4, 64], BF16)
        qT = fw.sb("sw_qT", [64, 16, 128], BF16)
        kT = Rot([fw.sb("sw_kT", [64, 4, 128], BF16) for _ in range(2)])
        V1 = Rot([fw.sb("sw_V1", [128, 4, 66], BF16) for _ in range(2)])
        E = Rot([fw.sb("sw_E", [128, 4, 128], BF16) for _ in range(4)])
        den = fw.sb("sw_den", [128, 4], F32)
        yt = Rot([fw.sb("sw_y", [128, 16, 64], BF16) for _ in range(2)])
        psT = Rot([fw.ps("sw_psT", [128, 8, 128], BF16) for _ in range(2)])
        pss = Rot([fw.ps("sw_pss", [128, 512], F32) for _ in range(2)])
        pso = Rot([fw.ps("sw_pso", [128, 4, 128], F32) for _ in range(2)])
        for v_ in V1.items:
            fw.memset("dve", v_, 1.0)
        kT_prev = None
        V1_prev = None
        for n in range(NB):
            u = ut.next()
            fw.dma("sp", u, u_blk[n][:, 0:1536])
            head_norm(fw, u[:, 0:1024].re("p (h d) -> p h d", h=16), gq, qn, 16, 64)
            rope(fw, qn, qf, C.cos64[:, n, :], C.sin64[:, n, :], 16, 64)
            head_norm(fw, u[:, 1024:1280].re("p (h d) -> p h d", h=4), gk, kn, 4, 64)
            rope(fw, kn, kf, C.cos64[:, n, :], C.sin64[:, n, :], 4, 64)
            V1c = V1.next()
            fw.copy("act", V1c[:, :, 0:64], u[:, 1280:1536].re("p (h d) -> p h d", h=4))
            for h0 in (0, 8):
                p = psT.next()
                for h in range(8):
                    fw.tr(p[0:64, h, :], qf[:, h0 + h, :], C.ident)
                fw.copy("dve", qT[:, h0:h0 + 8, :], p[0:64, :, :])
            kTc = kT.next()
            p = psT.next()
            for h in range(4):
                fw.tr(p[0:64, h, :], kf[:, h, :], C.ident)
            fw.copy("dve", kTc, p[0:64, 0:4, :])
            y = yt.next()
            for g in range(4):
                srcs = []
                if n > 0:
                    srcs.append((kT_prev, V1_prev, C.maskL))
                srcs.append((kTc, V1c, C.maskU))
                es = []
                for (kt_, v1_, mk_) in srcs:
                    ps_ = pss.next()
                    fw.mm(ps_, kt_[:, g, :], qT[:, 4 * g:4 * g + 4, :].re("p a b -> p (a b)"))
                    e_ = E.next()
                    fw.act(e_.re("p a b -> p (a b)"), ps_, AF.Exp)
                    fw.tt("pool", e_, e_, mk_.un(1).bc([128, 4, 128]), ALU.mult)
                    es.append((e_, v1_))
                po = pso.next()
                for hq in range(4):
                    for j, (e_, v1_) in enumerate(es):
                        fw.mm(po[:, hq, 0:65], e_[:, hq, :], v1_[:, g, 0:65], start=(j == 0), stop=(j == len(es) - 1))
                fw.tt("dve", den, po[:, :, 64], esink[:, 4 * g:4 * g + 4], ALU.add)
                fw.recip(den, den)
                fw.tt("dve", y[:, 4 * g:4 * g + 4, :], po[:, :, 0:64], den.un(2).bc([128, 4, 64]), ALU.mult)
            fw.dma("sp", yA_blk[n][:, 0:1024], y.re("p h d -> p (h d)"))
            kT_prev, V1_prev = kTc, V1c


def rwkv_prep_phase(fw, C, I, u, u_blk, rowsTM_blk, gS_blk, bonS_blk):
    RW0 = 1536
    RWW = 3360
    with fw.phase():
        mu = load_bc(fw, "rk_mu", I["rwkv_mu"][0], RWW)
        w0 = load_bc(fw, "rk_w0", I["rwkv_w0"][0], 1024)
        a0 = load_bc(fw, "rk_a0", I["rwkv_a0"][0], 1024)
        kkw = load_bc(fw, "rk_kk", I["rwkv_k_k"][0], 1024)
        kaw = load_bc(fw, "rk_ka", I["rwkv_k_a"][0], 1024)
        rkw = load_bc(fw, "rk_rk", I["rwkv_r_k"][0], 1024)
        w2 = fw.sb("rk_w2", [64, 1024], BF16)
        a2 = fw.sb("rk_a2", [64, 1024], BF16)
        g2a = fw.sb("rk_g2a", [128, 1024], BF16)
        g2b = fw.sb("rk_g2b", [32, 1024], BF16)
        fw.dma("pool", w2, I["rwkv_w2"][0])
        fw.dma("pool", a2, I["rwkv_a2"][0])
        fw.dma("pool", g2a, I["rwkv_g2"][0][0:128, :])
        fw.dma("pool", g2b, I["rwkv_g2"][0][128:160, :])
        pt = Rot([fw.sb("rk_p", [128, RWW], F32) for _ in range(2)])
        pv = Rot([fw.sb("rk_pv", [128, RWW], F32) for _ in range(1)])
        xs = fw.sb("rk_xs", [128, RWW], F32)
        lo = fw.sb("rk_lo", [128, 288], BF16)
        loT = fw.sb("rk_loT", [128, 4, 128], BF16)
        t1 = fw.sb("rk_t1", [128, 1024], F32)
        t2 = fw.sb("rk_t2", [128, 1024], F32)
        aa = fw.sb("rk_a", [128, 1024], F32)
        kk = fw.sb("rk_kkn", [128, 1024], F32)
        st = fw.sb("rk_st", [128, 4, 16], F32)
        rows = Rot([fw.sb("rk_rows", [128, 6, 1024], F32) for _ in range(1)])
        gg = Rot([fw.sb("rk_g", [128, 1024], F32) for _ in range(1)])
        bon = Rot([fw.sb("rk_bon", [128, 1024], F32) for _ in range(1)])
        psT = Rot([fw.ps("rk_psT", [128, 8, 128], BF16) for _ in range(1)])
        psm = Rot([fw.ps("rk_psm", [128, 512], F32) for _ in range(4)])
        H3 = lambda v: v.re("p (h d) -> p h d", h=16)
        for n in range(NB):
            p = pt.next()
            pr = pv.next()
            fw.dma("sp", p, u_blk[n][:, RW0:RW0 + RWW])
            if n == 0:
                fw.memset("dve", pr, 0.0)
                fw.dma("act", pr[1:128, :], u[0:127, RW0:RW0 + RWW])
            else:
                fw.dma("act", pr, u[n * 128 - 1:n * 128 + 127, RW0:RW0 + RWW])
            fw.tt("dve", pr, pr, p, ALU.subtract)
            fw.tt("pool", pr, pr, mu, ALU.mult)
            fw.tt("dve", xs, pr, p, ALU.add)
            r_ = xs[:, 0:1024]
            k_ = xs[:, 1024:2048]
            v_ = xs[:, 2048:3072]
            fw.act(lo[:, 0:64], xs[:, 3072:3136], AF.Tanh)
            fw.copy("dve", lo[:, 64:128], xs[:, 3136:3200])
            fw.act(lo[:, 128:288], xs[:, 3200:3360], AF.Sigmoid)
            ptt = psT.next()
            fw.tr(ptt[0:64, 0, :], lo[:, 0:64], C.ident)
            fw.tr(ptt[0:64, 1, :], lo[:, 64:128], C.ident)
            fw.tr(ptt[:, 2, :], lo[:, 128:256], C.ident)
            fw.tr(ptt[0:32, 3, :], lo[:, 256:288], C.ident)
            fw.copy("dve", loT[0:64, 0:2, :], ptt[0:64, 0:2, :])
            fw.copy("dve", loT[:, 2, :], ptt[:, 2, :])
            fw.copy("dve", loT[0:32, 3, :], ptt[0:32, 3, :])
            R = rows.next()
            g_ = gg.next()
            for cb in range(2):
                cs = slice(cb * 512, (cb + 1) * 512)
                pw = psm.next()
                fw.mm(pw, loT[0:64, 0, :], w2[:, cs])
                pa = psm.next()
                fw.mm(pa, loT[0:64, 1, :], a2[:, cs])
                pg = psm.next()
                fw.mm(pg, loT[:, 2, :], g2a[:, cs], start=True, stop=False)
                fw.mm(pg, loT[0:32, 3, :], g2b[:, cs], start=False, stop=True)
                fw.copy("act", g_[:, cs], pg)
                fw.tt("dve", t1[:, cs], pw, w0[:, cs], ALU.add)
                fw.tt("dve", aa[:, cs], pa, a0[:, cs], ALU.add)
            fw.act(t1, t1, AF.Exp, scale=-1.0)
            fw.act(t1, t1, AF.Ln, bias=1.0)
            fw.act(t1, t1, AF.Exp, scale=-1.0, bias=C.cst[:, 3:4])
            fw.copy("act", R[:, 1, :], t1)
            fw.act(aa, aa, AF.Sigmoid)
            fw.tt("pool", kk, k_, kkw, ALU.mult)
            fw.tt("pool", t2, kk, kk, ALU.mult)
            fw.red("dve", st[:, 0, :], H3(t2))
            fw.act(st[:, 1, :], st[:, 0, :], AF.Sqrt)
            fw.ts("dve", st[:, 1, :], st[:, 1, :], 1e-12, ALU.max)
            fw.recip(st[:, 2, :], st[:, 1, :])
            fw.tt("dve", H3(kk), H3(kk), st[:, 2, :].un(2).bc([128, 16, 64]), ALU.mult)
            fw.ts("pool", R[:, 0, :], kk, -1.0, ALU.mult)
            fw.tt("pool", R[:, 2, :], kk, aa, ALU.mult)
            fw.ts("dve", t2, aa, -1.0, ALU.add)
            fw.tt("dve", t2, t2, kaw, ALU.mult)
            fw.stt("dve", R[:, 3, :], t2, 1.0, k_, ALU.add, ALU.mult)
            fw.copy("act", R[:, 4, :], r_)
            fw.tt("pool", t2, r_, rkw, ALU.mult)
            fw.tt("pool", t2, t2, R[:, 3, :], ALU.mult)
            fw.red("dve", st[:, 3, :], H3(t2))
            b_ = bon.next()
            fw.tt("dve", H3(b_), H3(v_), st[:, 3, :].un(2).bc([128, 16, 64]), ALU.mult)
            fw.dma("sp", gS_blk[n], g_)
            fw.dma("sp", bonS_blk[n], b_)
            fw.copy("act", R[:, 5, :], v_)
            fw.dma("act", rowsTM_blk[n], R.re("p a c -> p (a c)"))


def rwkv_scan_phase(fw, C, rowS, vS, yS):
    TB = 32
    with fw.phase():
        St = [fw.sb("sc_S", [128, 8, 64], F32) for _ in range(2)]
        A = fw.sb("sc_A", [128, 8, 64], F32)
        B = fw.sb("sc_B", [128, 8, 64], F32)
        Cc = Rot([fw.sb("sc_C", [128, 8, 64], F32) for _ in range(2)])
        Ee_r = Rot([fw.sb("sc_E", [128, 8, 64], F32) for _ in range(2)])
        sa = fw.sb("sc_sa", [128, 8], F32)
        rb = Rot([fw.sb("sc_rows", [128, 5, TB, 64], F32) for _ in range(2)])
        vb = Rot([fw.sb("sc_v", [128, TB, 8], F32) for _ in range(2)])
        yb = Rot([fw.sb("sc_y", [128, TB, 8], F32) for _ in range(2)])
        fw.memset("dve", St[0], 0.0)
        cur = 0
        qs = Rot(["sp", "act"])
        for b in range(S // TB):
            t0 = b * TB
            R = rb.next()
            vv = vb.next()
            yy = yb.next()
            for ty in range(5):
                for h in range(16):
                    fw.dma(qs.next(), R[h * 8:(h + 1) * 8, ty, :, :], rowS[ty, h, t0:t0 + TB, :].pb(8))
            fw.dma(qs.next(), vv, vS[:, t0:t0 + TB, :])
            for t in range(TB):
                S0 = St[cur]
                S1 = St[1 - cur]
                rowb = lambda ty: R[:, ty, t, :].un(1).bc([128, 8, 64])
                c_ = Cc.next()
                fw.tt("dve", c_, rowb(3), vv[:, t, :].un(2).bc([128, 8, 64]), ALU.mult)
                fw.tt("dve", A, S0, rowb(0), ALU.mult)
                fw.red("dve", sa, A)
                fw.tt("dve", S1, S0, rowb(1), ALU.mult)
                fw.tt("dve", B, rowb(2), sa.un(2).bc([128, 8, 64]), ALU.mult)
                fw.tt("dve", S1, S1, B, ALU.add)
                fw.tt("dve", S1, S1, c_, ALU.add)
                Ee = Ee_r.next()
                fw.tt("dve", Ee, S1, rowb(4), ALU.mult)
                fw.red("dve", yy[:, t, :], Ee)
                cur = 1 - cur
            fw.dma(qs.next(), yS[:, t0:t0 + TB, :], yy)


def rwkv_chunk_phase(fw, C, rowsTM_blk, yTM_blk):
    HG = 4
    with fw.phase():
        triU = fw.sb("ck_tri", [128, 128], F32)
        mk2 = fw.sb("ck_mk2", [128, 2, 128], F32)
        mSL = fw.sb("ck_msl", [128, 128], F32)
        ones = fw.sb("ck_ones", [128, 1], F32)
        fw.copy("dve", triU, C.maskU)
        fw.copy("dve", mk2[:, 1, :], C.maskU)
        fw.tt("dve", mk2[:, 0, :], C.maskU, C.ident, ALU.subtract)
        fw.copy("dve", mSL, C.maskL)
        fw.memset("dve", ones, 1.0)
        H32 = [fw.sb("ck_H32", [64, 64], F32) for _ in range(16)]
        Hb = [fw.sb("ck_Hb", [64, 64], BF16) for _ in range(16)]
        for h in range(16):
            fw.memset("dve", H32[h], 0.0)
            fw.memset("dve", Hb[h], 0.0)
        inb = Rot([fw.sb("ck_in", [128, 6, 1024], F32) for _ in range(2)])
        Lt = fw.sb("ck_L", [128, 1024], F32)
        Pt = fw.sb("ck_Pt", [128, 1024], F32)
        Pinv = fw.sb("ck_Pinv", [128, 1024], F32)
        Pm1 = fw.sb("ck_Pm1", [128, 1024], F32)
        At = fw.sb("ck_At", [128, 1024], BF16)
        Bt = fw.sb("ck_Bt", [128, 1024], BF16)
        Kt = fw.sb("ck_Kt", [128, 1024], BF16)
        Rt = fw.sb("ck_Rt", [128, 1024], BF16)
        Vt = fw.sb("ck_Vt", [128, 1024], BF16)
        PC = fw.sb("ck_PC", [64, 16], F32)
        ytile = Rot([fw.sb("ck_y", [128, 1024], F32) for _ in range(2)])
        T4s = [fw.sb("ck_T4", [64, 4, 128], BF16) for _ in range(HG)]
        G1s = [fw.sb("ck_G1", [128, 2, 128], BF16) for _ in range(HG)]
        G2s = [fw.sb("ck_G2", [128, 2, 128], BF16) for _ in range(HG)]
        Ns = [[fw.sb("ck_N", [128, 128], BF16) for _ in range(HG)] for _ in range(2)]
        Ms = [[fw.sb("ck_M", [128, 128], BF16) for _ in range(HG)] for _ in range(2)]
        Zs = [[fw.sb("ck_Z", [128, 128], BF16) for _ in range(HG)] for _ in range(2)]
        Yvs = [fw.sb("ck_Yv", [128, 64], F32) for _ in range(HG)]
        Gps = [fw.sb("ck_Gp", [64, 64], F32) for _ in range(HG)]
        WTs = [fw.sb("ck_WT", [64, 128], BF16) for _ in range(HG)]
        Us = [fw.sb("ck_U", [128, 64], BF16) for _ in range(HG)]
        tHs = [fw.sb("ck_tH", [64, 64], F32) for _ in range(HG)]
        banks = [fw.ps("ck_pp", [128, 512], F32) for _ in range(7)]
        for b_ in banks:
            b_.buf.excl = True
        pp = Rot([V(b_.buf, b_.ap[:, hh * 256:(hh + 1) * 256]) for hh in range(2) for b_ in banks])
        bankb = fw.ps("ck_pb", [128, 1024], BF16)
        bankb.buf.excl = True
        pb = Rot([V(bankb.buf, bankb.ap[:, hh * 512:(hh + 1) * 512]) for hh in range(2)])
        ev = Rot(["act", "dve"])
        for n in range(NB):
            X = inb.next()
            fw.dma("sp", X.re("p a c -> p (a c)"), rowsTM_blk[n])
            nkk, ew, kka, km, r_, v_ = (X[:, j, :] for j in range(6))
            for c4 in range(4):
                cs = slice(c4 * 256, (c4 + 1) * 256)
                p = pp.next()
                fw.mm(p, triU, ew[:, cs])
                fw.act(Pinv[:, cs], p, AF.Exp)
                fw.act(Pt[:, cs], p, AF.Exp, scale=-1.0)
                fw.tt("dve", Lt[:, cs], ew[:, cs], p, ALU.subtract)
                fw.act(Pm1[:, cs], Lt[:, cs], AF.Exp)
            pc = pp.next()
            for h in range(16):
                fw.mm(pc[0:64, h:h + 1], ew[:, h * 64:(h + 1) * 64], ones)
            fw.act(PC, pc[0:64, 0:16], AF.Exp, scale=-1.0)
            fw.tt("dve", At, nkk, Pm1, ALU.mult)
            fw.tt("pool", Bt, kka, Pinv, ALU.mult)
            fw.tt("pool", Kt, km, Pinv, ALU.mult)
            fw.tt("dve", Rt, r_, Pt, ALU.mult)
            fw.copy("act", Vt, v_)
            y = ytile.next()
            for g0 in range(0, 16, HG):
                hs = list(range(g0, g0 + HG))
                for h in hs:
                    i = h % HG
                    sl = slice(h * 64, (h + 1) * 64)
                    pT = pb.next()
                    pv = pT[0:64, :].re("p (a t) -> p a t", a=4)
                    for j, src in enumerate((At, Rt, Bt, Kt)):
                        fw.tr(pv[:, j, :], src[:, sl], C.ident)
                    T4 = T4s[i]
                    fw.copy("act", T4, pv)
                    ar = T4[:, 0:2, :].re("p a t -> p (a t)")
                    p1 = pp.next()
                    fw.mm(p1, T4[:, 2, :], ar)
                    p2 = pp.next()
                    fw.mm(p2, T4[:, 3, :], ar)
                    p3 = pp.next()
                    fw.mm(p3[:, 0:128], T4[:, 0, :], T4[:, 2, :])
                    fw.tt("dve", G1s[i], p1.re("p (a t) -> p a t", a=2), mk2, ALU.mult)
                    fw.tt("dve", G2s[i], p2.re("p (a t) -> p a t", a=2), mk2, ALU.mult)
                    fw.tt("dve", Ms[0][i], p3[:, 0:128], mSL, ALU.mult)
                    p4 = pp.next()
                    fw.mm(p4[:, 0:64], G2s[i][:, 0, :], Vt[:, sl])
                    fw.mm(p4[:, 64:128], G2s[i][:, 1, :], Vt[:, sl])
                    fw.copy("act", Zs[0][i][:, 0:64], At[:, sl])
                    fw.copy("act", Zs[0][i][:, 64:128], p4[:, 0:64])
                    fw.copy("dve", Yvs[i], p4[:, 64:128])
                    p5 = pp.next()
                    fw.mm(p5[0:64, 0:64], Kt[:, sl], Vt[:, sl])
                    fw.ts("dve", Gps[i], p5[0:64, 0:64], PC[:, h:h + 1], ALU.mult)
                for k in range(7):
                    for h in hs:
                        i = h % HG
                        Nk = G1s[i][:, 0, :] if k == 0 else Ns[k % 2][i]
                        Mk = Ms[k % 2][i]
                        Zk = Zs[k % 2][i]
                        pz = pp.next()
                        fw.mm(pz[:, 0:128], Nk, Zk)
                        fw.tt("dve", Zs[(k + 1) % 2][i], Zk, pz[:, 0:128], ALU.add)
                        if k < 6:
                            pn = pp.next()
                            fw.mm(pn[:, 0:128], Mk, Nk)
                            fw.copy("act", Ns[(k + 1) % 2][i], pn[:, 0:128])
                            pm = pp.next()
                            fw.mm(pm[:, 0:128], Nk, Mk)
                            fw.copy(ev.next(), Ms[(k + 1) % 2][i], pm[:, 0:128])
                for h in hs:
                    i = h % HG
                    sl = slice(h * 64, (h + 1) * 64)
                    Zf = Zs[1][i]
                    pT = pb.next()
                    fw.tr(pT[0:64, 0:128], Zf[:, 0:64], C.ident)
                    fw.copy("act", WTs[i], pT[0:64, 0:128])
                    pu = pp.next()
                    fw.mm(pu[:, 0:64], WTs[i], Hb[h])
                    fw.tt("dve", Us[i], pu[:, 0:64], Zf[:, 64:128], ALU.add)
                    py = pp.next()
                    fw.mm(py[:, 0:64], T4s[i][:, 1, :], Hb[h], start=True, stop=False)
                    fw.mm(py[:, 0:64], G1s[i][:, 1, :], Us[i], start=False, stop=True)
                    fw.tt("dve", y[:, sl], py[:, 0:64], Yvs[i], ALU.add)
                    ph = pp.next()
                    fw.mm(ph[0:64, 0:64], Bt[:, sl], Us[i])
                    fw.tt("dve", tHs[i], ph[0:64, 0:64], H32[h], ALU.add)
                    fw.stt("dve", H32[h], tHs[i], PC[:, h:h + 1], Gps[i], ALU.mult, ALU.add)
                    fw.copy("act", Hb[h], H32[h])
            fw.dma("sp", yTM_blk[n], y)


def rwkv_post_phase(fw, C, I, yTM_blk, gS_blk, bonS_blk, yA_blk):
    with fw.phase():
        gng = load_bc(fw, "rp_g", I["rwkv_gn_g"][0], 1024)
        gnb = load_bc(fw, "rp_b", I["rwkv_gn_b"][0], 1024)
        yt = Rot([fw.sb("rp_y", [128, 16, 64], F32) for _ in range(2)])
        gt = Rot([fw.sb("rp_gt", [128, 1024], F32) for _ in range(2)])
        bt = Rot([fw.sb("rp_bt", [128, 1024], F32) for _ in range(2)])
        sq = fw.sb("rp_sq", [128, 16, 64], F32)
        st = fw.sb("rp_st", [128, 4, 16], F32)
        ob = Rot([fw.sb("rp_o", [128, 1024], BF16) for _ in range(2)])
        F2 = lambda v: v.re("p h d -> p (h d)")
        for n in range(NB):
            y = yt.next()
            g_ = gt.next()
            b_ = bt.next()
            fw.dma("sp", y.re("p h d -> p (h d)"), yTM_blk[n])
            fw.dma("act", g_, gS_blk[n])
            fw.dma("act", b_, bonS_blk[n])
            fw.red("dve", st[:, 0, :], y)
            fw.ts("dve", st[:, 0, :], st[:, 0, :], 1.0 / 64, ALU.mult)
            fw.tt("dve", y, y, st[:, 0, :].un(2).bc([128, 16, 64]), ALU.subtract)
            fw.tt("pool", sq, y, y, ALU.mult)
            fw.red("dve", st[:, 1, :], sq)
            rstd_from_ss(fw, st[:, 1, :], 64, 64e-5, st[:, 2, :], st[:, 3, :])
            fw.tt("dve", y, y, st[:, 3, :].un(2).bc([128, 16, 64]), ALU.mult)
            fw.tt("pool", F2(y), F2(y), gng, ALU.mult)
            fw.tt("pool", F2(y), F2(y), gnb, ALU.add)
            fw.tt("dve", F2(y), F2(y), b_, ALU.add)
            o_ = ob.next()
            fw.tt("dve", o_, F2(y), g_, ALU.mult)
            fw.dma("sp", yA_blk[n][:, 1024:2048], o_)


def causal_attn(fw, C, heads, dv, pss, pso, E, masks_eng):
    groups = []
    for hi in range(len(heads)):
        for qi in range(NB):
            for sg in range(0, qi + 1, 4):
                groups.append((hi, qi, sg, min(4, qi + 1 - sg)))
    loaded = {}

    def qk(g):
        hi, qi, sg, n = g
        if hi not in loaded:
            loaded[hi] = heads[hi]["load"]()
        qT, kT, V1 = loaded[hi]
        ps_ = pss.next()
        for j in range(n):
            fw.mm(ps_[:, j * 128:(j + 1) * 128], kT[:, (sg + j) * 128:(sg + j + 1) * 128], qT[:, qi * 128:(qi + 1) * 128])
        return ps_
    pend = qk(groups[0])
    po = None
    for gi, g in enumerate(groups):
        ps_ = pend
        if gi + 1 < len(groups):
            pend = qk(groups[gi + 1])
        hi, qi, sg, n = g
        qT, kT, V1 = loaded[hi]
        if sg == 0:
            po = pso.next()
        e_ = E.next()
        fw.act(e_[:, 0:n * 128], ps_[:, 0:n * 128], AF.Exp)
        if sg + n - 1 == qi:
            j = n - 1
            fw.tt(masks_eng, e_[:, j * 128:(j + 1) * 128], e_[:, j * 128:(j + 1) * 128], C.maskU, ALU.mult)
        for j in range(n):
            fw.mm(po[:, 0:dv + 1], e_[:, j * 128:(j + 1) * 128], V1[:, sg + j, 0:dv + 1], start=(sg + j == 0), stop=(sg + j == qi))
        if sg + n - 1 == qi:
            heads[hi]["on_out"](qi, po)


def mla_prep_phase(fw, C, I, u_blk, qTS, kTS, V1S_blk):
    sc = 96 ** -0.5
    with fw.phase():
        gcq = load_bc(fw, "ml_gcq", I["mla_cq_norm"][0], 512)
        gckv = load_bc(fw, "ml_gckv", I["mla_ckv_norm"][0], 256)
        gqn = load_bc(fw, "ml_gqn", I["mla_q_nope_norm"][0], 64, mul=sc)
        gkn = load_bc(fw, "ml_gkn", I["mla_k_nope_norm"][0], 64)
        gqr = load_bc(fw, "ml_gqr", I["mla_q_rope_norm"][0], 32, mul=sc)
        gkr = load_bc(fw, "ml_gkr", I["mla_k_rope_norm"][0], 32)
        wuq = fw.sb("ml_wuq", [128, 4, 1536], BF16)
        wukv = fw.sb("ml_wukv", [128, 2, 2048], BF16)
        fw.dma("pool", wuq, wview(I["mla_w_uq"][0], 0, 1536))
        fw.dma("pool", wukv, wview(I["mla_w_ukv"][0], 0, 2048))
        ut = Rot([fw.sb("ml_u", [128, 800], F32) for _ in range(2)])
        cb = fw.sb("ml_cb", [128, 768], BF16)
        junk = fw.sb("ml_junk", [128, 512], BF16)
        st = fw.sb("ml_st", [128, 8], F32)
        cT = fw.sb("ml_cT", [128, 6, 128], BF16)
        qfull = fw.sb("ml_qfull", [128, 16, 96], F32)
        kvfull = fw.sb("ml_kvfull", [128, 16, 128], F32)
        tq = fw.sb("ml_tq", [128, 16, 64], F32)
        tr_ = fw.sb("ml_tr", [128, 16, 32], F32)
        tkr = fw.sb("ml_tkr", [128, 1, 32], F32)
        kpe = fw.sb("ml_kpe", [128, 1, 32], BF16)
        qf = fw.sb("ml_qf", [128, 16, 96], BF16)
        kf = fw.sb("ml_kf", [128, 16, 96], BF16)
        V1 = Rot([fw.sb("ml_V1", [128, 16, 66], BF16) for _ in range(2)])
        qTt = Rot([fw.sb("ml_qTt", [96, 16, 128], BF16) for _ in range(2)])
        kTt = Rot([fw.sb("ml_kTt", [96, 16, 128], BF16) for _ in range(2)])
        for v_ in V1.items:
            fw.memset("dve", v_, 1.0)
        psT = Rot([fw.ps("ml_psT", [128, 8, 128], BF16) for _ in range(2)])
        psm = Rot([fw.ps("ml_psm", [128, 512], F32) for _ in range(4)])
        for n in range(NB):
            u = ut.next()
            fw.dma("sp", u, u_blk[n][:, 0:800])
            for (c0, c1, g_, j0) in ((0, 512, gcq, 0), (512, 768, gckv, 3)):
                w = c1 - c0
                fw.act(junk[:, 0:w], u[:, c0:c1], AF.Square, accum=st[:, j0:j0 + 1])
                rstd_from_ss(fw, st[:, j0:j0 + 1], w, EPS, st[:, j0 + 1:j0 + 2], st[:, j0 + 2:j0 + 3])
                fw.stt("dve", cb[:, c0:c1], u[:, c0:c1], st[:, j0 + 2:j0 + 3], g_, ALU.mult, ALU.mult)
            p = psT.next()
            for j in range(6):
                fw.tr(p[:, j, :], cb[:, j * 128:(j + 1) * 128], C.ident)
            fw.copy("dve", cT, p[:, 0:6, :])
            for cbk in range(3):
                pq = psm.next()
                for k in range(4):
                    fw.mm(pq, cT[:, k, :], wuq[:, k, cbk * 512:(cbk + 1) * 512], start=(k == 0), stop=(k == 3))
                fw.copy("act", qfull.re("p h d -> p (h d)")[:, cbk * 512:(cbk + 1) * 512], pq)
            for cbk in range(4):
                pk = psm.next()
                for k in range(2):
                    fw.mm(pk, cT[:, 4 + k, :], wukv[:, k, cbk * 512:(cbk + 1) * 512], start=(k == 0), stop=(k == 1))
                fw.copy("act", kvfull.re("p h d -> p (h d)")[:, cbk * 512:(cbk + 1) * 512], pk)
            cs32 = (C.cos32[:, n, :], C.sin32[:, n, :])
            head_norm(fw, qfull[:, :, 0:64], gqn, qf[:, :, 0:64], 16, 64)
            head_norm(fw, qfull[:, :, 64:96], gqr, tr_, 16, 32)
            rope(fw, tr_, qf[:, :, 64:96], cs32[0], cs32[1], 16, 32)
            head_norm(fw, kvfull[:, :, 0:64], gkn, kf[:, :, 0:64], 16, 64)
            head_norm(fw, u[:, 768:800].re("p (h d) -> p h d", h=1), gkr, tkr, 1, 32)
            rope(fw, tkr, kpe, cs32[0], cs32[1], 1, 32)
            fw.copy("dve", kf[:, :, 64:96], kpe.bc([128, 16, 32]))
            V1c = V1.next()
            fw.copy("act", V1c[:, :, 0:64], kvfull[:, :, 64:128])
            fw.dma("sp", V1S_blk[n], V1c.re("p h d -> p (h d)"))
            for (src, dstS, rot) in ((qf, qTS, qTt), (kf, kTS, kTt)):
                tt_ = rot.next()
                for h0 in (0, 8):
                    p = psT.next()
                    for h in range(8):
                        fw.tr(p[0:96, h, :], src[:, h0 + h, :], C.ident)
                    fw.copy("dve", tt_[:, h0:h0 + 8, :], p[0:96, :, :])
                fw.dma("act", dstS[:, :, n * 128:(n + 1) * 128].re("h d t -> d h t"), tt_)


def mla_attn_phase(fw, C, qTS, kTS, V1S, yA):
    with fw.phase():
        qT = Rot([fw.sb("ma_qT", [96, S], BF16) for _ in range(2)])
        kT = Rot([fw.sb("ma_kT", [96, S], BF16) for _ in range(2)])
        V1 = Rot([fw.sb("ma_V1", [128, NB, 66], BF16) for _ in range(2)])
        E = Rot([fw.sb("ma_E", [128, 512], BF16) for _ in range(3)])
        yo = Rot([fw.sb("ma_yo", [128, NB, 64], BF16) for _ in range(2)])
        rec = fw.sb("ma_rec", [128, 2], F32)
        pss = Rot([fw.ps("ma_pss", [128, 512], F32) for _ in range(3)])
        pso = Rot([fw.ps("ma_pso", [128, 512], F32) for _ in range(2)])
        heads = []
        for h in range(16):
            def load(h=h):
                q_ = qT.next()
                k_ = kT.next()
                v_ = V1.next()
                fw.dma("sp", q_, qTS[h])
                fw.dma("sp", k_, kTS[h])
                fw.dma("sp", v_, V1S[:, h * 66:(h + 1) * 66].re("(n p) d -> p n d", p=128))
                return (q_, k_, v_)
            st_ = {}

            def on_out(qi, po, h=h, st_=st_):
                if qi == 0:
                    st_["y"] = yo.next()
                y_ = st_["y"]
                fw.recip(rec[:, 0:1], po[:, 64:65])
                fw.ts("dve", y_[:, qi, :], po[:, 0:64], rec[:, 0:1], ALU.mult)
                if qi == NB - 1:
                    fw.dma("act", yA[:, h * 64:(h + 1) * 64].re("(n p) d -> p n d", p=128), y_)
            heads.append({"load": load, "on_out": on_out})
        causal_attn(fw, C, heads, 64, pss, pso, E, "pool")


def diff_prep_phase(fw, C, I, u_blk, qTS, kTS, V1S_blk):
    with fw.phase():
        gq = load_bc(fw, "df_gq", I["diff_q_norm"][0], 64, mul=0.125)
        gk = load_bc(fw, "df_gk", I["diff_k_norm"][0], 64)
        ut = Rot([fw.sb("df_u", [128, 3072], F32) for _ in range(2)])
        tn = fw.sb("df_tn", [128, 16, 64], F32)
        qf = fw.sb("df_qf", [128, 16, 64], BF16)
        kf = fw.sb("df_kf", [128, 16, 64], BF16)
        V1 = Rot([fw.sb("df_V1", [128, 8, 130], BF16) for _ in range(2)])
        qTt = Rot([fw.sb("df_qTt", [64, 16, 128], BF16) for _ in range(2)])
        kTt = Rot([fw.sb("df_kTt", [64, 16, 128], BF16) for _ in range(2)])
        for v_ in V1.items:
            fw.memset("dve", v_, 1.0)
        psT = Rot([fw.ps("df_psT", [128, 8, 128], BF16) for _ in range(2)])
        for n in range(NB):
            u = ut.next()
            fw.dma("sp", u, u_blk[n][:, 800:3872])
            cs = (C.cos64[:, n, :], C.sin64[:, n, :])
            head_norm(fw, u[:, 0:1024].re("p (h d) -> p h d", h=16), gq, tn, 16, 64)
            rope(fw, tn, qf, cs[0], cs[1], 16, 64)
            head_norm(fw, u[:, 1024:2048].re("p (h d) -> p h d", h=16), gk, tn, 16, 64)
            rope(fw, tn, kf, cs[0], cs[1], 16, 64)
            V1c = V1.next()
            fw.copy("act", V1c[:, :, 0:128], u[:, 2048:3072].re("p (h d) -> p h d", h=8))
            fw.dma("sp", V1S_blk[n], V1c.re("p h d -> p (h d)"))
            for (src, dstS, rot) in ((qf, qTS, qTt), (kf, kTS, kTt)):
                tt_ = rot.next()
                for h0 in (0, 8):
                    p = psT.next()
                    for h in range(8):
                        fw.tr(p[0:64, h, :], src[:, h0 + h, :], C.ident)
                    fw.copy("dve", tt_[:, h0:h0 + 8, :], p[0:64, :, :])
                fw.dma("act", dstS[:, :, n * 128:(n + 1) * 128].re("h d t -> d h t"), tt_)


def diff_attn_phase(fw, C, I, qTS, kTS, V1S, yA):
    lam_init = 0.8 - 0.6 * math.exp(-0.3 * 1)
    with fw.phase():
        lt = [load_bc(fw, "df_l%d" % j, I[nm][0], 64) for j, nm in enumerate(("diff_lq1", "diff_lk1", "diff_lq2", "diff_lk2"))]
        ls = fw.sb("df_ls", [128, 4], F32)
        fw.tt("dve", lt[0], lt[0], lt[1], ALU.mult)
        fw.tt("dve", lt[2], lt[2], lt[3], ALU.mult)
        fw.red("dve", ls[:, 0:1], lt[0])
        fw.red("dve", ls[:, 1:2], lt[2])
        fw.act(ls[:, 0:2], ls[:, 0:2], AF.Exp)
        fw.tt("dve", ls[:, 2:3], ls[:, 0:1], ls[:, 1:2], ALU.subtract)
        fw.ts("dve", ls[:, 3:4], ls[:, 2:3], lam_init, ALU.add, -1.0, ALU.mult)
        gsub = load_bc(fw, "df_gs", I["diff_subln"][0], 128, mul=(1.0 - lam_init))
        qT = Rot([fw.sb("da_qT", [64, S], BF16) for _ in range(2)])
        kT = Rot([fw.sb("da_kT", [64, S], BF16) for _ in range(2)])
        V1 = Rot([fw.sb("da_V1", [128, NB, 130], BF16) for _ in range(2)])
        E = Rot([fw.sb("da_E", [128, 512], BF16) for _ in range(3)])
        o1 = fw.sb("da_o1", [128, NB, 128], F32)
        o2 = fw.sb("da_o2", [128, 128], F32)
        sq = fw.sb("da_sq", [128, 128], F32)
        yo = Rot([fw.sb("da_yo", [128, NB, 128], BF16) for _ in range(2)])
        rec = fw.sb("da_rec", [128, 4], F32)
        pss = Rot([fw.ps("da_pss", [128, 512], F32) for _ in range(3)])
        pso = Rot([fw.ps("da_pso", [128, 512], F32) for _ in range(2)])
        heads = []
        vcur = {}
        for h in range(8):
            for m in range(2):
                def load(h=h, m=m):
                    if m == 0:
                        v_ = V1.next()
                        fw.dma("sp", v_, V1S[:, h * 130:(h + 1) * 130].re("(n p) d -> p n d", p=128))
                        vcur[h] = v_
                    q_ = qT.next()
                    k_ = kT.next()
                    fw.dma("sp", q_, qTS[2 * h + m])
                    fw.dma("sp", k_, kTS[2 * h + m])
                    return (q_, k_, vcur[h])
                st_ = vcur

                def on_out(qi, po, h=h, m=m):
                    fw.recip(rec[:, 0:1], po[:, 128:129])
                    if m == 0:
                        fw.ts("dve", o1[:, qi, :], po[:, 0:128], rec[:, 0:1], ALU.mult)
                    else:
                        if qi == 0:
                            vcur[("y", h)] = yo.next()
                        y_ = vcur[("y", h)]
                        fw.tt("dve", rec[:, 1:2], rec[:, 0:1], ls[:, 3:4], ALU.mult)
                        fw.stt("dve", o2, po[:, 0:128], rec[:, 1:2], o1[:, qi, :], ALU.mult, ALU.add)
                        fw.act(sq, o2, AF.Square, accum=rec[:, 2:3])
                        rstd_from_ss(fw, rec[:, 2:3], 128, EPS, rec[:, 3:4], rec[:, 2:3])
                        fw.stt("dve", y_[:, qi, :], o2, rec[:, 2:3], gsub, ALU.mult, ALU.mult)
                        if qi == NB - 1:
                            fw.dma("act", yA[:, 1024 + h * 128:1024 + (h + 1) * 128].re("(n p) d -> p n d", p=128), y_)
                heads.append({"load": load, "on_out": on_out})
        causal_attn(fw, C, heads, 128, pss, pso, E, "pool")


WNAMES = ['ffn1_norm', 'ffn1_w_gate', 'ffn1_w_up', 'ffn1_w_down', 'mix_norm', 'ab_w_in', 'ab_w_out', 'swa_q_norm',
          'swa_k_norm', 'swa_sinks', 'rwkv_mu', 'rwkv_w0', 'rwkv_w2', 'rwkv_a0', 'rwkv_a2', 'rwkv_g2', 'rwkv_k_k',
          'rwkv_k_a', 'rwkv_r_k', 'rwkv_gn_g', 'rwkv_gn_b', 'cd_w_in', 'cd_w_out', 'mla_cq_norm', 'mla_ckv_norm',
          'mla_w_uq', 'mla_w_ukv', 'mla_q_nope_norm', 'mla_k_nope_norm', 'mla_q_rope_norm', 'mla_k_rope_norm',
          'diff_q_norm', 'diff_k_norm', 'diff_lq1', 'diff_lk1', 'diff_lq2', 'diff_lk2', 'diff_subln', 'memx_norm',
          'memx_w_q', 'memx_q_norm', 'memx_w_o', 'mem_norm', 'mem_w_kv', 'mem_k_norm', 'ffn2_norm', 'ffn2_w_gate',
          'ffn2_w_up', 'ffn2_w_down']


class Consts:
    pass


def build(shapes, phases=None, debug=()):
    nc = bass.Bass("TRN2", target_bir_lowering=False)
    fw = FW(nc)
    I = {}
    I["x"] = fw.dram("x", [S, D], F32, kind="ExternalInput")
    I["mem"] = fw.dram("mem", [256, D], F32, kind="ExternalInput")
    I["pos"] = fw.dram("pos", [128, NB], I32, kind="ExternalInput")
    for nm in WNAMES:
        I[nm] = fw.dram(nm, list(shapes[nm]), F32, kind="ExternalInput")
    I["c_ident"] = fw.dram("c_ident", [128, 128], BF16, kind="ExternalInput")
    I["c_masku"] = fw.dram("c_masku", [128, 128], BF16, kind="ExternalInput")
    I["c_maskl"] = fw.dram("c_maskl", [128, 128], BF16, kind="ExternalInput")
    I["c_inv64"] = fw.dram("c_inv64", [128, 32], F32, kind="ExternalInput")
    I["c_inv32"] = fw.dram("c_inv32", [128, 16], F32, kind="ExternalInput")
    out = fw.dram("out", [S, D], F32, kind="ExternalOutput")

    def scratch(name, shape, dt):
        return fw.dram(name, shape, dt, kind=("ExternalOutput" if name in debug else "Internal"))

    def blks(v, n=NB):
        return [V(Buf(v.buf.name + "_b%d" % i), v.ap[i * 128:(i + 1) * 128]) for i in range(n)]
    xs = scratch("xs", [S, D], F32)
    xs_blk = blks(xs)
    u = scratch("u", [S, 4896], F32)
    u_blk = blks(u)
    yA = scratch("yA", [S, D], BF16)
    yA_blk = blks(yA)
    rowsTM = scratch("rowsTM", [S, 6 * 1024], F32)
    yTM = scratch("yTM", [S, 1024], F32)
    gS = scratch("gS", [S, 1024], F32)
    bonS = scratch("bonS", [S, 1024], F32)
    qTS = scratch("qTS", [16, 96, S], BF16)
    kTS = scratch("kTS", [16, 96, S], BF16)
    mV1S = scratch("mV1S", [S, 16 * 66], BF16)
    dqTS = scratch("dqTS", [16, 64, S], BF16)
    dkTS = scratch("dkTS", [16, 64, S], BF16)
    dV1S = scratch("dV1S", [S, 8 * 130], BF16)
    C = Consts()
    ph = phases

    def on(p):
        return ph is None or p in ph

    setup_phase(fw, C, I)
    fw.memset("dve", C.cst[:, 3:4], -0.5)
    x_blk = blks(I["x"])
    for i in range(NB):
        fw.dma("sp" if i % 2 else "act", xs_blk[i], x_blk[i])
    for layer in range(2):
        L = "L%d" % layer
        if on(L + "ffn1"):
            ffn_phase(fw, C, xs_blk, I["ffn1_norm"][layer], I["ffn1_w_gate"][layer], I["ffn1_w_up"][layer], I["ffn1_w_down"][layer])
        if layer == 0:
            if on("L0win"):
                mk, ev = evac_store(u_blk)
                linear_phase(fw, C, xs_blk, D, I["mix_norm"][0], I["ab_w_in"][0], 4896, ev, extra=mk)
                fw.barrier()
            if on("L0swa"):
                swa_phase(fw, C, I, u_blk, yA_blk)
            if on("L0rwkv"):
                rwkv_prep_phase(fw, C, I, u, u_blk, blks(rowsTM), blks(gS), blks(bonS))
                rwkv_chunk_phase(fw, C, blks(rowsTM), blks(yTM))
                rwkv_post_phase(fw, C, I, blks(yTM), blks(gS), blks(bonS), yA_blk)
            if on("L0wout"):
                mk, ev = evac_resid(xs_blk)
                linear_phase(fw, C, yA_blk, D, None, I["ab_w_out"][0], D, ev, cast_only=True, src_bf16=True, extra=mk)
        else:
            if on("L1win"):
                mk, ev = evac_store(u_blk)
                linear_phase(fw, C, xs_blk, D, I["mix_norm"][1], I["cd_w_in"][0], 3872, ev, extra=mk)
            if on("L1mla"):
                mla_prep_phase(fw, C, I, u_blk, qTS, kTS, blks(mV1S))
                mla_attn_phase(fw, C, qTS, kTS, mV1S, yA)
            if on("L1diff"):
                diff_prep_phase(fw, C, I, u_blk, dqTS, dkTS, blks(dV1S))
                diff_attn_phase(fw, C, I, dqTS, dkTS, dV1S, yA)
            if on("L1wout"):
                mk, ev = evac_resid(xs_blk)
                linear_phase(fw, C, blks(yA), D, None, I["cd_w_out"][0], D, ev, cast_only=True, src_bf16=True, extra=mk)
        if on(L + "memx"):
            memx_phase(fw, C, I, layer, xs_blk)
        if on(L + "ffn2"):
            ffn_phase(fw, C, xs_blk, I["ffn2_norm"][layer], I["ffn2_w_gate"][layer], I["ffn2_w_up"][layer], I["ffn2_w_down"][layer])
    fw.barrier()
    toks = []
    for i in range(NB):
        toks.append(fw.dma("sp" if i % 2 else "act", V(Buf("out%d" % i), out.ap[i * 128:(i + 1) * 128]), xs_blk[i]))
    for t in toks:
        fw._wait("sp", t)
    fw.barrier()
    return nc


def host_consts():
    bf = ml_dtypes.bfloat16
    s = np.arange(128)[:, None]
    q = np.arange(128)[None, :]
    inv64 = (1.0 / (10000.0 ** (np.arange(0, 64, 2, dtype=np.float32) / 64))).astype(np.float32)
    inv32 = (1.0 / (10000.0 ** (np.arange(0, 32, 2, dtype=np.float32) / 32))).astype(np.float32)
    return {
        "c_ident": np.eye(128, dtype=np.float32).astype(bf),
        "c_masku": (q >= s).astype(np.float32).astype(bf),
        "c_maskl": (s > q).astype(np.float32).astype(bf),
        "c_inv64": np.ascontiguousarray(np.broadcast_to(inv64[None, :], (128, 32))),
        "c_inv32": np.ascontiguousarray(np.broadcast_to(inv32[None, :], (128, 16))),
    }


def make_in_maps(inputs, cores):
    cst = host_consts()
    maps = []
    for b in cores:
        m = {"x": np.ascontiguousarray(inputs["x"][b], dtype=np.float32),
             "mem": np.ascontiguousarray(inputs["mem"][b], dtype=np.float32),
             "pos": np.ascontiguousarray(np.asarray(inputs["positions"][b]).astype(np.int32).reshape(NB, 128).T)}
        for nm in WNAMES:
            m[nm] = np.ascontiguousarray(inputs[nm], dtype=np.float32)
        m.update(cst)
        maps.append(m)
    return maps


def kernel(**inputs):
    inputs = {k: np.asarray(v) for k, v in inputs.items()}
    shapes = {nm: inputs[nm].shape for nm in WNAMES}
    nc = build(shapes)
    maps = make_in_maps(inputs, list(range(8)))
    res = run_bass_kernel_spmd(nc, maps, core_ids=list(range(8)))
    return np.stack([np.asarray(r["out"], dtype=np.float32) for r in res.results], axis=0)
```

```python
import math
from contextlib import ExitStack
import numpy as np
import ml_dtypes
import concourse.bass as bass
import concourse.mybir as mybir
from concourse.bass_utils import run_bass_kernel_spmd

F32 = mybir.dt.float32
BF16 = mybir.dt.bfloat16
I32 = mybir.dt.int32
ALU = mybir.AluOpType
AF = mybir.ActivationFunctionType
AX = mybir.AxisListType

S = 2048
D = 2048
NB = S // 128
FF = 5632
EPS = 1e-6
PI = math.pi


class Buf:
    __slots__ = ("name", "w", "r", "excl")

    def __init__(self, name, excl=False):
        self.name = name
        self.w = None
        self.r = []
        self.excl = excl


class V:
    __slots__ = ("buf", "ap")

    def __init__(self, buf, ap):
        self.buf = buf
        self.ap = ap

    def __getitem__(self, k):
        return V(self.buf, self.ap[k])

    def re(self, s, **kw):
        return V(self.buf, self.ap.rearrange(s, **kw))

    def bc(self, shape):
        return V(self.buf, self.ap.to_broadcast(list(shape)))

    def un(self, axis):
        return V(self.buf, self.ap.unsqueeze(axis))

    def pb(self, n):
        return V(self.buf, self.ap.partition_broadcast(n))


class FW:
    NDMA = 6

    def __init__(self, nc):
        self.nc = nc
        self.E = {"pe": nc.tensor, "act": nc.scalar, "dve": nc.vector, "pool": nc.gpsimd, "sp": nc.sync}
        self.sems = {}
        self.cnt = {}
        for e in self.E:
            self.sems[e] = nc.alloc_semaphore("s_" + e)
            self.cnt[e] = 0
        self.dq = {}
        for q in ("sp", "act", "pool"):
            lst = []
            for i in range(self.NDMA):
                k = "d_%s%d" % (q, i)
                self.sems[k] = nc.alloc_semaphore(k)
                self.cnt[k] = 0
                lst.append(k)
            self.dq[q] = [lst, 0]
        self.waited = {e: {} for e in self.E}
        self.stack = None
        self.cache = {}
        self.uid = 0

    def phase(self):
        fw = self

        class P:
            def __enter__(s):
                fw.stack = ExitStack()
                fw.stack.__enter__()
                fw.cache = {}
                return fw

            def __exit__(s, *a):
                fw.barrier()
                fw.stack.close()
                fw.stack = None
                return False
        return P()

    def _nm(self, name):
        self.uid += 1
        return "%s_%d" % (name, self.uid)

    def sb(self, name, shape, dt, persistent=False):
        nm = self._nm(name)
        if persistent or self.stack is None:
            t = self.nc.alloc_sbuf_tensor(nm, list(shape), dt)
        else:
            t = self.stack.enter_context(self.nc.sbuf_tensor(nm, list(shape), dt))
        return V(Buf(nm), t.ap())

    def tmp(self, name, shape, dt):
        key = (name, tuple(shape), str(dt))
        if key not in self.cache:
            self.cache[key] = self.sb(name, shape, dt)
        return self.cache[key]

    def ps(self, name, shape, dt):
        nm = self._nm(name)
        t = self.stack.enter_context(self.nc.psum_tensor(nm, list(shape), dt))
        return V(Buf(nm), t.ap())

    def dram(self, name, shape, dt, kind="Internal"):
        t = self.nc.dram_tensor(name, list(shape), dt, kind=kind)
        return V(Buf(name), t.ap())

    def _wait(self, eng, tok):
        if tok is None:
            return
        k, v = tok
        w = self.waited[eng]
        if w.get(k, 0) >= v:
            return
        if k == eng and eng == "pe":
            return
        self.E[eng].wait_ge(self.sems[k], v)
        w[k] = v

    def _deps(self, eng, reads, writes):
        for x in reads:
            self._wait(eng, x.buf.w)
            if x.buf.excl:
                for t in x.buf.r:
                    self._wait(eng, t)
        for x in writes:
            b = x.buf
            self._wait(eng, b.w)
            for t in b.r:
                self._wait(eng, t)

    def _commit(self, tok, reads, writes):
        for x in reads:
            if x.buf.excl:
                x.buf.w = tok
                x.buf.r = []
                continue
            r = x.buf.r
            r.append(tok)
            if len(r) > 16:
                m = {}
                for k, v in r:
                    if m.get(k, 0) < v:
                        m[k] = v
                x.buf.r = list(m.items())
        for x in writes:
            x.buf.w = tok
            x.buf.r = []

    def op(self, eng, fn, reads, writes):
        self._deps(eng, reads, writes)
        ins = fn()
        self.cnt[eng] += 1
        ins.then_inc(self.sems[eng], 1)
        tok = (eng, self.cnt[eng])
        self._commit(tok, reads, writes)
        return tok

    def dma(self, q, out, in_, **kw):
        lst, i = self.dq[q]
        k = lst[i % len(lst)]
        self.dq[q][1] = i + 1
        self._wait(q, (k, self.cnt[k]))
        self._deps(q, [in_], [out])
        ins = self.E[q].dma_start(out=out.ap, in_=in_.ap, **kw)
        self.cnt[k] += 16
        ins.then_inc(self.sems[k], 16)
        tok = (k, self.cnt[k])
        self._commit(tok, [in_], [out])
        return tok

    def barrier(self):
        for e in self.E:
            for k in self.sems:
                if self.cnt[k] > 0:
                    self._wait(e, (k, self.cnt[k]))

    def mm(self, out, lhsT, rhs, start=True, stop=True):
        return self.op("pe", lambda: self.nc.tensor.matmul(out.ap, lhsT.ap, rhs.ap, start=start, stop=stop),
                       [lhsT, rhs], [out])

    def tr(self, out, in_, ident):
        return self.op("pe", lambda: self.nc.tensor.transpose(out.ap, in_.ap, ident.ap), [in_, ident], [out])

    def act(self, out, in_, func, bias=None, scale=1.0, accum=None):
        reads = [in_]
        kw = {}
        if isinstance(bias, V):
            reads.append(bias)
            kw["bias"] = bias.ap
        elif bias is not None:
            kw["bias"] = bias
        if isinstance(scale, V):
            reads.append(scale)
            kw["scale"] = scale.ap
        else:
            kw["scale"] = scale
        writes = [out]
        if accum is not None:
            kw["accum_out"] = accum.ap
            writes.append(accum)
        return self.op("act", lambda: self.nc.scalar.activation(out.ap, in_.ap, func, **kw), reads, writes)

    def tt(self, eng, out, a, b, op):
        e = self.E[eng]
        return self.op(eng, lambda: e.tensor_tensor(out.ap, a.ap, b.ap, op), [a, b], [out])

    def ts(self, eng, out, a, s1, op0, s2=None, op1=None):
        e = self.E[eng]
        reads = [a]
        if isinstance(s1, V):
            reads.append(s1)
            s1 = s1.ap
        if isinstance(s2, V):
            reads.append(s2)
            s2 = s2.ap
        if op1 is None:
            s2 = 0.0
            op1 = ALU.add
        return self.op(eng, lambda: e.tensor_scalar(out.ap, a.ap, s1, s2, op0, op1), reads, [out])

    def stt(self, eng, out, a, s, b, op0, op1):
        e = self.E[eng]
        reads = [a, b]
        if isinstance(s, V):
            reads.append(s)
            s = s.ap
        return self.op(eng, lambda: e.scalar_tensor_tensor(out.ap, a.ap, s, b.ap, op0, op1), reads, [out])

    def copy(self, eng, out, in_):
        if eng == "act":
            return self.op("act", lambda: self.nc.scalar.copy(out.ap, in_.ap), [in_], [out])
        e = self.E[eng]
        return self.op(eng, lambda: e.tensor_copy(out.ap, in_.ap), [in_], [out])

    def red(self, eng, out, in_, op=ALU.add, axis=AX.X):
        e = self.E[eng]
        return self.op(eng, lambda: e.tensor_reduce(out.ap, in_.ap, axis, op), [in_], [out])

    def memset(self, eng, out, val):
        e = self.E[eng]
        return self.op(eng, lambda: e.memset(out.ap, val), [], [out])

    def recip(self, out, in_):
        return self.op("dve", lambda: self.nc.vector.reciprocal(out.ap, in_.ap), [in_], [out])


class Rot:
    def __init__(self, items):
        self.items = items
        self.i = 0

    def next(self):
        x = self.items[self.i % len(self.items)]
        self.i += 1
        return x


def wview(W, c0, c1, r0=0, r1=None):
    ap = W.ap
    if r1 is None:
        r1 = ap.shape[0]
    return V(W.buf, ap[r0:r1, c0:c1].rearrange("(k p) c -> p k c", p=128))


def rstd_from_ss(fw, ss, n, eps, tmp, out):
    fw.act(tmp, ss, AF.Ln, bias=fw.eps_tiles[eps], scale=1.0 / n)
    fw.act(out, tmp, AF.Exp, scale=-0.5)


def norm_T(fw, C, src_blks, K, gain_bc, dstT, psT, nblk=None, dst_off=0, cast_only=False, src_bf16=False):
    nblk = len(src_blks) if nblk is None else nblk
    KC = K // 128
    xt = Rot([fw.sb("nt_x", [128, K], BF16 if src_bf16 else F32) for _ in range(2)])
    hb = Rot([fw.sb("nt_h", [128, K], BF16) for _ in range(2)])
    junk = fw.sb("nt_j", [128, K], BF16)
    st = Rot([fw.sb("nt_s", [128, 4], F32) for _ in range(2)])
    ev = Rot(["dve", "act"])
    for i in range(nblk):
        x = xt.next()
        fw.dma("sp", x, src_blks[i])
        if cast_only:
            h = x if src_bf16 else hb.next()
            if not src_bf16:
                fw.copy("dve", h, x)
        else:
            s = st.next()
            h = hb.next()
            fw.act(junk, x, AF.Square, accum=s[:, 0:1])
            rstd_from_ss(fw, s[:, 0:1], K, EPS, s[:, 1:2], s[:, 2:3])
            fw.stt("dve", h, x, s[:, 2:3], gain_bc, ALU.mult, ALU.mult)
        for k0 in range(0, KC, 8):
            n = min(8, KC - k0)
            p = psT.next()
            for j in range(n):
                fw.tr(p[:, j, :], h[:, (k0 + j) * 128:(k0 + j + 1) * 128], C.ident)
            fw.copy(ev.next(), dstT[:, k0:k0 + n, dst_off + i * 128: dst_off + (i + 1) * 128], p[:, 0:n, :])


def load_bc(fw, name, vec, n, q="sp", mul=None):
    t = fw.sb(name, [128, n], F32)
    fw.dma(q, t, vec.pb(128))
    if mul is not None:
        fw.ts("dve", t, t, float(mul), ALU.mult)
    return t


def head_norm(fw, src, gain_bc, out, H, d, eng="dve", eps=EPS):
    sq = fw.tmp("hn_sq", [128, H, d], F32)
    st = fw.tmp("hn_st", [128, 3, H], F32)
    fw.act(sq, src, AF.Square)
    fw.red("dve", st[:, 0, :], sq)
    rstd_from_ss(fw, st[:, 0, :], d, eps, st[:, 1, :], st[:, 2, :])
    fw.tt(eng, sq, src, st[:, 2, :].un(2).bc([128, H, d]), ALU.mult)
    fw.tt(eng, out, sq, gain_bc.un(1).bc([128, H, d]), ALU.mult)


def rope(fw, x, out, cos, sin, H, d, eng="pool"):
    h = d // 2
    t1 = fw.tmp("rp_1" + eng, [128, H, h], F32)
    t2 = fw.tmp("rp_2" + eng, [128, H, h], F32)
    cb = cos.un(1).bc([128, H, h])
    sbb = sin.un(1).bc([128, H, h])
    x1 = x[:, :, 0:h]
    x2 = x[:, :, h:d]
    fw.tt(eng, t1, x1, cb, ALU.mult)
    fw.tt(eng, t2, x2, sbb, ALU.mult)
    fw.tt(eng, out[:, :, 0:h], t1, t2, ALU.subtract)
    fw.tt(eng, t1, x1, sbb, ALU.mult)
    fw.tt(eng, t2, x2, cb, ALU.mult)
    fw.tt(eng, out[:, :, h:d], t1, t2, ALU.add)


def ffn_phase(fw, C, xs_blk, norm_g, wg, wu, wd, T=1024, NF=4):
    FC = FF // 128
    FG = FC // NF
    with fw.phase():
        gain = load_bc(fw, "ffn_g", norm_g, D)
        hT = fw.sb("hT", [128, 16, T], BF16)
        actT = fw.sb("actT", [128, FG, T], BF16)
        wgb = Rot([fw.sb("wgb", [128, 16, 256], BF16) for _ in range(2)])
        wub = Rot([fw.sb("wub", [128, 16, 256], BF16) for _ in range(2)])
        wdb = Rot([fw.sb("wdb", [128, FG, 512], BF16) for _ in range(2)])
        sg = Rot([fw.sb("sg", [128, 512], F32) for _ in range(2)])
        xo = Rot([fw.sb("xo", [128, 512], F32) for _ in range(4)])
        xn = Rot([fw.sb("xn", [128, 512], F32) for _ in range(4)])
        psT = Rot([fw.ps("psT", [128, 8, 128], BF16) for _ in range(1)])
        psg = Rot([fw.ps("psg", [128, 512], F32) for _ in range(2)])
        psu = Rot([fw.ps("psu", [128, 512], F32) for _ in range(2)])
        pso = Rot([fw.ps("pso", [128, 512], F32) for _ in range(3)])
        for tt in range(S // T):
            nsub = T // 128
            norm_T(fw, C, xs_blk[tt * nsub:(tt + 1) * nsub], D, gain, hT, psT)
            for fg in range(NF):
                for fp in range(FG // 2 + FG % 2):
                    nj = min(2, FG - fp * 2)
                    c0 = (fg * FG + fp * 2) * 128
                    g_ = wgb.next()
                    u_ = wub.next()
                    fw.dma("pool", g_[:, :, 0:nj * 128], wview(wg, c0, c0 + nj * 128))
                    fw.dma("pool", u_[:, :, 0:nj * 128], wview(wu, c0, c0 + nj * 128))
                    for j in range(nj):
                        fl = fp * 2 + j
                        for th in range(T // 512):
                            pg = psg.next()
                            pu = psu.next()
                            tk = slice(th * 512, (th + 1) * 512)
                            for k in range(16):
                                fw.mm(pg, g_[:, k, j * 128:(j + 1) * 128], hT[:, k, tk], start=(k == 0), stop=(k == 15))
                            for k in range(16):
                                fw.mm(pu, u_[:, k, j * 128:(j + 1) * 128], hT[:, k, tk], start=(k == 0), stop=(k == 15))
                            s_ = sg.next()
                            fw.act(s_, pg, AF.Silu)
                            fw.tt("dve", actT[:, fl, tk], s_, pu, ALU.mult)
                for db in range(4):
                    w_ = wdb.next()
                    fw.dma("pool", w_, wview(wd, db * 512, (db + 1) * 512, fg * FG * 128, (fg + 1) * FG * 128))
                    for sub in range(nsub):
                        blk = xs_blk[tt * nsub + sub]
                        po = pso.next()
                        for f in range(FG):
                            fw.mm(po, actT[:, f, sub * 128:(sub + 1) * 128], w_[:, f, :], start=(f == 0), stop=(f == FG - 1))
                        o_ = xo.next()
                        n_ = xn.next()
                        fw.dma("sp", o_, blk[:, db * 512:(db + 1) * 512])
                        fw.stt("dve", n_, po, 0.5, o_, ALU.mult, ALU.add)
                        fw.dma("act", blk[:, db * 512:(db + 1) * 512], n_)


def linear_phase(fw, C, src_blks, K, norm_g, W, ncols, evac, cast_only=False, src_bf16=False, extra=None):
    KC = K // 128
    with fw.phase():
        gain = None if cast_only else load_bc(fw, "lin_g", norm_g, K)
        hT = fw.sb("lhT", [128, KC, S], BF16)
        psT = Rot([fw.ps("lpsT", [128, 8, 128], BF16) for _ in range(2)])
        pso = Rot([fw.ps("lpso", [128, 512], F32) for _ in range(4)])
        wb = Rot([fw.sb("lwb", [128, KC, 512], BF16) for _ in range(2)])
        ctx = extra(fw) if extra is not None else None
        norm_T(fw, C, src_blks, K, gain, hT, psT, cast_only=cast_only, src_bf16=src_bf16)
        for c0 in range(0, ncols, 512):
            c1 = min(ncols, c0 + 512)
            w_ = wb.next()
            fw.dma("pool", w_[:, :, 0:c1 - c0], wview(W, c0, c1))
            for i in range(NB):
                po = pso.next()
                for k in range(KC):
                    fw.mm(po[:, 0:c1 - c0], hT[:, k, i * 128:(i + 1) * 128], w_[:, k, 0:c1 - c0], start=(k == 0), stop=(k == KC - 1))
                evac(fw, i, c0, c1, po[:, 0:c1 - c0], ctx)


def evac_store(dst_blks):
    def mk(fw):
        return {"t": Rot([fw.sb("ev_t", [128, 512], F32) for _ in range(3)]), "e": Rot(["act", "dve"])}

    def ev(fw, i, c0, c1, po, ctx):
        t = ctx["t"].next()
        fw.copy(ctx["e"].next(), t[:, 0:c1 - c0], po)
        fw.dma("sp", dst_blks[i][:, c0:c1], t[:, 0:c1 - c0])
    return mk, ev


def evac_resid(xs_blk):
    def mk(fw):
        return {"o": Rot([fw.sb("er_o", [128, 512], F32) for _ in range(3)]),
                "n": Rot([fw.sb("er_n", [128, 512], F32) for _ in range(3)])}

    def ev(fw, i, c0, c1, po, ctx):
        o_ = ctx["o"].next()
        n_ = ctx["n"].next()
        fw.dma("sp", o_, xs_blk[i][:, c0:c1])
        fw.tt("dve", n_, po, o_, ALU.add)
        fw.dma("act", xs_blk[i][:, c0:c1], n_)
    return mk, ev


def setup_phase(fw, C, I):
    C.ident = fw.sb("ident", [128, 128], BF16, persistent=True)
    C.maskU = fw.sb("maskU", [128, 128], BF16, persistent=True)
    C.maskL = fw.sb("maskL", [128, 128], BF16, persistent=True)
    C.cos64 = fw.sb("cos64", [128, NB, 32], F32, persistent=True)
    C.sin64 = fw.sb("sin64", [128, NB, 32], F32, persistent=True)
    C.cos32 = fw.sb("cos32", [128, NB, 16], F32, persistent=True)
    C.sin32 = fw.sb("sin32", [128, NB, 16], F32, persistent=True)
    C.memKT = fw.sb("memKT", [128, 4, 256], BF16, persistent=True)
    C.memV1 = fw.sb("memV1", [128, 2, 4, 130], BF16, persistent=True)
    C.cst = fw.sb("cst", [128, 8], F32, persistent=True)
    fw.eps_tiles = {}
    for j, e in enumerate([EPS, 64e-5, 0.0]):
        fw.memset("dve", C.cst[:, j:j + 1], e)
        fw.eps_tiles[e] = C.cst[:, j:j + 1]
    fw.dma("sp", C.ident, I["c_ident"])
    fw.dma("sp", C.maskU, I["c_masku"])
    fw.dma("sp", C.maskL, I["c_maskl"])
    C.negU4 = fw.sb("negU4", [128, 4, 128], BF16, persistent=True)
    C.negL4 = fw.sb("negL4", [128, 4, 128], BF16, persistent=True)
    fw.ts("dve", C.negU4, C.maskU.un(1).bc([128, 4, 128]), -1.0, ALU.add, 200.0, ALU.mult)
    fw.ts("dve", C.negL4, C.maskL.un(1).bc([128, 4, 128]), -1.0, ALU.add, 200.0, ALU.mult)
    with fw.phase():
        posi = fw.sb("posi", [128, NB], I32)
        posf = fw.sb("posf", [128, NB], F32)
        fw.dma("sp", posi, I["pos"])
        fw.copy("dve", posf, posi)
        for (d2, invn, ct, st_) in ((32, "c_inv64", C.cos64, C.sin64), (16, "c_inv32", C.cos32, C.sin32)):
            inv = fw.sb("inv", [128, d2], F32)
            fw.dma("sp", inv, I[invn])
            ang = fw.sb("ang", [128, NB, d2], F32)
            a2 = fw.sb("ang2", [128, NB, d2], F32)
            ni = fw.sb("angi", [128, NB, d2], I32)
            nf = fw.sb("angf", [128, NB, d2], F32)
            fw.tt("dve", ang, posf.un(2).bc([128, NB, d2]), inv.un(1).bc([128, NB, d2]), ALU.mult)
            for (off, dst) in ((0.0, st_), (PI / 2, ct)):
                fw.ts("dve", a2, ang, off, ALU.add)
                fw.ts("dve", nf, a2, 1.0 / (2 * PI), ALU.mult)
                fw.copy("dve", ni, nf)
                fw.copy("dve", nf, ni)
                fw.stt("dve", a2, nf, -2 * PI, a2, ALU.mult, ALU.add)
                fw.ts("dve", a2, a2, 3.1415925, ALU.min, -3.1415925, ALU.max)
                fw.act(dst, a2, AF.Sin)
    with fw.phase():
        gain = load_bc(fw, "mem_g", I["mem_norm"], D)
        gk = load_bc(fw, "mem_gk", I["mem_k_norm"], 128)
        memT = fw.sb("memT", [128, 16, 256], BF16)
        wkv = fw.sb("wkv", [128, 16, 1024], BF16)
        psT = Rot([fw.ps("mpsT", [128, 8, 128], BF16) for _ in range(2)])
        pso = Rot([fw.ps("mpso", [128, 512], F32) for _ in range(2)])
        kf = fw.sb("mkf", [128, 4, 128], BF16)
        mem_blks = [I["mem"][i * 128:(i + 1) * 128, :] for i in range(2)]
        fw.dma("pool", wkv, wview(I["mem_w_kv"], 0, 1024))
        fw.memset("dve", C.memV1, 1.0)
        norm_T(fw, C, mem_blks, D, gain, memT, psT)
        for mt in range(2):
            pk = pso.next()
            for k in range(16):
                fw.mm(pk, memT[:, k, mt * 128:(mt + 1) * 128], wkv[:, k, 0:512], start=(k == 0), stop=(k == 15))
            head_norm(fw, pk.re("p (h d) -> p h d", h=4), gk, kf, 4, 128)
            p = psT.next()
            for h in range(4):
                fw.tr(p[:, h, :], kf[:, h, :], C.ident)
            fw.copy("dve", C.memKT[:, :, mt * 128:(mt + 1) * 128], p[:, 0:4, :])
            pv = pso.next()
            for k in range(16):
                fw.mm(pv, memT[:, k, mt * 128:(mt + 1) * 128], wkv[:, k, 512:1024], start=(k == 0), stop=(k == 15))
            fw.copy("dve", C.memV1[:, mt, :, 0:128], pv.re("p (h d) -> p h d", h=4))


def memx_phase(fw, C, I, layer, xs_blk):
    with fw.phase():
        gain = load_bc(fw, "mx_g", I["memx_norm"][layer], D)
        gq = load_bc(fw, "mx_gq", I["memx_q_norm"][layer], 128, mul=128 ** -0.5)
        hT = fw.sb("mx_hT", [128, 16, S], BF16)
        wq = fw.sb("mx_wq", [128, 16, 512], BF16)
        wo = fw.sb("mx_wo", [128, 4, 2048], BF16)
        fw.dma("pool", wq, wview(I["memx_w_q"][layer], 0, 512))
        fw.dma("pool", wo, wview(I["memx_w_o"][layer], 0, 2048))
        psT = Rot([fw.ps("mx_psT", [128, 8, 128], BF16) for _ in range(2)])
        psq = Rot([fw.ps("mx_psq", [128, 512], F32) for _ in range(2)])
        pss = Rot([fw.ps("mx_pss", [128, 2, 256], F32) for _ in range(2)])
        pso = Rot([fw.ps("mx_pso", [128, 512], F32) for _ in range(2)])
        norm_T(fw, C, xs_blk, D, gain, hT, psT)
        qf_r = Rot([fw.sb("mx_qf", [128, 4, 128], BF16) for _ in range(2)])
        qT_r = Rot([fw.sb("mx_qT", [128, 4, 128], BF16) for _ in range(2)])
        E = Rot([fw.sb("mx_E", [128, 2, 128], BF16) for _ in range(2)])
        rec_r = Rot([fw.sb("mx_rec", [128, 4], F32) for _ in range(2)])
        ob_r = Rot([fw.sb("mx_ob", [128, 4, 128], BF16) for _ in range(2)])
        oT_r = Rot([fw.sb("mx_oT", [128, 4, 128], BF16) for _ in range(2)])
        xo = Rot([fw.sb("mx_xo", [128, 512], F32) for _ in range(2)])
        xn = Rot([fw.sb("mx_xn", [128, 512], F32) for _ in range(2)])
        for i in range(NB):
            qf, qT, rec, ob, oT = qf_r.next(), qT_r.next(), rec_r.next(), ob_r.next(), oT_r.next()
            pq = psq.next()
            for k in range(16):
                fw.mm(pq, hT[:, k, i * 128:(i + 1) * 128], wq[:, k, :], start=(k == 0), stop=(k == 15))
            head_norm(fw, pq.re("p (h d) -> p h d", h=4), gq, qf, 4, 128)
            p = psT.next()
            for h in range(4):
                fw.tr(p[:, h, :], qf[:, h, :], C.ident)
            fw.copy("dve", qT, p[:, 0:4, :])
            pa = pss.next()
            pb = pss.next()
            for h in range(4):
                ps_ = psq.next()
                for mt in range(2):
                    fw.mm(ps_[:, mt * 128:(mt + 1) * 128], C.memKT[:, h, mt * 128:(mt + 1) * 128], qT[:, h, :])
                e_ = E.next()
                fw.act(e_.re("p a b -> p (a b)"), ps_[:, 0:256], AF.Exp)
                pacc = (pa if h < 2 else pb)[:, h % 2, 0:129]
                for mt in range(2):
                    fw.mm(pacc, e_[:, mt, :], C.memV1[:, mt, h, 0:129], start=(mt == 0), stop=(mt == 1))
            for (pp, h0) in ((pa, 0), (pb, 2)):
                fw.recip(rec[:, h0:h0 + 2], pp[:, :, 128])
                fw.tt("dve", ob[:, h0:h0 + 2, :], pp[:, :, 0:128], rec[:, h0:h0 + 2].un(2).bc([128, 2, 128]), ALU.mult)
            p = psT.next()
            for h in range(4):
                fw.tr(p[:, h, :], ob[:, h, :], C.ident)
            fw.copy("dve", oT, p[:, 0:4, :])
            for db in range(4):
                po = pso.next()
                for k in range(4):
                    fw.mm(po, oT[:, k, :], wo[:, k, db * 512:(db + 1) * 512], start=(k == 0), stop=(k == 3))
                o_ = xo.next()
                n_ = xn.next()
                fw.dma("sp", o_, xs_blk[i][:, db * 512:(db + 1) * 512])
                fw.tt("dve", n_, po, o_, ALU.add)
                fw.dma("act", xs_blk[i][:, db * 512:(db + 1) * 512], n_)


def swa_phase(fw, C, I, u_blk, yA_blk):
    with fw.phase():
        gq = load_bc(fw, "sw_gq", I["swa_q_norm"][0], 64, mul=0.125)
        gk = load_bc(fw, "sw_gk", I["swa_k_norm"][0], 64)
        esink = load_bc(fw, "sw_sink", I["swa_sinks"][0], 16)
        fw.act(esink, esink, AF.Exp)
        ut = Rot([fw.sb("sw_u", [128, 1536], F32) for _ in range(2)])
        qn_r = Rot([fw.sb("sw_qn", [128, 16, 64], F32) for _ in range(2)])
        kn_r = Rot([fw.sb("sw_kn", [128, 4, 64], F32) for _ in range(2)])
        qf_r = Rot([fw.sb("sw_qf", [128, 16, 64], BF16) for _ in range(2)])
        kf_r = Rot([fw.sb("sw_kf", [128, 4, 64], BF16) for _ in range(2)])
        qT_r = Rot([fw.sb("sw_qT", [64, 16, 128], BF16) for _ in range(2)])
        kT = Rot([fw.sb("sw_kT", [64, 4, 128], BF16) for _ in range(3)])
        V1 = Rot([fw.sb("sw_V1", [128, 4, 66], BF16) for _ in range(3)])
        E = Rot([fw.sb("sw_E", [128, 4, 128], BF16) for _ in range(4)])
        den = fw.sb("sw_den", [128, 4], F32)
        yt = Rot([fw.sb("sw_y", [128, 16, 64], BF16) for _ in range(2)])
        psT = Rot([fw.ps("sw_psT", [128, 8, 128], BF16) for _ in range(2)])
        pss = Rot([fw.ps("sw_pss", [128, 512], F32) for _ in range(3)])
        pso = Rot([fw.ps("sw_pso", [128, 4, 128], F32) for _ in range(2)])
        for v_ in V1.items:
            fw.memset("dve", v_, 1.0)

        def prep(n):
            u = ut.next()
            qn, kn, qf, kf, qT = qn_r.next(), kn_r.next(), qf_r.next(), kf_r.next(), qT_r.next()
            fw.dma("sp", u, u_blk[n][:, 0:1536])
            head_norm(fw, u[:, 0:1024].re("p (h d) -> p h d", h=16), gq, qn, 16, 64)
            rope(fw, qn, qf, C.cos64[:, n, :], C.sin64[:, n, :], 16, 64, eng="dve")
            head_norm(fw, u[:, 1024:1280].re("p (h d) -> p h d", h=4), gk, kn, 4, 64)
            rope(fw, kn, kf, C.cos64[:, n, :], C.sin64[:, n, :], 4, 64)
            V1c = V1.next()
            fw.copy("act", V1c[:, :, 0:64], u[:, 1280:1536].re("p (h d) -> p h d", h=4))
            for h0 in (0, 8):
                p = psT.next()
                for h in range(8):
                    fw.tr(p[0:64, h, :], qf[:, h0 + h, :], C.ident)
                fw.copy("dve", qT[:, h0:h0 + 8, :], p[0:64, :, :])
            kTc = kT.next()
            p = psT.next()
            for h in range(4):
                fw.tr(p[0:64, h, :], kf[:, h, :], C.ident)
            fw.copy("dve", kTc, p[0:64, 0:4, :])
            return (qT, kTc, V1c)

        def attn(n, cur, prev):
            qT, kTc, V1c = cur
            y = yt.next()
            items = []
            for g in range(4):
                if prev is not None:
                    items.append((g, prev[1], prev[2], C.negL4, False))
                items.append((g, kTc, V1c, C.negU4, True))

            def scores(it):
                g, kt_, v1_, neg, last = it
                ps_ = pss.next()
                fw.mm(ps_, kt_[:, g, :], qT[:, 4 * g:4 * g + 4, :].re("p a b -> p (a b)"), start=True, stop=False)
                fw.mm(ps_, C.ident, neg.re("p a b -> p (a b)"), start=False, stop=True)
                return ps_
            pend = scores(items[0])
            es = []
            for i, it in enumerate(items):
                ps_ = pend
                if i + 1 < len(items):
                    pend = scores(items[i + 1])
                g, kt_, v1_, neg, last = it
                e_ = E.next()
                fw.act(e_.re("p a b -> p (a b)"), ps_, AF.Exp)
                es.append((e_, v1_))
                if last:
                    po = pso.next()
                    for hq in range(4):
                        for j, (ee, vv) in enumerate(es):
                            fw.mm(po[:, hq, 0:65], ee[:, hq, :], vv[:, g, 0:65], start=(j == 0), stop=(j == len(es) - 1))
                    fw.tt("dve", den, po[:, :, 64], esink[:, 4 * g:4 * g + 4], ALU.add)
                    fw.recip(den, den)
                    fw.tt("dve", y[:, 4 * g:4 * g + 4, :], po[:, :, 0:64], den.un(2).bc([128, 4, 64]), ALU.mult)
                    es = []
            fw.dma("act", yA_blk[n][:, 0:1024], y.re("p h d -> p (h d)"))

        cur = prep(0)
        prev = None
        for n in range(NB):
            nxt = prep(n + 1) if n + 1 < NB else None
            attn(n, cur, prev)
            prev, cur = cur, nxt


def rwkv_prep_phase(fw, C, I, u, u_blk, rowsTM_blk, gS_blk, bonS_blk):
    RW0 = 1536
    RWW = 3360
    with fw.phase():
        mu = load_bc(fw, "rk_mu", I["rwkv_mu"][0], RWW)
        w0 = load_bc(fw, "rk_w0", I["rwkv_w0"][0], 1024)
        a0 = load_bc(fw, "rk_a0", I["rwkv_a0"][0], 1024)
        kkw = load_bc(fw, "rk_kk", I["rwkv_k_k"][0], 1024)
        kaw = load_bc(fw, "rk_ka", I["rwkv_k_a"][0], 1024)
        rkw = load_bc(fw, "rk_rk", I["rwkv_r_k"][0], 1024)
        w2 = fw.sb("rk_w2", [64, 1024], BF16)
        a2 = fw.sb("rk_a2", [64, 1024], BF16)
        g2a = fw.sb("rk_g2a", [128, 1024], BF16)
        g2b = fw.sb("rk_g2b", [32, 1024], BF16)
        fw.dma("pool", w2, I["rwkv_w2"][0])
        fw.dma("pool", a2, I["rwkv_a2"][0])
        fw.dma("pool", g2a, I["rwkv_g2"][0][0:128, :])
        fw.dma("pool", g2b, I["rwkv_g2"][0][128:160, :])
        pt = Rot([fw.sb("rk_p", [128, RWW], F32) for _ in range(2)])
        pv = Rot([fw.sb("rk_pv", [128, RWW], F32) for _ in range(1)])
        xs = fw.sb("rk_xs", [128, RWW], F32)
        lo = fw.sb("rk_lo", [128, 288], BF16)
        loT = fw.sb("rk_loT", [128, 4, 128], BF16)
        t1 = fw.sb("rk_t1", [128, 1024], F32)
        t2 = fw.sb("rk_t2", [128, 1024], F32)
        aa = fw.sb("rk_a", [128, 1024], F32)
        kk = fw.sb("rk_kkn", [128, 1024], F32)
        st = fw.sb("rk_st", [128, 4, 16], F32)
        rows = Rot([fw.sb("rk_rows", [128, 6, 1024], F32) for _ in range(1)])
        gg = Rot([fw.sb("rk_g", [128, 1024], F32) for _ in range(1)])
        bon = Rot([fw.sb("rk_bon", [128, 1024], F32) for _ in range(1)])
        psT = Rot([fw.ps("rk_psT", [128, 8, 128], BF16) for _ in range(1)])
        psm = Rot([fw.ps("rk_psm", [128, 512], F32) for _ in range(4)])
        H3 = lambda v: v.re("p (h d) -> p h d", h=16)
        for n in range(NB):
            p = pt.next()
            pr = pv.next()
            fw.dma("sp", p, u_blk[n][:, RW0:RW0 + RWW])
            if n == 0:
                fw.memset("dve", pr, 0.0)
                fw.dma("act", pr[1:128, :], u[0:127, RW0:RW0 + RWW])
            else:
                fw.dma("act", pr, u[n * 128 - 1:n * 128 + 127, RW0:RW0 + RWW])
            fw.tt("dve", pr, pr, p, ALU.subtract)
            fw.tt("pool", pr, pr, mu, ALU.mult)
            fw.tt("dve", xs, pr, p, ALU.add)
            r_ = xs[:, 0:1024]
            k_ = xs[:, 1024:2048]
            v_ = xs[:, 2048:3072]
            fw.act(lo[:, 0:64], xs[:, 3072:3136], AF.Tanh)
            fw.copy("dve", lo[:, 64:128], xs[:, 3136:3200])
            fw.act(lo[:, 128:288], xs[:, 3200:3360], AF.Sigmoid)
            ptt = psT.next()
            fw.tr(ptt[0:64, 0, :], lo[:, 0:64], C.ident)
            fw.tr(ptt[0:64, 1, :], lo[:, 64:128], C.ident)
            fw.tr(ptt[:, 2, :], lo[:, 128:256], C.ident)
            fw.tr(ptt[0:32, 3, :], lo[:, 256:288], C.ident)
            fw.copy("dve", loT[0:64, 0:2, :], ptt[0:64, 0:2, :])
            fw.copy("dve", loT[:, 2, :], ptt[:, 2, :])
            fw.copy("dve", loT[0:32, 3, :], ptt[0:32, 3, :])
            R = rows.next()
            g_ = gg.next()
            for cb in range(2):
                cs = slice(cb * 512, (cb + 1) * 512)
                pw = psm.next()
                fw.mm(pw, loT[0:64, 0, :], w2[:, cs])
                pa = psm.next()
                fw.mm(pa, loT[0:64, 1, :], a2[:, cs])
                pg = psm.next()
                fw.mm(pg, loT[:, 2, :], g2a[:, cs], start=True, stop=False)
                fw.mm(pg, loT[0:32, 3, :], g2b[:, cs], start=False, stop=True)
                fw.copy("act", g_[:, cs], pg)
                fw.tt("dve", t1[:, cs], pw, w0[:, cs], ALU.add)
                fw.tt("dve", aa[:, cs], pa, a0[:, cs], ALU.add)
            fw.act(t1, t1, AF.Exp, scale=-1.0)
            fw.act(t1, t1, AF.Ln, bias=1.0)
            fw.act(t1, t1, AF.Exp, scale=-1.0, bias=C.cst[:, 3:4])
            fw.copy("act", R[:, 1, :], t1)
            fw.act(aa, aa, AF.Sigmoid)
            fw.tt("pool", kk, k_, kkw, ALU.mult)
            fw.act(t2, kk, AF.Square)
            fw.red("dve", st[:, 0, :], H3(t2))
            fw.act(st[:, 1, :], st[:, 0, :], AF.Sqrt)
            fw.ts("dve", st[:, 1, :], st[:, 1, :], 1e-12, ALU.max)
            fw.recip(st[:, 2, :], st[:, 1, :])
            fw.tt("dve", H3(kk), H3(kk), st[:, 2, :].un(2).bc([128, 16, 64]), ALU.mult)
            fw.op("act", lambda: fw.nc.scalar.mul(R[:, 0, :].ap, kk.ap, -1.0), [kk], [R[:, 0, :]])
            fw.tt("pool", R[:, 2, :], kk, aa, ALU.mult)
            fw.ts("dve", t2, aa, -1.0, ALU.add)
            fw.tt("dve", t2, t2, kaw, ALU.mult)
            fw.stt("dve", R[:, 3, :], t2, 1.0, k_, ALU.add, ALU.mult)
            fw.copy("act", R[:, 4, :], r_)
            fw.tt("dve", t2, r_, rkw, ALU.mult)
            fw.tt("dve", t2, t2, R[:, 3, :], ALU.mult)
            fw.red("dve", st[:, 3, :], H3(t2))
            b_ = bon.next()
            fw.tt("dve", H3(b_), H3(v_), st[:, 3, :].un(2).bc([128, 16, 64]), ALU.mult)
            fw.dma("sp", gS_blk[n], g_)
            fw.dma("sp", bonS_blk[n], b_)
            fw.copy("act", R[:, 5, :], v_)
            fw.dma("act", rowsTM_blk[n], R.re("p a c -> p (a c)"))


def rwkv_scan_phase(fw, C, rowS, vS, yS):
    TB = 32
    with fw.phase():
        St = [fw.sb("sc_S", [128, 8, 64], F32) for _ in range(2)]
        A = fw.sb("sc_A", [128, 8, 64], F32)
        B = fw.sb("sc_B", [128, 8, 64], F32)
        Cc = Rot([fw.sb("sc_C", [128, 8, 64], F32) for _ in range(2)])
        Ee_r = Rot([fw.sb("sc_E", [128, 8, 64], F32) for _ in range(2)])
        sa = fw.sb("sc_sa", [128, 8], F32)
        rb = Rot([fw.sb("sc_rows", [128, 5, TB, 64], F32) for _ in range(2)])
        vb = Rot([fw.sb("sc_v", [128, TB, 8], F32) for _ in range(2)])
        yb = Rot([fw.sb("sc_y", [128, TB, 8], F32) for _ in range(2)])
        fw.memset("dve", St[0], 0.0)
        cur = 0
        qs = Rot(["sp", "act"])
        for b in range(S // TB):
            t0 = b * TB
            R = rb.next()
            vv = vb.next()
            yy = yb.next()
            for ty in range(5):
                for h in range(16):
                    fw.dma(qs.next(), R[h * 8:(h + 1) * 8, ty, :, :], rowS[ty, h, t0:t0 + TB, :].pb(8))
            fw.dma(qs.next(), vv, vS[:, t0:t0 + TB, :])
            for t in range(TB):
                S0 = St[cur]
                S1 = St[1 - cur]
                rowb = lambda ty: R[:, ty, t, :].un(1).bc([128, 8, 64])
                c_ = Cc.next()
                fw.tt("dve", c_, rowb(3), vv[:, t, :].un(2).bc([128, 8, 64]), ALU.mult)
                fw.tt("dve", A, S0, rowb(0), ALU.mult)
                fw.red("dve", sa, A)
                fw.tt("dve", S1, S0, rowb(1), ALU.mult)
                fw.tt("dve", B, rowb(2), sa.un(2).bc([128, 8, 64]), ALU.mult)
                fw.tt("dve", S1, S1, B, ALU.add)
                fw.tt("dve", S1, S1, c_, ALU.add)
                Ee = Ee_r.next()
                fw.tt("dve", Ee, S1, rowb(4), ALU.mult)
                fw.red("dve", yy[:, t, :], Ee)
                cur = 1 - cur
            fw.dma(qs.next(), yS[:, t0:t0 + TB, :], yy)


def rwkv_chunk_phase(fw, C, rowsTM_blk, yTM_blk):
    HG = 8
    with fw.phase():
        triU = fw.sb("ck_tri", [128, 128], F32)
        mk2 = fw.sb("ck_mk2", [128, 2, 128], F32)
        mSL = fw.sb("ck_msl", [128, 128], F32)
        ones = fw.sb("ck_ones", [128, 1], F32)
        fw.copy("dve", triU, C.maskU)
        fw.copy("dve", mk2[:, 1, :], C.maskU)
        fw.tt("dve", mk2[:, 0, :], C.maskU, C.ident, ALU.subtract)
        fw.copy("dve", mSL, C.maskL)
        fw.memset("dve", ones, 1.0)
        H32 = [fw.sb("ck_H32", [64, 64], F32) for _ in range(16)]
        Hb = [fw.sb("ck_Hb", [64, 64], BF16) for _ in range(16)]
        for h in range(16):
            fw.memset("dve", H32[h], 0.0)
            fw.memset("dve", Hb[h], 0.0)
        inb = Rot([fw.sb("ck_in", [128, 6, 1024], F32) for _ in range(2)])
        Lt = fw.sb("ck_L", [128, 1024], F32)
        Pt = fw.sb("ck_Pt", [128, 1024], F32)
        Pinv = fw.sb("ck_Pinv", [128, 1024], F32)
        Pm1 = fw.sb("ck_Pm1", [128, 1024], F32)
        At = fw.sb("ck_At", [128, 1024], BF16)
        Bt = fw.sb("ck_Bt", [128, 1024], BF16)
        Kt = fw.sb("ck_Kt", [128, 1024], BF16)
        Rt = fw.sb("ck_Rt", [128, 1024], BF16)
        Vt = fw.sb("ck_Vt", [128, 1024], BF16)
        PC = fw.sb("ck_PC", [64, 16], F32)
        ytile = Rot([fw.sb("ck_y", [128, 1024], F32) for _ in range(2)])
        T4s = [fw.sb("ck_T4", [64, 4, 128], BF16) for _ in range(HG)]
        G1s = [fw.sb("ck_G1", [128, 2, 128], BF16) for _ in range(HG)]
        G2s = [fw.sb("ck_G2", [128, 2, 128], BF16) for _ in range(HG)]
        Ns = [[fw.sb("ck_N", [128, 128], BF16) for _ in range(HG)] for _ in range(2)]
        Ms = [[fw.sb("ck_M", [128, 128], BF16) for _ in range(HG)] for _ in range(2)]
        Zs = [[fw.sb("ck_Z", [128, 128], BF16) for _ in range(HG)] for _ in range(2)]
        Yvs = [fw.sb("ck_Yv", [128, 64], F32) for _ in range(HG)]
        Gps = [fw.sb("ck_Gp", [64, 64], F32) for _ in range(HG)]
        WTs = [fw.sb("ck_WT", [64, 128], BF16) for _ in range(HG)]
        Us = [fw.sb("ck_U", [128, 64], BF16) for _ in range(HG)]
        tHs = [fw.sb("ck_tH", [64, 64], F32) for _ in range(HG)]
        banks = [fw.ps("ck_pp", [128, 512], F32) for _ in range(7)]
        for b_ in banks:
            b_.buf.excl = True
        pp = Rot([V(b_.buf, b_.ap[:, hh * 256:(hh + 1) * 256]) for hh in range(2) for b_ in banks])
        bankb = fw.ps("ck_pb", [128, 1024], BF16)
        bankb.buf.excl = True
        pb = Rot([V(bankb.buf, bankb.ap[:, hh * 512:(hh + 1) * 512]) for hh in range(2)])
        ev = Rot(["act", "dve"])
        for n in range(NB):
            X = inb.next()
            fw.dma("sp", X.re("p a c -> p (a c)"), rowsTM_blk[n])
            nkk, ew, kka, km, r_, v_ = (X[:, j, :] for j in range(6))
            for c4 in range(4):
                cs = slice(c4 * 256, (c4 + 1) * 256)
                p = pp.next()
                fw.mm(p, triU, ew[:, cs])
                fw.act(Pinv[:, cs], p, AF.Exp)
                fw.act(Pt[:, cs], p, AF.Exp, scale=-1.0)
                fw.tt("dve", Lt[:, cs], ew[:, cs], p, ALU.subtract)
                fw.act(Pm1[:, cs], Lt[:, cs], AF.Exp)
            pc = pp.next()
            for h in range(16):
                fw.mm(pc[0:64, h:h + 1], ew[:, h * 64:(h + 1) * 64], ones)
            fw.act(PC, pc[0:64, 0:16], AF.Exp, scale=-1.0)
            fw.tt("dve", At, nkk, Pm1, ALU.mult)
            fw.tt("pool", Bt, kka, Pinv, ALU.mult)
            fw.tt("pool", Kt, km, Pinv, ALU.mult)
            fw.tt("dve", Rt, r_, Pt, ALU.mult)
            fw.copy("act", Vt, v_)
            y = ytile.next()
            for g0 in range(0, 16, HG):
                hs = list(range(g0, g0 + HG))
                for h in hs:
                    i = h % HG
                    sl = slice(h * 64, (h + 1) * 64)
                    pT = pb.next()
                    pv = pT[0:64, :].re("p (a t) -> p a t", a=4)
                    for j, src in enumerate((At, Rt, Bt, Kt)):
                        fw.tr(pv[:, j, :], src[:, sl], C.ident)
                    T4 = T4s[i]
                    fw.copy("act", T4, pv)
                    ar = T4[:, 0:2, :].re("p a t -> p (a t)")
                    p1 = pp.next()
                    fw.mm(p1, T4[:, 2, :], ar)
                    p2 = pp.next()
                    fw.mm(p2, T4[:, 3, :], ar)
                    p3 = pp.next()
                    fw.mm(p3[:, 0:128], T4[:, 0, :], T4[:, 2, :])
                    fw.tt("dve", G1s[i], p1.re("p (a t) -> p a t", a=2), mk2, ALU.mult)
                    fw.tt("dve", G2s[i], p2.re("p (a t) -> p a t", a=2), mk2, ALU.mult)
                    fw.tt("dve", Ms[0][i], p3[:, 0:128], mSL, ALU.mult)
                    p4 = pp.next()
                    fw.mm(p4[:, 0:64], G2s[i][:, 0, :], Vt[:, sl])
                    fw.mm(p4[:, 64:128], G2s[i][:, 1, :], Vt[:, sl])
                    fw.copy("act", Zs[0][i][:, 0:64], At[:, sl])
                    fw.copy("act", Zs[0][i][:, 64:128], p4[:, 0:64])
                    fw.copy("dve", Yvs[i], p4[:, 64:128])
                    p5 = pp.next()
                    fw.mm(p5[0:64, 0:64], Kt[:, sl], Vt[:, sl])
                    fw.ts("dve", Gps[i], p5[0:64, 0:64], PC[:, h:h + 1], ALU.mult)
                for k in range(7):
                    for h in hs:
                        i = h % HG
                        Nk = G1s[i][:, 0, :] if k == 0 else Ns[k % 2][i]
                        Mk = Ms[k % 2][i]
                        Zk = Zs[k % 2][i]
                        pz = pp.next()
                        fw.mm(pz[:, 0:128], Nk, Zk)
                        fw.tt("dve", Zs[(k + 1) % 2][i], Zk, pz[:, 0:128], ALU.add)
                        if k < 6:
                            pn = pp.next()
                            fw.mm(pn[:, 0:128], Mk, Nk)
                            fw.copy("act", Ns[(k + 1) % 2][i], pn[:, 0:128])
                            pm = pp.next()
                            fw.mm(pm[:, 0:128], Nk, Mk)
                            fw.copy(ev.next(), Ms[(k + 1) % 2][i], pm[:, 0:128])
                for h in hs:
                    i = h % HG
                    sl = slice(h * 64, (h + 1) * 64)
                    Zf = Zs[1][i]
                    pT = pb.next()
                    fw.tr(pT[0:64, 0:128], Zf[:, 0:64], C.ident)
                    fw.copy("act", WTs[i], pT[0:64, 0:128])
                    pu = pp.next()
                    fw.mm(pu[:, 0:64], WTs[i], Hb[h])
                    fw.tt("dve", Us[i], pu[:, 0:64], Zf[:, 64:128], ALU.add)
                    py = pp.next()
                    fw.mm(py[:, 0:64], T4s[i][:, 1, :], Hb[h], start=True, stop=False)
                    fw.mm(py[:, 0:64], G1s[i][:, 1, :], Us[i], start=False, stop=True)
                    fw.tt("dve", y[:, sl], py[:, 0:64], Yvs[i], ALU.add)
                    ph = pp.next()
                    fw.mm(ph[0:64, 0:64], Bt[:, sl], Us[i])
                    fw.tt("dve", tHs[i], ph[0:64, 0:64], H32[h], ALU.add)
                    fw.stt("dve", H32[h], tHs[i], PC[:, h:h + 1], Gps[i], ALU.mult, ALU.add)
                    fw.copy("act", Hb[h], H32[h])
            fw.dma("sp", yTM_blk[n], y)


def rwkv_post_phase(fw, C, I, yTM_blk, gS_blk, bonS_blk, yA_blk):
    with fw.phase():
        gng = load_bc(fw, "rp_g", I["rwkv_gn_g"][0], 1024)
        gnb = load_bc(fw, "rp_b", I["rwkv_gn_b"][0], 1024)
        yt = Rot([fw.sb("rp_y", [128, 16, 64], F32) for _ in range(2)])
        gt = Rot([fw.sb("rp_gt", [128, 1024], F32) for _ in range(2)])
        bt = Rot([fw.sb("rp_bt", [128, 1024], F32) for _ in range(2)])
        sq = fw.sb("rp_sq", [128, 16, 64], F32)
        st = fw.sb("rp_st", [128, 4, 16], F32)
        ob = Rot([fw.sb("rp_o", [128, 1024], BF16) for _ in range(2)])
        F2 = lambda v: v.re("p h d -> p (h d)")
        for n in range(NB):
            y = yt.next()
            g_ = gt.next()
            b_ = bt.next()
            fw.dma("sp", y.re("p h d -> p (h d)"), yTM_blk[n])
            fw.dma("act", g_, gS_blk[n])
            fw.dma("act", b_, bonS_blk[n])
            fw.red("dve", st[:, 0, :], y)
            fw.ts("dve", st[:, 0, :], st[:, 0, :], 1.0 / 64, ALU.mult)
            fw.tt("dve", y, y, st[:, 0, :].un(2).bc([128, 16, 64]), ALU.subtract)
            fw.tt("pool", sq, y, y, ALU.mult)
            fw.red("dve", st[:, 1, :], sq)
            rstd_from_ss(fw, st[:, 1, :], 64, 64e-5, st[:, 2, :], st[:, 3, :])
            fw.tt("dve", y, y, st[:, 3, :].un(2).bc([128, 16, 64]), ALU.mult)
            fw.tt("pool", F2(y), F2(y), gng, ALU.mult)
            fw.tt("pool", F2(y), F2(y), gnb, ALU.add)
            fw.tt("dve", F2(y), F2(y), b_, ALU.add)
            o_ = ob.next()
            fw.tt("dve", o_, F2(y), g_, ALU.mult)
            fw.dma("sp", yA_blk[n][:, 1024:2048], o_)


def causal_attn(fw, C, heads, dv, pss, pso, E, masks_eng):
    groups = []
    for hi in range(len(heads)):
        for qi in range(NB):
            for sg in range(0, qi + 1, 4):
                groups.append((hi, qi, sg, min(4, qi + 1 - sg)))
    loaded = {}

    def qk(g):
        hi, qi, sg, n = g
        if hi not in loaded:
            loaded[hi] = heads[hi]["load"]()
        qT, kT, V1 = loaded[hi]
        ps_ = pss.next()
        for j in range(n):
            dg = (sg + j == qi)
            fw.mm(ps_[:, j * 128:(j + 1) * 128], kT[:, (sg + j) * 128:(sg + j + 1) * 128], qT[:, qi * 128:(qi + 1) * 128],
                  start=True, stop=not dg)
            if dg:
                fw.mm(ps_[:, j * 128:(j + 1) * 128], C.ident, C.negU4[:, 0, :], start=False, stop=True)
        return ps_
    pend = qk(groups[0])
    po = None
    for gi, g in enumerate(groups):
        ps_ = pend
        if gi + 1 < len(groups):
            pend = qk(groups[gi + 1])
        hi, qi, sg, n = g
        qT, kT, V1 = loaded[hi]
        if sg == 0:
            po = pso.next()
        e_ = E.next()
        fw.act(e_[:, 0:n * 128], ps_[:, 0:n * 128], AF.Exp)
        for j in range(n):
            fw.mm(po[:, 0:dv + 1], e_[:, j * 128:(j + 1) * 128], V1[:, sg + j, 0:dv + 1], start=(sg + j == 0), stop=(sg + j == qi))
        if sg + n - 1 == qi:
            heads[hi]["on_out"](qi, po)


def mla_prep_phase(fw, C, I, u_blk, qTS, kTS, V1S_blk):
    sc = 96 ** -0.5
    with fw.phase():
        gcq = load_bc(fw, "ml_gcq", I["mla_cq_norm"][0], 512)
        gckv = load_bc(fw, "ml_gckv", I["mla_ckv_norm"][0], 256)
        gqn = load_bc(fw, "ml_gqn", I["mla_q_nope_norm"][0], 64, mul=sc)
        gkn = load_bc(fw, "ml_gkn", I["mla_k_nope_norm"][0], 64)
        gqr = load_bc(fw, "ml_gqr", I["mla_q_rope_norm"][0], 32, mul=sc)
        gkr = load_bc(fw, "ml_gkr", I["mla_k_rope_norm"][0], 32)
        wuq = fw.sb("ml_wuq", [128, 4, 1536], BF16)
        wukv = fw.sb("ml_wukv", [128, 2, 2048], BF16)
        fw.dma("pool", wuq, wview(I["mla_w_uq"][0], 0, 1536))
        fw.dma("pool", wukv, wview(I["mla_w_ukv"][0], 0, 2048))
        ut = Rot([fw.sb("ml_u", [128, 800], F32) for _ in range(2)])
        cb = fw.sb("ml_cb", [128, 768], BF16)
        junk = fw.sb("ml_junk", [128, 512], BF16)
        st = fw.sb("ml_st", [128, 8], F32)
        cT = fw.sb("ml_cT", [128, 6, 128], BF16)
        qfull = fw.sb("ml_qfull", [128, 16, 96], F32)
        kvfull = fw.sb("ml_kvfull", [128, 16, 128], F32)
        tq = fw.sb("ml_tq", [128, 16, 64], F32)
        tr_ = fw.sb("ml_tr", [128, 16, 32], F32)
        tkr = fw.sb("ml_tkr", [128, 1, 32], F32)
        kpe = fw.sb("ml_kpe", [128, 1, 32], BF16)
        qf = fw.sb("ml_qf", [128, 16, 96], BF16)
        kf = fw.sb("ml_kf", [128, 16, 96], BF16)
        V1 = Rot([fw.sb("ml_V1", [128, 16, 66], BF16) for _ in range(2)])
        qTt = Rot([fw.sb("ml_qTt", [96, 16, 128], BF16) for _ in range(2)])
        kTt = Rot([fw.sb("ml_kTt", [96, 16, 128], BF16) for _ in range(2)])
        for v_ in V1.items:
            fw.memset("dve", v_, 1.0)
        psT = Rot([fw.ps("ml_psT", [128, 8, 128], BF16) for _ in range(2)])
        psm = Rot([fw.ps("ml_psm", [128, 512], F32) for _ in range(4)])
        for n in range(NB):
            u = ut.next()
            fw.dma("sp", u, u_blk[n][:, 0:800])
            for (c0, c1, g_, j0) in ((0, 512, gcq, 0), (512, 768, gckv, 3)):
                w = c1 - c0
                fw.act(junk[:, 0:w], u[:, c0:c1], AF.Square, accum=st[:, j0:j0 + 1])
                rstd_from_ss(fw, st[:, j0:j0 + 1], w, EPS, st[:, j0 + 1:j0 + 2], st[:, j0 + 2:j0 + 3])
                fw.stt("dve", cb[:, c0:c1], u[:, c0:c1], st[:, j0 + 2:j0 + 3], g_, ALU.mult, ALU.mult)
            p = psT.next()
            for j in range(6):
                fw.tr(p[:, j, :], cb[:, j * 128:(j + 1) * 128], C.ident)
            fw.copy("dve", cT, p[:, 0:6, :])
            for cbk in range(3):
                pq = psm.next()
                for k in range(4):
                    fw.mm(pq, cT[:, k, :], wuq[:, k, cbk * 512:(cbk + 1) * 512], start=(k == 0), stop=(k == 3))
                fw.copy("act", qfull.re("p h d -> p (h d)")[:, cbk * 512:(cbk + 1) * 512], pq)
            for cbk in range(4):
                pk = psm.next()
                for k in range(2):
                    fw.mm(pk, cT[:, 4 + k, :], wukv[:, k, cbk * 512:(cbk + 1) * 512], start=(k == 0), stop=(k == 1))
                fw.copy("act", kvfull.re("p h d -> p (h d)")[:, cbk * 512:(cbk + 1) * 512], pk)
            cs32 = (C.cos32[:, n, :], C.sin32[:, n, :])
            head_norm(fw, qfull[:, :, 0:64], gqn, qf[:, :, 0:64], 16, 64)
            head_norm(fw, qfull[:, :, 64:96], gqr, tr_, 16, 32)
            rope(fw, tr_, qf[:, :, 64:96], cs32[0], cs32[1], 16, 32)
            head_norm(fw, kvfull[:, :, 0:64], gkn, kf[:, :, 0:64], 16, 64)
            head_norm(fw, u[:, 768:800].re("p (h d) -> p h d", h=1), gkr, tkr, 1, 32)
            rope(fw, tkr, kpe, cs32[0], cs32[1], 1, 32)
            fw.copy("dve", kf[:, :, 64:96], kpe.bc([128, 16, 32]))
            V1c = V1.next()
            fw.copy("act", V1c[:, :, 0:64], kvfull[:, :, 64:128])
            fw.dma("sp", V1S_blk[n], V1c.re("p h d -> p (h d)"))
            for (src, dstS, rot) in ((qf, qTS, qTt), (kf, kTS, kTt)):
                tt_ = rot.next()
                for h0 in (0, 8):
                    p = psT.next()
                    for h in range(8):
                        fw.tr(p[0:96, h, :], src[:, h0 + h, :], C.ident)
                    fw.copy("dve", tt_[:, h0:h0 + 8, :], p[0:96, :, :])
                fw.dma("act", dstS[:, :, n * 128:(n + 1) * 128].re("h d t -> d h t"), tt_)


def mla_attn_phase(fw, C, qTS, kTS, V1S, yA):
    with fw.phase():
        qT = Rot([fw.sb("ma_qT", [96, S], BF16) for _ in range(2)])
        kT = Rot([fw.sb("ma_kT", [96, S], BF16) for _ in range(2)])
        V1 = Rot([fw.sb("ma_V1", [128, NB, 66], BF16) for _ in range(2)])
        E = Rot([fw.sb("ma_E", [128, 512], BF16) for _ in range(3)])
        yo = Rot([fw.sb("ma_yo", [128, NB, 64], BF16) for _ in range(2)])
        rec = fw.sb("ma_rec", [128, 2], F32)
        pss = Rot([fw.ps("ma_pss", [128, 512], F32) for _ in range(3)])
        pso = Rot([fw.ps("ma_pso", [128, 512], F32) for _ in range(2)])
        heads = []
        for h in range(16):
            def load(h=h):
                q_ = qT.next()
                k_ = kT.next()
                v_ = V1.next()
                fw.dma("sp", q_, qTS[h])
                fw.dma("sp", k_, kTS[h])
                fw.dma("sp", v_, V1S[:, h * 66:(h + 1) * 66].re("(n p) d -> p n d", p=128))
                return (q_, k_, v_)
            st_ = {}

            def on_out(qi, po, h=h, st_=st_):
                if qi == 0:
                    st_["y"] = yo.next()
                y_ = st_["y"]
                fw.recip(rec[:, 0:1], po[:, 64:65])
                fw.ts("dve", y_[:, qi, :], po[:, 0:64], rec[:, 0:1], ALU.mult)
                if qi == NB - 1:
                    fw.dma("act", yA[:, h * 64:(h + 1) * 64].re("(n p) d -> p n d", p=128), y_)
            heads.append({"load": load, "on_out": on_out})
        causal_attn(fw, C, heads, 64, pss, pso, E, "pool")


def diff_prep_phase(fw, C, I, u_blk, qTS, kTS, V1S_blk):
    with fw.phase():
        gq = load_bc(fw, "df_gq", I["diff_q_norm"][0], 64, mul=0.125)
        gk = load_bc(fw, "df_gk", I["diff_k_norm"][0], 64)
        ut = Rot([fw.sb("df_u", [128, 3072], F32) for _ in range(2)])
        tn_r = Rot([fw.sb("df_tn", [128, 16, 64], F32) for _ in range(2)])
        qf_r = Rot([fw.sb("df_qf", [128, 16, 64], BF16) for _ in range(2)])
        kf_r = Rot([fw.sb("df_kf", [128, 16, 64], BF16) for _ in range(2)])
        V1 = Rot([fw.sb("df_V1", [128, 8, 130], BF16) for _ in range(2)])
        qTt = Rot([fw.sb("df_qTt", [64, 16, 128], BF16) for _ in range(2)])
        kTt = Rot([fw.sb("df_kTt", [64, 16, 128], BF16) for _ in range(2)])
        for v_ in V1.items:
            fw.memset("dve", v_, 1.0)
        psT = Rot([fw.ps("df_psT", [128, 8, 128], BF16) for _ in range(2)])
        for n in range(NB):
            u = ut.next()
            fw.dma("sp", u, u_blk[n][:, 800:3872])
            cs = (C.cos64[:, n, :], C.sin64[:, n, :])
            qf, kf = qf_r.next(), kf_r.next()
            tn = tn_r.next()
            head_norm(fw, u[:, 0:1024].re("p (h d) -> p h d", h=16), gq, tn, 16, 64)
            rope(fw, tn, qf, cs[0], cs[1], 16, 64, eng="dve")
            tn = tn_r.next()
            head_norm(fw, u[:, 1024:2048].re("p (h d) -> p h d", h=16), gk, tn, 16, 64)
            rope(fw, tn, kf, cs[0], cs[1], 16, 64)
            V1c = V1.next()
            fw.copy("act", V1c[:, :, 0:128], u[:, 2048:3072].re("p (h d) -> p h d", h=8))
            fw.dma("sp", V1S_blk[n], V1c.re("p h d -> p (h d)"))
            for (src, dstS, rot) in ((qf, qTS, qTt), (kf, kTS, kTt)):
                tt_ = rot.next()
                for h0 in (0, 8):
                    p = psT.next()
                    for h in range(8):
                        fw.tr(p[0:64, h, :], src[:, h0 + h, :], C.ident)
                    fw.copy("dve", tt_[:, h0:h0 + 8, :], p[0:64, :, :])
                fw.dma("act", dstS[:, :, n * 128:(n + 1) * 128].re("h d t -> d h t"), tt_)


def diff_attn_phase(fw, C, I, qTS, kTS, V1S, yA):
    lam_init = 0.8 - 0.6 * math.exp(-0.3 * 1)
    with fw.phase():
        lt = [load_bc(fw, "df_l%d" % j, I[nm][0], 64) for j, nm in enumerate(("diff_lq1", "diff_lk1", "diff_lq2", "diff_lk2"))]
        ls = fw.sb("df_ls", [128, 4], F32)
        fw.tt("dve", lt[0], lt[0], lt[1], ALU.mult)
        fw.tt("dve", lt[2], lt[2], lt[3], ALU.mult)
        fw.red("dve", ls[:, 0:1], lt[0])
        fw.red("dve", ls[:, 1:2], lt[2])
        fw.act(ls[:, 0:2], ls[:, 0:2], AF.Exp)
        fw.tt("dve", ls[:, 2:3], ls[:, 0:1], ls[:, 1:2], ALU.subtract)
        fw.ts("dve", ls[:, 3:4], ls[:, 2:3], lam_init, ALU.add, -1.0, ALU.mult)
        gsub = load_bc(fw, "df_gs", I["diff_subln"][0], 128, mul=(1.0 - lam_init))
        qT = Rot([fw.sb("da_qT", [64, S], BF16) for _ in range(2)])
        kT = Rot([fw.sb("da_kT", [64, S], BF16) for _ in range(2)])
        V1 = Rot([fw.sb("da_V1", [128, NB, 130], BF16) for _ in range(2)])
        E = Rot([fw.sb("da_E", [128, 512], BF16) for _ in range(3)])
        o1 = fw.sb("da_o1", [128, NB, 128], F32)
        o2 = fw.sb("da_o2", [128, 128], F32)
        sq = fw.sb("da_sq", [128, 128], F32)
        yo = Rot([fw.sb("da_yo", [128, NB, 128], BF16) for _ in range(2)])
        rec = fw.sb("da_rec", [128, 4], F32)
        pss = Rot([fw.ps("da_pss", [128, 512], F32) for _ in range(3)])
        pso = Rot([fw.ps("da_pso", [128, 512], F32) for _ in range(2)])
        heads = []
        vcur = {}
        for h in range(8):
            for m in range(2):
                def load(h=h, m=m):
                    if m == 0:
                        v_ = V1.next()
                        fw.dma("sp", v_, V1S[:, h * 130:(h + 1) * 130].re("(n p) d -> p n d", p=128))
                        vcur[h] = v_
                    q_ = qT.next()
                    k_ = kT.next()
                    fw.dma("sp", q_, qTS[2 * h + m])
                    fw.dma("sp", k_, kTS[2 * h + m])
                    return (q_, k_, vcur[h])
                st_ = vcur

                def on_out(qi, po, h=h, m=m):
                    fw.recip(rec[:, 0:1], po[:, 128:129])
                    if m == 0:
                        fw.ts("dve", o1[:, qi, :], po[:, 0:128], rec[:, 0:1], ALU.mult)
                    else:
                        if qi == 0:
                            vcur[("y", h)] = yo.next()
                        y_ = vcur[("y", h)]
                        fw.tt("dve", rec[:, 1:2], rec[:, 0:1], ls[:, 3:4], ALU.mult)
                        fw.stt("dve", o2, po[:, 0:128], rec[:, 1:2], o1[:, qi, :], ALU.mult, ALU.add)
                        fw.tt("dve", sq, o2, o2, ALU.mult)
                        fw.red("dve", rec[:, 2:3], sq)
                        rstd_from_ss(fw, rec[:, 2:3], 128, EPS, rec[:, 3:4], rec[:, 2:3])
                        fw.stt("dve", y_[:, qi, :], o2, rec[:, 2:3], gsub, ALU.mult, ALU.mult)
                        if qi == NB - 1:
                            fw.dma("act", yA[:, 1024 + h * 128:1024 + (h + 1) * 128].re("(n p) d -> p n d", p=128), y_)
                heads.append({"load": load, "on_out": on_out})
        causal_attn(fw, C, heads, 128, pss, pso, E, "pool")


WNAMES = ['ffn1_norm', 'ffn1_w_gate', 'ffn1_w_up', 'ffn1_w_down', 'mix_norm', 'ab_w_in', 'ab_w_out', 'swa_q_norm',
          'swa_k_norm', 'swa_sinks', 'rwkv_mu', 'rwkv_w0', 'rwkv_w2', 'rwkv_a0', 'rwkv_a2', 'rwkv_g2', 'rwkv_k_k',
          'rwkv_k_a', 'rwkv_r_k', 'rwkv_gn_g', 'rwkv_gn_b', 'cd_w_in', 'cd_w_out', 'mla_cq_norm', 'mla_ckv_norm',
          'mla_w_uq', 'mla_w_ukv', 'mla_q_nope_norm', 'mla_k_nope_norm', 'mla_q_rope_norm', 'mla_k_rope_norm',
          'diff_q_norm', 'diff_k_norm', 'diff_lq1', 'diff_lk1', 'diff_lq2', 'diff_lk2', 'diff_subln', 'memx_norm',
          'memx_w_q', 'memx_q_norm', 'memx_w_o', 'mem_norm', 'mem_w_kv', 'mem_k_norm', 'ffn2_norm', 'ffn2_w_gate',
          'ffn2_w_up', 'ffn2_w_down']


class Consts:
    pass


def build(shapes, phases=None, debug=()):
    nc = bass.Bass("TRN2", target_bir_lowering=False)
    fw = FW(nc)
    I = {}
    I["x"] = fw.dram("x", [S, D], F32, kind="ExternalInput")
    I["mem"] = fw.dram("mem", [256, D], F32, kind="ExternalInput")
    I["pos"] = fw.dram("pos", [128, NB], I32, kind="ExternalInput")
    for nm in WNAMES:
        I[nm] = fw.dram(nm, list(shapes[nm]), F32, kind="ExternalInput")
    I["c_ident"] = fw.dram("c_ident", [128, 128], BF16, kind="ExternalInput")
    I["c_masku"] = fw.dram("c_masku", [128, 128], BF16, kind="ExternalInput")
    I["c_maskl"] = fw.dram("c_maskl", [128, 128], BF16, kind="ExternalInput")
    I["c_inv64"] = fw.dram("c_inv64", [128, 32], F32, kind="ExternalInput")
    I["c_inv32"] = fw.dram("c_inv32", [128, 16], F32, kind="ExternalInput")
    out = fw.dram("out", [S, D], F32, kind="ExternalOutput")

    def scratch(name, shape, dt):
        return fw.dram(name, shape, dt, kind=("ExternalOutput" if name in debug else "Internal"))

    def blks(v, n=NB):
        return [V(Buf(v.buf.name + "_b%d" % i), v.ap[i * 128:(i + 1) * 128]) for i in range(n)]
    xs = scratch("xs", [S, D], F32)
    xs_blk = blks(xs)
    u = scratch("u", [S, 4896], F32)
    u_blk = blks(u)
    yA = scratch("yA", [S, D], BF16)
    yA_blk = blks(yA)
    rowsTM = scratch("rowsTM", [S, 6 * 1024], F32)
    yTM = scratch("yTM", [S, 1024], F32)
    gS = scratch("gS", [S, 1024], F32)
    bonS = scratch("bonS", [S, 1024], F32)
    qTS = scratch("qTS", [16, 96, S], BF16)
    kTS = scratch("kTS", [16, 96, S], BF16)
    mV1S = scratch("mV1S", [S, 16 * 66], BF16)
    dqTS = scratch("dqTS", [16, 64, S], BF16)
    dkTS = scratch("dkTS", [16, 64, S], BF16)
    dV1S = scratch("dV1S", [S, 8 * 130], BF16)
    C = Consts()
    ph = phases

    def on(p):
        return ph is None or p in ph

    setup_phase(fw, C, I)
    fw.memset("dve", C.cst[:, 3:4], -0.5)
    x_blk = blks(I["x"])
    for i in range(NB):
        fw.dma("sp" if i % 2 else "act", xs_blk[i], x_blk[i])
    for layer in range(2):
        L = "L%d" % layer
        if on(L + "ffn1"):
            ffn_phase(fw, C, xs_blk, I["ffn1_norm"][layer], I["ffn1_w_gate"][layer], I["ffn1_w_up"][layer], I["ffn1_w_down"][layer])
        if layer == 0:
            if on("L0win"):
                mk, ev = evac_store(u_blk)
                linear_phase(fw, C, xs_blk, D, I["mix_norm"][0], I["ab_w_in"][0], 4896, ev, extra=mk)
                fw.barrier()
            if on("L0swa"):
                swa_phase(fw, C, I, u_blk, yA_blk)
            if on("L0rwkv"):
                rwkv_prep_phase(fw, C, I, u, u_blk, blks(rowsTM), blks(gS), blks(bonS))
                rwkv_chunk_phase(fw, C, blks(rowsTM), blks(yTM))
                rwkv_post_phase(fw, C, I, blks(yTM), blks(gS), blks(bonS), yA_blk)
            if on("L0wout"):
                mk, ev = evac_resid(xs_blk)
                linear_phase(fw, C, yA_blk, D, None, I["ab_w_out"][0], D, ev, cast_only=True, src_bf16=True, extra=mk)
        else:
            if on("L1win"):
                mk, ev = evac_store(u_blk)
                linear_phase(fw, C, xs_blk, D, I["mix_norm"][1], I["cd_w_in"][0], 3872, ev, extra=mk)
            if on("L1mla"):
                mla_prep_phase(fw, C, I, u_blk, qTS, kTS, blks(mV1S))
                mla_attn_phase(fw, C, qTS, kTS, mV1S, yA)
            if on("L1diff"):
                diff_prep_phase(fw, C, I, u_blk, dqTS, dkTS, blks(dV1S))
                diff_attn_phase(fw, C, I, dqTS, dkTS, dV1S, yA)
            if on("L1wout"):
                mk, ev = evac_resid(xs_blk)
                linear_phase(fw, C, blks(yA), D, None, I["cd_w_out"][0], D, ev, cast_only=True, src_bf16=True, extra=mk)
        if on(L + "memx"):
            memx_phase(fw, C, I, layer, xs_blk)
        if on(L + "ffn2"):
            ffn_phase(fw, C, xs_blk, I["ffn2_norm"][layer], I["ffn2_w_gate"][layer], I["ffn2_w_up"][layer], I["ffn2_w_down"][layer])
    fw.barrier()
    toks = []
    for i in range(NB):
        toks.append(fw.dma("sp" if i % 2 else "act", V(Buf("out%d" % i), out.ap[i * 128:(i + 1) * 128]), xs_blk[i]))
    for t in toks:
        fw._wait("sp", t)
    fw.barrier()
    return nc


def host_consts():
    bf = ml_dtypes.bfloat16
    s = np.arange(128)[:, None]
    q = np.arange(128)[None, :]
    inv64 = (1.0 / (10000.0 ** (np.arange(0, 64, 2, dtype=np.float32) / 64))).astype(np.float32)
    inv32 = (1.0 / (10000.0 ** (np.arange(0, 32, 2, dtype=np.float32) / 32))).astype(np.float32)
    return {
        "c_ident": np.eye(128, dtype=np.float32).astype(bf),
        "c_masku": (q >= s).astype(np.float32).astype(bf),
        "c_maskl": (s > q).astype(np.float32).astype(bf),
        "c_inv64": np.ascontiguousarray(np.broadcast_to(inv64[None, :], (128, 32))),
        "c_inv32": np.ascontiguousarray(np.broadcast_to(inv32[None, :], (128, 16))),
    }


def make_in_maps(inputs, cores):
    cst = host_consts()
    maps = []
    for b in cores:
        m = {"x": np.ascontiguousarray(inputs["x"][b], dtype=np.float32),
             "mem": np.ascontiguousarray(inputs["mem"][b], dtype=np.float32),
             "pos": np.ascontiguousarray(np.asarray(inputs["positions"][b]).astype(np.int32).reshape(NB, 128).T)}
        for nm in WNAMES:
            m[nm] = np.ascontiguousarray(inputs[nm], dtype=np.float32)
        m.update(cst)
        maps.append(m)
    return maps


def kernel(**inputs):
    inputs = {k: np.asarray(v) for k, v in inputs.items()}
    shapes = {nm: inputs[nm].shape for nm in WNAMES}
    nc = build(shapes)
    maps = make_in_maps(inputs, list(range(8)))
    res = run_bass_kernel_spmd(nc, maps, core_ids=list(range(8)))
    return np.stack([np.asarray(r["out"], dtype=np.float32) for r in res.results], axis=0)
```

```python
import math
from contextlib import ExitStack
import numpy as np
import ml_dtypes
import concourse.bass as bass
import concourse.mybir as mybir
from concourse.bass_utils import run_bass_kernel_spmd

F32 = mybir.dt.float32
BF16 = mybir.dt.bfloat16
I32 = mybir.dt.int32
ALU = mybir.AluOpType
AF = mybir.ActivationFunctionType
AX = mybir.AxisListType

S = 2048
D = 2048
NB = S // 128
FF = 5632
EPS = 1e-6
PI = math.pi


class Buf:
    __slots__ = ("name", "w", "r", "excl")

    def __init__(self, name, excl=False):
        self.name = name
        self.w = None
        self.r = []
        self.excl = excl


class V:
    __slots__ = ("buf", "ap")

    def __init__(self, buf, ap):
        self.buf = buf
        self.ap = ap

    def __getitem__(self, k):
        return V(self.buf, self.ap[k])

    def re(self, s, **kw):
        return V(self.buf, self.ap.rearrange(s, **kw))

    def bc(self, shape):
        return V(self.buf, self.ap.to_broadcast(list(shape)))

    def un(self, axis):
        return V(self.buf, self.ap.unsqueeze(axis))

    def pb(self, n):
        return V(self.buf, self.ap.partition_broadcast(n))


class FW:
    NDMA = 6

    def __init__(self, nc):
        self.nc = nc
        self.E = {"pe": nc.tensor, "act": nc.scalar, "dve": nc.vector, "pool": nc.gpsimd, "sp": nc.sync}
        self.sems = {}
        self.cnt = {}
        for e in self.E:
            self.sems[e] = nc.alloc_semaphore("s_" + e)
            self.cnt[e] = 0
        self.dq = {}
        for q in ("sp", "act", "pool"):
            lst = []
            for i in range(self.NDMA):
                k = "d_%s%d" % (q, i)
                self.sems[k] = nc.alloc_semaphore(k)
                self.cnt[k] = 0
                lst.append(k)
            self.dq[q] = [lst, 0]
        self.waited = {e: {} for e in self.E}
        self.stack = None
        self.cache = {}
        self.uid = 0

    def phase(self):
        fw = self

        class P:
            def __enter__(s):
                fw.stack = ExitStack()
                fw.stack.__enter__()
                fw.cache = {}
                return fw

            def __exit__(s, *a):
                fw.barrier()
                fw.stack.close()
                fw.stack = None
                return False
        return P()

    def _nm(self, name):
        self.uid += 1
        return "%s_%d" % (name, self.uid)

    def sb(self, name, shape, dt, persistent=False):
        nm = self._nm(name)
        if persistent or self.stack is None:
            t = self.nc.alloc_sbuf_tensor(nm, list(shape), dt)
        else:
            t = self.stack.enter_context(self.nc.sbuf_tensor(nm, list(shape), dt))
        return V(Buf(nm), t.ap())

    def tmp(self, name, shape, dt):
        key = (name, tuple(shape), str(dt))
        if key not in self.cache:
            self.cache[key] = self.sb(name, shape, dt)
        return self.cache[key]

    def ps(self, name, shape, dt):
        nm = self._nm(name)
        t = self.stack.enter_context(self.nc.psum_tensor(nm, list(shape), dt))
        return V(Buf(nm), t.ap())

    def dram(self, name, shape, dt, kind="Internal"):
        t = self.nc.dram_tensor(name, list(shape), dt, kind=kind)
        return V(Buf(name), t.ap())

    def _wait(self, eng, tok):
        if tok is None:
            return
        k, v = tok
        w = self.waited[eng]
        if w.get(k, 0) >= v:
            return
        if k == eng and eng == "pe":
            return
        self.E[eng].wait_ge(self.sems[k], v)
        w[k] = v

    def _deps(self, eng, reads, writes):
        for x in reads:
            self._wait(eng, x.buf.w)
            if x.buf.excl:
                for t in x.buf.r:
                    self._wait(eng, t)
        for x in writes:
            b = x.buf
            self._wait(eng, b.w)
            for t in b.r:
                self._wait(eng, t)

    def _commit(self, tok, reads, writes):
        for x in reads:
            if x.buf.excl:
                x.buf.w = tok
                x.buf.r = []
                continue
            r = x.buf.r
            r.append(tok)
            if len(r) > 16:
                m = {}
                for k, v in r:
                    if m.get(k, 0) < v:
                        m[k] = v
                x.buf.r = list(m.items())
        for x in writes:
            x.buf.w = tok
            x.buf.r = []

    def op(self, eng, fn, reads, writes):
        self._deps(eng, reads, writes)
        ins = fn()
        self.cnt[eng] += 1
        ins.then_inc(self.sems[eng], 1)
        tok = (eng, self.cnt[eng])
        self._commit(tok, reads, writes)
        return tok

    def dma(self, q, out, in_, **kw):
        lst, i = self.dq[q]
        k = lst[i % len(lst)]
        self.dq[q][1] = i + 1
        self._wait(q, (k, self.cnt[k]))
        self._deps(q, [in_], [out])
        ins = self.E[q].dma_start(out=out.ap, in_=in_.ap, **kw)
        self.cnt[k] += 16
        ins.then_inc(self.sems[k], 16)
        tok = (k, self.cnt[k])
        self._commit(tok, [in_], [out])
        return tok

    def barrier(self):
        for e in self.E:
            for k in self.sems:
                if self.cnt[k] > 0:
                    self._wait(e, (k, self.cnt[k]))

    def mm(self, out, lhsT, rhs, start=True, stop=True):
        return self.op("pe", lambda: self.nc.tensor.matmul(out.ap, lhsT.ap, rhs.ap, start=start, stop=stop),
                       [lhsT, rhs], [out])

    def tr(self, out, in_, ident):
        return self.op("pe", lambda: self.nc.tensor.transpose(out.ap, in_.ap, ident.ap), [in_, ident], [out])

    def act(self, out, in_, func, bias=None, scale=1.0, accum=None):
        reads = [in_]
        kw = {}
        if isinstance(bias, V):
            reads.append(bias)
            kw["bias"] = bias.ap
        elif bias is not None:
            kw["bias"] = bias
        if isinstance(scale, V):
            reads.append(scale)
            kw["scale"] = scale.ap
        else:
            kw["scale"] = scale
        writes = [out]
        if accum is not None:
            kw["accum_out"] = accum.ap
            writes.append(accum)
        return self.op("act", lambda: self.nc.scalar.activation(out.ap, in_.ap, func, **kw), reads, writes)

    def tt(self, eng, out, a, b, op):
        e = self.E[eng]
        return self.op(eng, lambda: e.tensor_tensor(out.ap, a.ap, b.ap, op), [a, b], [out])

    def ts(self, eng, out, a, s1, op0, s2=None, op1=None):
        e = self.E[eng]
        reads = [a]
        if isinstance(s1, V):
            reads.append(s1)
            s1 = s1.ap
        if isinstance(s2, V):
            reads.append(s2)
            s2 = s2.ap
        if op1 is None:
            s2 = 0.0
            op1 = ALU.add
        return self.op(eng, lambda: e.tensor_scalar(out.ap, a.ap, s1, s2, op0, op1), reads, [out])

    def stt(self, eng, out, a, s, b, op0, op1):
        e = self.E[eng]
        reads = [a, b]
        if isinstance(s, V):
            reads.append(s)
            s = s.ap
        return self.op(eng, lambda: e.scalar_tensor_tensor(out.ap, a.ap, s, b.ap, op0, op1), reads, [out])

    def copy(self, eng, out, in_):
        if eng == "act":
            return self.op("act", lambda: self.nc.scalar.copy(out.ap, in_.ap), [in_], [out])
        e = self.E[eng]
        return self.op(eng, lambda: e.tensor_copy(out.ap, in_.ap), [in_], [out])

    def red(self, eng, out, in_, op=ALU.add, axis=AX.X):
        e = self.E[eng]
        return self.op(eng, lambda: e.tensor_reduce(out.ap, in_.ap, axis, op), [in_], [out])

    def memset(self, eng, out, val):
        e = self.E[eng]
        return self.op(eng, lambda: e.memset(out.ap, val), [], [out])

    def recip(self, out, in_):
        return self.op("dve", lambda: self.nc.vector.reciprocal(out.ap, in_.ap), [in_], [out])


class Rot:
    def __init__(self, items):
        self.items = items
        self.i = 0

    def next(self):
        x = self.items[self.i % len(self.items)]
        self.i += 1
        return x


def wview(W, c0, c1, r0=0, r1=None):
    ap = W.ap
    if r1 is None:
        r1 = ap.shape[0]
    return V(W.buf, ap[r0:r1, c0:c1].rearrange("(k p) c -> p k c", p=128))


def rstd_from_ss(fw, ss, n, eps, tmp, out):
    fw.act(tmp, ss, AF.Ln, bias=fw.eps_tiles[eps], scale=1.0 / n)
    fw.act(out, tmp, AF.Exp, scale=-0.5)


def norm_T(fw, C, src_blks, K, gain_bc, dstT, psT, nblk=None, dst_off=0, cast_only=False, src_bf16=False):
    nblk = len(src_blks) if nblk is None else nblk
    KC = K // 128
    xt = Rot([fw.sb("nt_x", [128, K], BF16 if src_bf16 else F32) for _ in range(2)])
    hb = Rot([fw.sb("nt_h", [128, K], BF16) for _ in range(2)])
    junk = fw.sb("nt_j", [128, K], BF16)
    st = Rot([fw.sb("nt_s", [128, 4], F32) for _ in range(2)])
    ev = Rot(["dve", "act"])
    for i in range(nblk):
        x = xt.next()
        fw.dma("sp", x, src_blks[i])
        if cast_only:
            h = x if src_bf16 else hb.next()
            if not src_bf16:
                fw.copy("dve", h, x)
        else:
            s = st.next()
            h = hb.next()
            fw.act(junk, x, AF.Square, accum=s[:, 0:1])
            rstd_from_ss(fw, s[:, 0:1], K, EPS, s[:, 1:2], s[:, 2:3])
            fw.stt("dve", h, x, s[:, 2:3], gain_bc, ALU.mult, ALU.mult)
        for k0 in range(0, KC, 8):
            n = min(8, KC - k0)
            p = psT.next()
            for j in range(n):
                fw.tr(p[:, j, :], h[:, (k0 + j) * 128:(k0 + j + 1) * 128], C.ident)
            fw.copy(ev.next(), dstT[:, k0:k0 + n, dst_off + i * 128: dst_off + (i + 1) * 128], p[:, 0:n, :])


def load_bc(fw, name, vec, n, q="sp", mul=None):
    t = fw.sb(name, [128, n], F32)
    fw.dma(q, t, vec.pb(128))
    if mul is not None:
        fw.ts("dve", t, t, float(mul), ALU.mult)
    return t


def head_norm(fw, src, gain_bc, out, H, d, eng="dve", eps=EPS):
    sq = fw.tmp("hn_sq", [128, H, d], F32)
    st = fw.tmp("hn_st", [128, 3, H], F32)
    fw.act(sq, src, AF.Square)
    fw.red("dve", st[:, 0, :], sq)
    rstd_from_ss(fw, st[:, 0, :], d, eps, st[:, 1, :], st[:, 2, :])
    fw.tt(eng, sq, src, st[:, 2, :].un(2).bc([128, H, d]), ALU.mult)
    fw.tt(eng, out, sq, gain_bc.un(1).bc([128, H, d]), ALU.mult)


def rope(fw, x, out, cos, sin, H, d, eng="pool"):
    h = d // 2
    t1 = fw.tmp("rp_1" + eng, [128, H, h], F32)
    t2 = fw.tmp("rp_2" + eng, [128, H, h], F32)
    cb = cos.un(1).bc([128, H, h])
    sbb = sin.un(1).bc([128, H, h])
    x1 = x[:, :, 0:h]
    x2 = x[:, :, h:d]
    fw.tt(eng, t1, x1, cb, ALU.mult)
    fw.tt(eng, t2, x2, sbb, ALU.mult)
    fw.tt(eng, out[:, :, 0:h], t1, t2, ALU.subtract)
    fw.tt(eng, t1, x1, sbb, ALU.mult)
    fw.tt(eng, t2, x2, cb, ALU.mult)
    fw.tt(eng, out[:, :, h:d], t1, t2, ALU.add)


def ffn_phase(fw, C, xs_blk, norm_g, wg, wu, wd, T=1024, NF=4):
    FC = FF // 128
    FG = FC // NF
    with fw.phase():
        gain = load_bc(fw, "ffn_g", norm_g, D)
        hT = fw.sb("hT", [128, 16, T], BF16)
        actT = fw.sb("actT", [128, FG, T], BF16)
        wgb = Rot([fw.sb("wgb", [128, 16, 256], BF16) for _ in range(2)])
        wub = Rot([fw.sb("wub", [128, 16, 256], BF16) for _ in range(2)])
        wdb = Rot([fw.sb("wdb", [128, FG, 512], BF16) for _ in range(2)])
        sg = Rot([fw.sb("sg", [128, 512], F32) for _ in range(2)])
        xo = Rot([fw.sb("xo", [128, 512], F32) for _ in range(4)])
        xn = Rot([fw.sb("xn", [128, 512], F32) for _ in range(4)])
        psT = Rot([fw.ps("psT", [128, 8, 128], BF16) for _ in range(1)])
        psg = Rot([fw.ps("psg", [128, 512], F32) for _ in range(2)])
        psu = Rot([fw.ps("psu", [128, 512], F32) for _ in range(2)])
        pso = Rot([fw.ps("pso", [128, 512], F32) for _ in range(3)])
        for tt in range(S // T):
            nsub = T // 128
            norm_T(fw, C, xs_blk[tt * nsub:(tt + 1) * nsub], D, gain, hT, psT)
            for fg in range(NF):
                for fp in range(FG // 2 + FG % 2):
                    nj = min(2, FG - fp * 2)
                    c0 = (fg * FG + fp * 2) * 128
                    g_ = wgb.next()
                    u_ = wub.next()
                    fw.dma("pool", g_[:, :, 0:nj * 128], wview(wg, c0, c0 + nj * 128))
                    fw.dma("pool", u_[:, :, 0:nj * 128], wview(wu, c0, c0 + nj * 128))
                    for j in range(nj):
                        fl = fp * 2 + j
                        for th in range(T // 512):
                            pg = psg.next()
                            pu = psu.next()
                            tk = slice(th * 512, (th + 1) * 512)
                            for k in range(16):
                                fw.mm(pg, g_[:, k, j * 128:(j + 1) * 128], hT[:, k, tk], start=(k == 0), stop=(k == 15))
                            for k in range(16):
                                fw.mm(pu, u_[:, k, j * 128:(j + 1) * 128], hT[:, k, tk], start=(k == 0), stop=(k == 15))
                            s_ = sg.next()
                            fw.act(s_, pg, AF.Silu)
                            fw.tt("dve", actT[:, fl, tk], s_, pu, ALU.mult)
                for db in range(4):
                    w_ = wdb.next()
                    fw.dma("pool", w_, wview(wd, db * 512, (db + 1) * 512, fg * FG * 128, (fg + 1) * FG * 128))
                    for sub in range(nsub):
                        blk = xs_blk[tt * nsub + sub]
                        po = pso.next()
                        for f in range(FG):
                            fw.mm(po, actT[:, f, sub * 128:(sub + 1) * 128], w_[:, f, :], start=(f == 0), stop=(f == FG - 1))
                        o_ = xo.next()
                        n_ = xn.next()
                        fw.dma("sp", o_, blk[:, db * 512:(db + 1) * 512])
                        fw.stt("dve", n_, po, 0.5, o_, ALU.mult, ALU.add)
                        fw.dma("act", blk[:, db * 512:(db + 1) * 512], n_)


def linear_phase(fw, C, src_blks, K, norm_g, W, ncols, evac, cast_only=False, src_bf16=False, extra=None):
    KC = K // 128
    with fw.phase():
        gain = None if cast_only else load_bc(fw, "lin_g", norm_g, K)
        hT = fw.sb("lhT", [128, KC, S], BF16)
        psT = Rot([fw.ps("lpsT", [128, 8, 128], BF16) for _ in range(2)])
        pso = Rot([fw.ps("lpso", [128, 512], F32) for _ in range(4)])
        wb = Rot([fw.sb("lwb", [128, KC, 512], BF16) for _ in range(2)])
        ctx = extra(fw) if extra is not None else None
        norm_T(fw, C, src_blks, K, gain, hT, psT, cast_only=cast_only, src_bf16=src_bf16)
        for c0 in range(0, ncols, 512):
            c1 = min(ncols, c0 + 512)
            w_ = wb.next()
            fw.dma("pool", w_[:, :, 0:c1 - c0], wview(W, c0, c1))
            for i in range(NB):
                po = pso.next()
                for k in range(KC):
                    fw.mm(po[:, 0:c1 - c0], hT[:, k, i * 128:(i + 1) * 128], w_[:, k, 0:c1 - c0], start=(k == 0), stop=(k == KC - 1))
                evac(fw, i, c0, c1, po[:, 0:c1 - c0], ctx)


def evac_store(dst_blks):
    def mk(fw):
        return {"t": Rot([fw.sb("ev_t", [128, 512], F32) for _ in range(3)]), "e": Rot(["act", "dve"])}

    def ev(fw, i, c0, c1, po, ctx):
        t = ctx["t"].next()
        fw.copy(ctx["e"].next(), t[:, 0:c1 - c0], po)
        fw.dma("sp", dst_blks[i][:, c0:c1], t[:, 0:c1 - c0])
    return mk, ev


def evac_resid(xs_blk):
    def mk(fw):
        return {"o": Rot([fw.sb("er_o", [128, 512], F32) for _ in range(3)]),
                "n": Rot([fw.sb("er_n", [128, 512], F32) for _ in range(3)])}

    def ev(fw, i, c0, c1, po, ctx):
        o_ = ctx["o"].next()
        n_ = ctx["n"].next()
        fw.dma("sp", o_, xs_blk[i][:, c0:c1])
        fw.tt("dve", n_, po, o_, ALU.add)
        fw.dma("act", xs_blk[i][:, c0:c1], n_)
    return mk, ev


def setup_phase(fw, C, I):
    C.ident = fw.sb("ident", [128, 128], BF16, persistent=True)
    C.maskU = fw.sb("maskU", [128, 128], BF16, persistent=True)
    C.maskL = fw.sb("maskL", [128, 128], BF16, persistent=True)
    C.cos64 = fw.sb("cos64", [128, NB, 32], F32, persistent=True)
    C.sin64 = fw.sb("sin64", [128, NB, 32], F32, persistent=True)
    C.cos32 = fw.sb("cos32", [128, NB, 16], F32, persistent=True)
    C.sin32 = fw.sb("sin32", [128, NB, 16], F32, persistent=True)
    C.memKT = fw.sb("memKT", [128, 4, 256], BF16, persistent=True)
    C.memV1 = fw.sb("memV1", [128, 2, 4, 130], BF16, persistent=True)
    C.cst = fw.sb("cst", [128, 8], F32, persistent=True)
    fw.eps_tiles = {}
    for j, e in enumerate([EPS, 64e-5, 0.0]):
        fw.memset("dve", C.cst[:, j:j + 1], e)
        fw.eps_tiles[e] = C.cst[:, j:j + 1]
    fw.dma("sp", C.ident, I["c_ident"])
    fw.dma("sp", C.maskU, I["c_masku"])
    fw.dma("sp", C.maskL, I["c_maskl"])
    C.negU4 = fw.sb("negU4", [128, 4, 128], BF16, persistent=True)
    C.negL4 = fw.sb("negL4", [128, 4, 128], BF16, persistent=True)
    fw.ts("dve", C.negU4, C.maskU.un(1).bc([128, 4, 128]), -1.0, ALU.add, 200.0, ALU.mult)
    fw.ts("dve", C.negL4, C.maskL.un(1).bc([128, 4, 128]), -1.0, ALU.add, 200.0, ALU.mult)
    with fw.phase():
        posi = fw.sb("posi", [128, NB], I32)
        posf = fw.sb("posf", [128, NB], F32)
        fw.dma("sp", posi, I["pos"])
        fw.copy("dve", posf, posi)
        for (d2, invn, ct, st_) in ((32, "c_inv64", C.cos64, C.sin64), (16, "c_inv32", C.cos32, C.sin32)):
            inv = fw.sb("inv", [128, d2], F32)
            fw.dma("sp", inv, I[invn])
            ang = fw.sb("ang", [128, NB, d2], F32)
            a2 = fw.sb("ang2", [128, NB, d2], F32)
            ni = fw.sb("angi", [128, NB, d2], I32)
            nf = fw.sb("angf", [128, NB, d2], F32)
            fw.tt("dve", ang, posf.un(2).bc([128, NB, d2]), inv.un(1).bc([128, NB, d2]), ALU.mult)
            for (off, dst) in ((0.0, st_), (PI / 2, ct)):
                fw.ts("dve", a2, ang, off, ALU.add)
                fw.ts("dve", nf, a2, 1.0 / (2 * PI), ALU.mult)
                fw.copy("dve", ni, nf)
                fw.copy("dve", nf, ni)
                fw.stt("dve", a2, nf, -2 * PI, a2, ALU.mult, ALU.add)
                fw.ts("dve", a2, a2, 3.1415925, ALU.min, -3.1415925, ALU.max)
                fw.act(dst, a2, AF.Sin)
    with fw.phase():
        gain = load_bc(fw, "mem_g", I["mem_norm"], D)
        gk = load_bc(fw, "mem_gk", I["mem_k_norm"], 128)
        memT = fw.sb("memT", [128, 16, 256], BF16)
        wkv = fw.sb("wkv", [128, 16, 1024], BF16)
        psT = Rot([fw.ps("mpsT", [128, 8, 128], BF16) for _ in range(2)])
        pso = Rot([fw.ps("mpso", [128, 512], F32) for _ in range(2)])
        kf = fw.sb("mkf", [128, 4, 128], BF16)
        mem_blks = [I["mem"][i * 128:(i + 1) * 128, :] for i in range(2)]
        fw.dma("pool", wkv, wview(I["mem_w_kv"], 0, 1024))
        fw.memset("dve", C.memV1, 1.0)
        norm_T(fw, C, mem_blks, D, gain, memT, psT)
        for mt in range(2):
            pk = pso.next()
            for k in range(16):
                fw.mm(pk, memT[:, k, mt * 128:(mt + 1) * 128], wkv[:, k, 0:512], start=(k == 0), stop=(k == 15))
            head_norm(fw, pk.re("p (h d) -> p h d", h=4), gk, kf, 4, 128)
            p = psT.next()
            for h in range(4):
                fw.tr(p[:, h, :], kf[:, h, :], C.ident)
            fw.copy("dve", C.memKT[:, :, mt * 128:(mt + 1) * 128], p[:, 0:4, :])
            pv = pso.next()
            for k in range(16):
                fw.mm(pv, memT[:, k, mt * 128:(mt + 1) * 128], wkv[:, k, 512:1024], start=(k == 0), stop=(k == 15))
            fw.copy("dve", C.memV1[:, mt, :, 0:128], pv.re("p (h d) -> p h d", h=4))


def memx_phase(fw, C, I, layer, xs_blk):
    with fw.phase():
        gain = load_bc(fw, "mx_g", I["memx_norm"][layer], D)
        gq = load_bc(fw, "mx_gq", I["memx_q_norm"][layer], 128, mul=128 ** -0.5)
        hT = fw.sb("mx_hT", [128, 16, S], BF16)
        wq = fw.sb("mx_wq", [128, 16, 512], BF16)
        wo = fw.sb("mx_wo", [128, 4, 2048], BF16)
        fw.dma("pool", wq, wview(I["memx_w_q"][layer], 0, 512))
        fw.dma("pool", wo, wview(I["memx_w_o"][layer], 0, 2048))
        psT = Rot([fw.ps("mx_psT", [128, 8, 128], BF16) for _ in range(1)])
        psq = Rot([fw.ps("mx_psq", [128, 512], F32) for _ in range(1)])
        pssc = Rot([fw.ps("mx_pssc", [128, 512], F32) for _ in range(2)])
        pss = Rot([fw.ps("mx_pss", [128, 2, 256], F32) for _ in range(2)])
        pso = Rot([fw.ps("mx_pso", [128, 512], F32) for _ in range(2)])
        norm_T(fw, C, xs_blk, D, gain, hT, psT)
        qf_r = Rot([fw.sb("mx_qf", [128, 4, 128], BF16) for _ in range(2)])
        qT_r = Rot([fw.sb("mx_qT", [128, 4, 128], BF16) for _ in range(2)])
        E = Rot([fw.sb("mx_E", [128, 2, 128], BF16) for _ in range(3)])
        rec_r = Rot([fw.sb("mx_rec", [128, 4], F32) for _ in range(2)])
        ob_r = Rot([fw.sb("mx_ob", [128, 4, 128], BF16) for _ in range(2)])
        oT_r = Rot([fw.sb("mx_oT", [128, 4, 128], BF16) for _ in range(2)])
        xo = Rot([fw.sb("mx_xo", [128, 2048], F32) for _ in range(2)])

        def prep(i):
            qf, qT = qf_r.next(), qT_r.next()
            pq = psq.next()
            for k in range(16):
                fw.mm(pq, hT[:, k, i * 128:(i + 1) * 128], wq[:, k, :], start=(k == 0), stop=(k == 15))
            head_norm(fw, pq.re("p (h d) -> p h d", h=4), gq, qf, 4, 128)
            p = psT.next()
            for h in range(4):
                fw.tr(p[:, h, :], qf[:, h, :], C.ident)
            fw.copy("dve", qT, p[:, 0:4, :])
            return qT

        def attn(i, qT):
            rec, ob, oT = rec_r.next(), ob_r.next(), oT_r.next()
            o_ = xo.next()
            fw.dma("sp", o_, xs_blk[i])
            pa = pss.next()
            pb = pss.next()

            def scores(h):
                ps_ = pssc.next()
                for mt in range(2):
                    fw.mm(ps_[:, mt * 128:(mt + 1) * 128], C.memKT[:, h, mt * 128:(mt + 1) * 128], qT[:, h, :])
                return ps_
            pend = scores(0)
            for h in range(4):
                ps_ = pend
                if h + 1 < 4:
                    pend = scores(h + 1)
                e_ = E.next()
                fw.act(e_.re("p a b -> p (a b)"), ps_[:, 0:256], AF.Exp)
                pacc = (pa if h < 2 else pb)[:, h % 2, 0:129]
                for mt in range(2):
                    fw.mm(pacc, e_[:, mt, :], C.memV1[:, mt, h, 0:129], start=(mt == 0), stop=(mt == 1))
            for (pp_, h0) in ((pa, 0), (pb, 2)):
                fw.recip(rec[:, h0:h0 + 2], pp_[:, :, 128])
                fw.tt("dve", ob[:, h0:h0 + 2, :], pp_[:, :, 0:128], rec[:, h0:h0 + 2].un(2).bc([128, 2, 128]), ALU.mult)
            p = psT.next()
            for h in range(4):
                fw.tr(p[:, h, :], ob[:, h, :], C.ident)
            fw.copy("act", oT, p[:, 0:4, :])
            for db in range(4):
                po = pso.next()
                for k in range(4):
                    fw.mm(po, oT[:, k, :], wo[:, k, db * 512:(db + 1) * 512], start=(k == 0), stop=(k == 3))
                fw.tt("dve", o_[:, db * 512:(db + 1) * 512], po, o_[:, db * 512:(db + 1) * 512], ALU.add)
            fw.dma("act", xs_blk[i], o_)

        cur = prep(0)
        for i in range(NB):
            nxt = prep(i + 1) if i + 1 < NB else None
            attn(i, cur)
            cur = nxt


def swa_phase(fw, C, I, u_blk, yA_blk):
    with fw.phase():
        gq = load_bc(fw, "sw_gq", I["swa_q_norm"][0], 64, mul=0.125)
        gk = load_bc(fw, "sw_gk", I["swa_k_norm"][0], 64)
        esink = load_bc(fw, "sw_sink", I["swa_sinks"][0], 16)
        fw.act(esink, esink, AF.Exp)
        ut = Rot([fw.sb("sw_u", [128, 1536], F32) for _ in range(2)])
        qn_r = Rot([fw.sb("sw_qn", [128, 16, 64], F32) for _ in range(2)])
        kn_r = Rot([fw.sb("sw_kn", [128, 4, 64], F32) for _ in range(2)])
        qf_r = Rot([fw.sb("sw_qf", [128, 16, 64], BF16) for _ in range(2)])
        kf_r = Rot([fw.sb("sw_kf", [128, 4, 64], BF16) for _ in range(2)])
        qT_r = Rot([fw.sb("sw_qT", [64, 16, 128], BF16) for _ in range(2)])
        kT = Rot([fw.sb("sw_kT", [64, 4, 128], BF16) for _ in range(3)])
        V1 = Rot([fw.sb("sw_V1", [128, 4, 66], BF16) for _ in range(3)])
        E = Rot([fw.sb("sw_E", [128, 4, 128], BF16) for _ in range(4)])
        den = fw.sb("sw_den", [128, 4], F32)
        yt = Rot([fw.sb("sw_y", [128, 16, 64], BF16) for _ in range(2)])
        psT = Rot([fw.ps("sw_psT", [128, 8, 128], BF16) for _ in range(2)])
        pss = Rot([fw.ps("sw_pss", [128, 512], F32) for _ in range(3)])
        pso = Rot([fw.ps("sw_pso", [128, 4, 128], F32) for _ in range(2)])
        for v_ in V1.items:
            fw.memset("dve", v_, 1.0)

        def prep(n):
            u = ut.next()
            qn, kn, qf, kf, qT = qn_r.next(), kn_r.next(), qf_r.next(), kf_r.next(), qT_r.next()
            fw.dma("sp", u, u_blk[n][:, 0:1536])
            head_norm(fw, u[:, 0:1024].re("p (h d) -> p h d", h=16), gq, qn, 16, 64)
            rope(fw, qn, qf, C.cos64[:, n, :], C.sin64[:, n, :], 16, 64, eng="dve")
            head_norm(fw, u[:, 1024:1280].re("p (h d) -> p h d", h=4), gk, kn, 4, 64)
            rope(fw, kn, kf, C.cos64[:, n, :], C.sin64[:, n, :], 4, 64)
            V1c = V1.next()
            fw.copy("act", V1c[:, :, 0:64], u[:, 1280:1536].re("p (h d) -> p h d", h=4))
            for h0 in (0, 8):
                p = psT.next()
                for h in range(8):
                    fw.tr(p[0:64, h, :], qf[:, h0 + h, :], C.ident)
                fw.copy("dve", qT[:, h0:h0 + 8, :], p[0:64, :, :])
            kTc = kT.next()
            p = psT.next()
            for h in range(4):
                fw.tr(p[0:64, h, :], kf[:, h, :], C.ident)
            fw.copy("dve", kTc, p[0:64, 0:4, :])
            return (qT, kTc, V1c)

        def attn(n, cur, prev):
            qT, kTc, V1c = cur
            y = yt.next()
            items = []
            for g in range(4):
                if prev is not None:
                    items.append((g, prev[1], prev[2], C.negL4, False))
                items.append((g, kTc, V1c, C.negU4, True))

            def scores(it):
                g, kt_, v1_, neg, last = it
                ps_ = pss.next()
                fw.mm(ps_, kt_[:, g, :], qT[:, 4 * g:4 * g + 4, :].re("p a b -> p (a b)"), start=True, stop=False)
                fw.mm(ps_, C.ident, neg.re("p a b -> p (a b)"), start=False, stop=True)
                return ps_
            pend = scores(items[0])
            es = []
            for i, it in enumerate(items):
                ps_ = pend
                if i + 1 < len(items):
                    pend = scores(items[i + 1])
                g, kt_, v1_, neg, last = it
                e_ = E.next()
                fw.act(e_.re("p a b -> p (a b)"), ps_, AF.Exp)
                es.append((e_, v1_))
                if last:
                    po = pso.next()
                    for hq in range(4):
                        for j, (ee, vv) in enumerate(es):
                            fw.mm(po[:, hq, 0:65], ee[:, hq, :], vv[:, g, 0:65], start=(j == 0), stop=(j == len(es) - 1))
                    fw.tt("dve", den, po[:, :, 64], esink[:, 4 * g:4 * g + 4], ALU.add)
                    fw.recip(den, den)
                    fw.tt("dve", y[:, 4 * g:4 * g + 4, :], po[:, :, 0:64], den.un(2).bc([128, 4, 64]), ALU.mult)
                    es = []
            fw.dma("act", yA_blk[n][:, 0:1024], y.re("p h d -> p (h d)"))

        cur = prep(0)
        prev = None
        for n in range(NB):
            nxt = prep(n + 1) if n + 1 < NB else None
            attn(n, cur, prev)
            prev, cur = cur, nxt


def rwkv_prep_phase(fw, C, I, u, u_blk, rowsTM_blk, gS_blk, bonS_blk):
    RW0 = 1536
    RWW = 3360
    with fw.phase():
        mu = load_bc(fw, "rk_mu", I["rwkv_mu"][0], RWW)
        w0 = load_bc(fw, "rk_w0", I["rwkv_w0"][0], 1024)
        a0 = load_bc(fw, "rk_a0", I["rwkv_a0"][0], 1024)
        kkw = load_bc(fw, "rk_kk", I["rwkv_k_k"][0], 1024)
        kaw = load_bc(fw, "rk_ka", I["rwkv_k_a"][0], 1024)
        rkw = load_bc(fw, "rk_rk", I["rwkv_r_k"][0], 1024)
        w2 = fw.sb("rk_w2", [64, 1024], BF16)
        a2 = fw.sb("rk_a2", [64, 1024], BF16)
        g2a = fw.sb("rk_g2a", [128, 1024], BF16)
        g2b = fw.sb("rk_g2b", [32, 1024], BF16)
        fw.dma("pool", w2, I["rwkv_w2"][0])
        fw.dma("pool", a2, I["rwkv_a2"][0])
        fw.dma("pool", g2a, I["rwkv_g2"][0][0:128, :])
        fw.dma("pool", g2b, I["rwkv_g2"][0][128:160, :])
        pt = Rot([fw.sb("rk_p", [128, RWW], F32) for _ in range(2)])
        pv = Rot([fw.sb("rk_pv", [128, RWW], F32) for _ in range(1)])
        xs = fw.sb("rk_xs", [128, RWW], F32)
        lo = fw.sb("rk_lo", [128, 288], BF16)
        loT = fw.sb("rk_loT", [128, 4, 128], BF16)
        t1 = fw.sb("rk_t1", [128, 1024], F32)
        t2 = fw.sb("rk_t2", [128, 1024], F32)
        aa = fw.sb("rk_a", [128, 1024], F32)
        kk = fw.sb("rk_kkn", [128, 1024], F32)
        st = fw.sb("rk_st", [128, 4, 16], F32)
        rows = Rot([fw.sb("rk_rows", [128, 6, 1024], F32) for _ in range(1)])
        gg = Rot([fw.sb("rk_g", [128, 1024], F32) for _ in range(1)])
        bon = Rot([fw.sb("rk_bon", [128, 1024], F32) for _ in range(1)])
        psT = Rot([fw.ps("rk_psT", [128, 8, 128], BF16) for _ in range(1)])
        psm = Rot([fw.ps("rk_psm", [128, 512], F32) for _ in range(4)])
        H3 = lambda v: v.re("p (h d) -> p h d", h=16)
        for n in range(NB):
            p = pt.next()
            pr = pv.next()
            fw.dma("sp", p, u_blk[n][:, RW0:RW0 + RWW])
            if n == 0:
                fw.memset("dve", pr, 0.0)
                fw.dma("act", pr[1:128, :], u[0:127, RW0:RW0 + RWW])
            else:
                fw.dma("act", pr, u[n * 128 - 1:n * 128 + 127, RW0:RW0 + RWW])
            fw.tt("dve", pr, pr, p, ALU.subtract)
            fw.tt("pool", pr, pr, mu, ALU.mult)
            fw.tt("dve", xs, pr, p, ALU.add)
            r_ = xs[:, 0:1024]
            k_ = xs[:, 1024:2048]
            v_ = xs[:, 2048:3072]
            fw.act(lo[:, 0:64], xs[:, 3072:3136], AF.Tanh)
            fw.copy("dve", lo[:, 64:128], xs[:, 3136:3200])
            fw.act(lo[:, 128:288], xs[:, 3200:3360], AF.Sigmoid)
            ptt = psT.next()
            fw.tr(ptt[0:64, 0, :], lo[:, 0:64], C.ident)
            fw.tr(ptt[0:64, 1, :], lo[:, 64:128], C.ident)
            fw.tr(ptt[:, 2, :], lo[:, 128:256], C.ident)
            fw.tr(ptt[0:32, 3, :], lo[:, 256:288], C.ident)
            fw.copy("dve", loT[0:64, 0:2, :], ptt[0:64, 0:2, :])
            fw.copy("dve", loT[:, 2, :], ptt[:, 2, :])
            fw.copy("dve", loT[0:32, 3, :], ptt[0:32, 3, :])
            R = rows.next()
            g_ = gg.next()
            for cb in range(2):
                cs = slice(cb * 512, (cb + 1) * 512)
                pw = psm.next()
                fw.mm(pw, loT[0:64, 0, :], w2[:, cs])
                pa = psm.next()
                fw.mm(pa, loT[0:64, 1, :], a2[:, cs])
                pg = psm.next()
                fw.mm(pg, loT[:, 2, :], g2a[:, cs], start=True, stop=False)
                fw.mm(pg, loT[0:32, 3, :], g2b[:, cs], start=False, stop=True)
                fw.copy("act", g_[:, cs], pg)
                fw.tt("dve", t1[:, cs], pw, w0[:, cs], ALU.add)
                fw.tt("dve", aa[:, cs], pa, a0[:, cs], ALU.add)
            fw.act(t1, t1, AF.Exp, scale=-1.0)
            fw.act(t1, t1, AF.Ln, bias=1.0)
            fw.act(t1, t1, AF.Exp, scale=-1.0, bias=C.cst[:, 3:4])
            fw.copy("act", R[:, 1, :], t1)
            fw.act(aa, aa, AF.Sigmoid)
            fw.tt("pool", kk, k_, kkw, ALU.mult)
            fw.act(t2, kk, AF.Square)
            fw.red("dve", st[:, 0, :], H3(t2))
            fw.act(st[:, 1, :], st[:, 0, :], AF.Sqrt)
            fw.ts("dve", st[:, 1, :], st[:, 1, :], 1e-12, ALU.max)
            fw.recip(st[:, 2, :], st[:, 1, :])
            fw.tt("dve", H3(kk), H3(kk), st[:, 2, :].un(2).bc([128, 16, 64]), ALU.mult)
            fw.op("act", lambda: fw.nc.scalar.mul(R[:, 0, :].ap, kk.ap, -1.0), [kk], [R[:, 0, :]])
            fw.tt("pool", R[:, 2, :], kk, aa, ALU.mult)
            fw.ts("dve", t2, aa, -1.0, ALU.add)
            fw.tt("dve", t2, t2, kaw, ALU.mult)
            fw.stt("dve", R[:, 3, :], t2, 1.0, k_, ALU.add, ALU.mult)
            fw.copy("act", R[:, 4, :], r_)
            fw.tt("dve", t2, r_, rkw, ALU.mult)
            fw.tt("dve", t2, t2, R[:, 3, :], ALU.mult)
            fw.red("dve", st[:, 3, :], H3(t2))
            b_ = bon.next()
            fw.tt("dve", H3(b_), H3(v_), st[:, 3, :].un(2).bc([128, 16, 64]), ALU.mult)
            fw.dma("sp", gS_blk[n], g_)
            fw.dma("sp", bonS_blk[n], b_)
            fw.copy("act", R[:, 5, :], v_)
            fw.dma("act", rowsTM_blk[n], R.re("p a c -> p (a c)"))


def rwkv_scan_phase(fw, C, rowS, vS, yS):
    TB = 32
    with fw.phase():
        St = [fw.sb("sc_S", [128, 8, 64], F32) for _ in range(2)]
        A = fw.sb("sc_A", [128, 8, 64], F32)
        B = fw.sb("sc_B", [128, 8, 64], F32)
        Cc = Rot([fw.sb("sc_C", [128, 8, 64], F32) for _ in range(2)])
        Ee_r = Rot([fw.sb("sc_E", [128, 8, 64], F32) for _ in range(2)])
        sa = fw.sb("sc_sa", [128, 8], F32)
        rb = Rot([fw.sb("sc_rows", [128, 5, TB, 64], F32) for _ in range(2)])
        vb = Rot([fw.sb("sc_v", [128, TB, 8], F32) for _ in range(2)])
        yb = Rot([fw.sb("sc_y", [128, TB, 8], F32) for _ in range(2)])
        fw.memset("dve", St[0], 0.0)
        cur = 0
        qs = Rot(["sp", "act"])
        for b in range(S // TB):
            t0 = b * TB
            R = rb.next()
            vv = vb.next()
            yy = yb.next()
            for ty in range(5):
                for h in range(16):
                    fw.dma(qs.next(), R[h * 8:(h + 1) * 8, ty, :, :], rowS[ty, h, t0:t0 + TB, :].pb(8))
            fw.dma(qs.next(), vv, vS[:, t0:t0 + TB, :])
            for t in range(TB):
                S0 = St[cur]
                S1 = St[1 - cur]
                rowb = lambda ty: R[:, ty, t, :].un(1).bc([128, 8, 64])
                c_ = Cc.next()
                fw.tt("dve", c_, rowb(3), vv[:, t, :].un(2).bc([128, 8, 64]), ALU.mult)
                fw.tt("dve", A, S0, rowb(0), ALU.mult)
                fw.red("dve", sa, A)
                fw.tt("dve", S1, S0, rowb(1), ALU.mult)
                fw.tt("dve", B, rowb(2), sa.un(2).bc([128, 8, 64]), ALU.mult)
                fw.tt("dve", S1, S1, B, ALU.add)
                fw.tt("dve", S1, S1, c_, ALU.add)
                Ee = Ee_r.next()
                fw.tt("dve", Ee, S1, rowb(4), ALU.mult)
                fw.red("dve", yy[:, t, :], Ee)
                cur = 1 - cur
            fw.dma(qs.next(), yS[:, t0:t0 + TB, :], yy)


def rwkv_chunk_phase(fw, C, rowsTM_blk, yTM_blk):
    HG = 8
    with fw.phase():
        triU = fw.sb("ck_tri", [128, 128], F32)
        mk2 = fw.sb("ck_mk2", [128, 2, 128], F32)
        mSL = fw.sb("ck_msl", [128, 128], F32)
        ones = fw.sb("ck_ones", [128, 1], F32)
        fw.copy("dve", triU, C.maskU)
        fw.copy("dve", mk2[:, 1, :], C.maskU)
        fw.tt("dve", mk2[:, 0, :], C.maskU, C.ident, ALU.subtract)
        fw.copy("dve", mSL, C.maskL)
        fw.memset("dve", ones, 1.0)
        H32 = [fw.sb("ck_H32", [64, 64], F32) for _ in range(16)]
        Hb = [fw.sb("ck_Hb", [64, 64], BF16) for _ in range(16)]
        for h in range(16):
            fw.memset("dve", H32[h], 0.0)
            fw.memset("dve", Hb[h], 0.0)
        inb = Rot([fw.sb("ck_in", [128, 6, 1024], F32) for _ in range(2)])
        Lt = fw.sb("ck_L", [128, 1024], F32)
        Pt = fw.sb("ck_Pt", [128, 1024], F32)
        Pinv = fw.sb("ck_Pinv", [128, 1024], F32)
        Pm1 = fw.sb("ck_Pm1", [128, 1024], F32)
        At = fw.sb("ck_At", [128, 1024], BF16)
        Bt = fw.sb("ck_Bt", [128, 1024], BF16)
        Kt = fw.sb("ck_Kt", [128, 1024], BF16)
        Rt = fw.sb("ck_Rt", [128, 1024], BF16)
        Vt = fw.sb("ck_Vt", [128, 1024], BF16)
        PC = fw.sb("ck_PC", [64, 16], F32)
        ytile = Rot([fw.sb("ck_y", [128, 1024], F32) for _ in range(2)])
        T4s = [fw.sb("ck_T4", [64, 4, 128], BF16) for _ in range(HG)]
        G1s = [fw.sb("ck_G1", [128, 2, 128], BF16) for _ in range(HG)]
        G2s = [fw.sb("ck_G2", [128, 2, 128], BF16) for _ in range(HG)]
        NMs = [[fw.sb("ck_NM", [128, 2, 128], BF16) for _ in range(HG)] for _ in range(2)]
        Ms = [[fw.sb("ck_M", [128, 128], BF16) for _ in range(HG)] for _ in range(1)]
        Zs = [[fw.sb("ck_Z", [128, 128], BF16) for _ in range(HG)] for _ in range(2)]
        Yvs = [fw.sb("ck_Yv", [128, 64], F32) for _ in range(HG)]
        Gps = [fw.sb("ck_Gp", [64, 64], F32) for _ in range(HG)]
        WTs = [fw.sb("ck_WT", [64, 128], BF16) for _ in range(HG)]
        Us = [fw.sb("ck_U", [128, 64], BF16) for _ in range(HG)]
        tHs = [fw.sb("ck_tH", [64, 64], F32) for _ in range(HG)]
        banks = [fw.ps("ck_pp", [128, 512], F32) for _ in range(7)]
        for b_ in banks:
            b_.buf.excl = True
        pp = Rot([V(b_.buf, b_.ap[:, hh * 256:(hh + 1) * 256]) for hh in range(2) for b_ in banks])
        bankb = fw.ps("ck_pb", [128, 1024], BF16)
        bankb.buf.excl = True
        pb = Rot([V(bankb.buf, bankb.ap[:, hh * 512:(hh + 1) * 512]) for hh in range(2)])
        ev = Rot(["act", "dve"])
        for n in range(NB):
            X = inb.next()
            fw.dma("sp", X.re("p a c -> p (a c)"), rowsTM_blk[n])
            nkk, ew, kka, km, r_, v_ = (X[:, j, :] for j in range(6))
            for c4 in range(4):
                cs = slice(c4 * 256, (c4 + 1) * 256)
                p = pp.next()
                fw.mm(p, triU, ew[:, cs])
                fw.act(Pinv[:, cs], p, AF.Exp)
                fw.act(Pt[:, cs], p, AF.Exp, scale=-1.0)
                fw.tt("dve", Lt[:, cs], ew[:, cs], p, ALU.subtract)
                fw.act(Pm1[:, cs], Lt[:, cs], AF.Exp)
            pc = pp.next()
            for h in range(16):
                fw.mm(pc[0:64, h:h + 1], ew[:, h * 64:(h + 1) * 64], ones)
            fw.act(PC, pc[0:64, 0:16], AF.Exp, scale=-1.0)
            fw.tt("dve", At, nkk, Pm1, ALU.mult)
            fw.tt("pool", Bt, kka, Pinv, ALU.mult)
            fw.tt("pool", Kt, km, Pinv, ALU.mult)
            fw.tt("dve", Rt, r_, Pt, ALU.mult)
            fw.copy("act", Vt, v_)
            y = ytile.next()
            for g0 in range(0, 16, HG):
                hs = list(range(g0, g0 + HG))
                for h in hs:
                    i = h % HG
                    sl = slice(h * 64, (h + 1) * 64)
                    pT = pb.next()
                    pv = pT[0:64, :].re("p (a t) -> p a t", a=4)
                    for j, src in enumerate((At, Rt, Bt, Kt)):
                        fw.tr(pv[:, j, :], src[:, sl], C.ident)
                    T4 = T4s[i]
                    fw.copy("act", T4, pv)
                    ar = T4[:, 0:2, :].re("p a t -> p (a t)")
                    p1 = pp.next()
                    fw.mm(p1, T4[:, 2, :], ar)
                    p2 = pp.next()
                    fw.mm(p2, T4[:, 3, :], ar)
                    p3 = pp.next()
                    fw.mm(p3[:, 0:128], T4[:, 0, :], T4[:, 2, :])
                    fw.tt("dve", G1s[i], p1.re("p (a t) -> p a t", a=2), mk2, ALU.mult)
                    fw.tt("dve", G2s[i], p2.re("p (a t) -> p a t", a=2), mk2, ALU.mult)
                    fw.tt("dve", Ms[0][i], p3[:, 0:128], mSL, ALU.mult)
                    p4 = pp.next()
                    fw.mm(p4[:, 0:64], G2s[i][:, 0, :], Vt[:, sl])
                    fw.mm(p4[:, 64:128], G2s[i][:, 1, :], Vt[:, sl])
                    fw.copy("act", Zs[0][i][:, 0:64], At[:, sl])
                    fw.copy("act", Zs[0][i][:, 64:128], p4[:, 0:64])
                    fw.copy("dve", Yvs[i], p4[:, 64:128])
                    p5 = pp.next()
                    fw.mm(p5[0:64, 0:64], Kt[:, sl], Vt[:, sl])
                    fw.ts("dve", Gps[i], p5[0:64, 0:64], PC[:, h:h + 1], ALU.mult)
                for k in range(7):
                    for h in hs:
                        i = h % HG
                        Nk = G1s[i][:, 0, :] if k == 0 else NMs[k % 2][i][:, 0, :]
                        Mk = Ms[0][i] if k == 0 else NMs[k % 2][i][:, 1, :]
                        Zk = Zs[k % 2][i]
                        pz = pp.next()
                        fw.mm(pz[:, 0:128], Nk, Zk)
                        fw.tt("dve", Zs[(k + 1) % 2][i], Zk, pz[:, 0:128], ALU.add)
                        if k < 6:
                            pn = pp.next()
                            fw.mm(pn[:, 0:128], Mk, Nk)
                            fw.mm(pn[:, 128:256], Nk, Mk)
                            fw.copy("act" if (h % 4) else "dve", NMs[(k + 1) % 2][i].re("p a t -> p (a t)"), pn)
                for h in hs:
                    i = h % HG
                    sl = slice(h * 64, (h + 1) * 64)
                    Zf = Zs[1][i]
                    pT = pb.next()
                    fw.tr(pT[0:64, 0:128], Zf[:, 0:64], C.ident)
                    fw.copy("act", WTs[i], pT[0:64, 0:128])
                    pu = pp.next()
                    fw.mm(pu[:, 0:64], WTs[i], Hb[h])
                    fw.tt("dve", Us[i], pu[:, 0:64], Zf[:, 64:128], ALU.add)
                    py = pp.next()
                    fw.mm(py[:, 0:64], T4s[i][:, 1, :], Hb[h], start=True, stop=False)
                    fw.mm(py[:, 0:64], G1s[i][:, 1, :], Us[i], start=False, stop=True)
                    fw.tt("dve", y[:, sl], py[:, 0:64], Yvs[i], ALU.add)
                    ph = pp.next()
                    fw.mm(ph[0:64, 0:64], Bt[:, sl], Us[i])
                    fw.tt("dve", tHs[i], ph[0:64, 0:64], H32[h], ALU.add)
                    fw.stt("dve", H32[h], tHs[i], PC[:, h:h + 1], Gps[i], ALU.mult, ALU.add)
                    fw.copy("act", Hb[h], H32[h])
            fw.dma("sp", yTM_blk[n], y)


def rwkv_post_phase(fw, C, I, yTM_blk, gS_blk, bonS_blk, yA_blk):
    with fw.phase():
        gng = load_bc(fw, "rp_g", I["rwkv_gn_g"][0], 1024)
        gnb = load_bc(fw, "rp_b", I["rwkv_gn_b"][0], 1024)
        yt = Rot([fw.sb("rp_y", [128, 16, 64], F32) for _ in range(2)])
        gt = Rot([fw.sb("rp_gt", [128, 1024], F32) for _ in range(2)])
        bt = Rot([fw.sb("rp_bt", [128, 1024], F32) for _ in range(2)])
        sq = fw.sb("rp_sq", [128, 16, 64], F32)
        st = fw.sb("rp_st", [128, 4, 16], F32)
        ob = Rot([fw.sb("rp_o", [128, 1024], BF16) for _ in range(2)])
        F2 = lambda v: v.re("p h d -> p (h d)")
        for n in range(NB):
            y = yt.next()
            g_ = gt.next()
            b_ = bt.next()
            fw.dma("sp", y.re("p h d -> p (h d)"), yTM_blk[n])
            fw.dma("act", g_, gS_blk[n])
            fw.dma("act", b_, bonS_blk[n])
            fw.red("dve", st[:, 0, :], y)
            fw.ts("dve", st[:, 0, :], st[:, 0, :], 1.0 / 64, ALU.mult)
            fw.tt("dve", y, y, st[:, 0, :].un(2).bc([128, 16, 64]), ALU.subtract)
            fw.tt("pool", sq, y, y, ALU.mult)
            fw.red("dve", st[:, 1, :], sq)
            rstd_from_ss(fw, st[:, 1, :], 64, 64e-5, st[:, 2, :], st[:, 3, :])
            fw.tt("dve", y, y, st[:, 3, :].un(2).bc([128, 16, 64]), ALU.mult)
            fw.tt("pool", F2(y), F2(y), gng, ALU.mult)
            fw.tt("pool", F2(y), F2(y), gnb, ALU.add)
            fw.tt("dve", F2(y), F2(y), b_, ALU.add)
            o_ = ob.next()
            fw.tt("dve", o_, F2(y), g_, ALU.mult)
            fw.dma("sp", yA_blk[n][:, 1024:2048], o_)


def causal_attn(fw, C, heads, dv, pss, pso, E, masks_eng):
    groups = []
    for hi in range(len(heads)):
        for qi in range(NB):
            for sg in range(0, qi + 1, 4):
                groups.append((hi, qi, sg, min(4, qi + 1 - sg)))
    loaded = {}

    def qk(g):
        hi, qi, sg, n = g
        if hi not in loaded:
            loaded[hi] = heads[hi]["load"]()
        qT, kT, V1 = loaded[hi]
        ps_ = pss.next()
        for j in range(n):
            dg = (sg + j == qi)
            fw.mm(ps_[:, j * 128:(j + 1) * 128], kT[:, (sg + j) * 128:(sg + j + 1) * 128], qT[:, qi * 128:(qi + 1) * 128],
                  start=True, stop=not dg)
            if dg:
                fw.mm(ps_[:, j * 128:(j + 1) * 128], C.ident, C.negU4[:, 0, :], start=False, stop=True)
        return ps_
    pend = qk(groups[0])
    po = None
    for gi, g in enumerate(groups):
        ps_ = pend
        if gi + 1 < len(groups):
            pend = qk(groups[gi + 1])
        hi, qi, sg, n = g
        qT, kT, V1 = loaded[hi]
        if sg == 0:
            po = pso.next()
        e_ = E.next()
        fw.act(e_[:, 0:n * 128], ps_[:, 0:n * 128], AF.Exp)
        for j in range(n):
            fw.mm(po[:, 0:dv + 1], e_[:, j * 128:(j + 1) * 128], V1[:, sg + j, 0:dv + 1], start=(sg + j == 0), stop=(sg + j == qi))
        if sg + n - 1 == qi:
            heads[hi]["on_out"](qi, po)


def mla_prep_phase(fw, C, I, u_blk, qTS, kTS, V1S_blk):
    sc = 96 ** -0.5
    with fw.phase():
        gcq = load_bc(fw, "ml_gcq", I["mla_cq_norm"][0], 512)
        gckv = load_bc(fw, "ml_gckv", I["mla_ckv_norm"][0], 256)
        gqn = load_bc(fw, "ml_gqn", I["mla_q_nope_norm"][0], 64, mul=sc)
        gkn = load_bc(fw, "ml_gkn", I["mla_k_nope_norm"][0], 64)
        gqr = load_bc(fw, "ml_gqr", I["mla_q_rope_norm"][0], 32, mul=sc)
        gkr = load_bc(fw, "ml_gkr", I["mla_k_rope_norm"][0], 32)
        wuq = fw.sb("ml_wuq", [128, 4, 1536], BF16)
        wukv = fw.sb("ml_wukv", [128, 2, 2048], BF16)
        fw.dma("pool", wuq, wview(I["mla_w_uq"][0], 0, 1536))
        fw.dma("pool", wukv, wview(I["mla_w_ukv"][0], 0, 2048))
        ut = Rot([fw.sb("ml_u", [128, 800], F32) for _ in range(2)])
        cb = fw.sb("ml_cb", [128, 768], BF16)
        junk = fw.sb("ml_junk", [128, 512], BF16)
        st = fw.sb("ml_st", [128, 8], F32)
        cT = fw.sb("ml_cT", [128, 6, 128], BF16)
        qfull = fw.sb("ml_qfull", [128, 16, 96], F32)
        kvfull = fw.sb("ml_kvfull", [128, 16, 128], F32)
        tq = fw.sb("ml_tq", [128, 16, 64], F32)
        tr_ = fw.sb("ml_tr", [128, 16, 32], F32)
        tkr = fw.sb("ml_tkr", [128, 1, 32], F32)
        kpe = fw.sb("ml_kpe", [128, 1, 32], BF16)
        qf = fw.sb("ml_qf", [128, 16, 96], BF16)
        kf = fw.sb("ml_kf", [128, 16, 96], BF16)
        V1 = Rot([fw.sb("ml_V1", [128, 16, 66], BF16) for _ in range(2)])
        qTt = Rot([fw.sb("ml_qTt", [96, 16, 128], BF16) for _ in range(2)])
        kTt = Rot([fw.sb("ml_kTt", [96, 16, 128], BF16) for _ in range(2)])
        for v_ in V1.items:
            fw.memset("dve", v_, 1.0)
        psT = Rot([fw.ps("ml_psT", [128, 8, 128], BF16) for _ in range(2)])
        psm = Rot([fw.ps("ml_psm", [128, 512], F32) for _ in range(4)])
        for n in range(NB):
            u = ut.next()
            fw.dma("sp", u, u_blk[n][:, 0:800])
            for (c0, c1, g_, j0) in ((0, 512, gcq, 0), (512, 768, gckv, 3)):
                w = c1 - c0
                fw.act(junk[:, 0:w], u[:, c0:c1], AF.Square, accum=st[:, j0:j0 + 1])
                rstd_from_ss(fw, st[:, j0:j0 + 1], w, EPS, st[:, j0 + 1:j0 + 2], st[:, j0 + 2:j0 + 3])
                fw.stt("dve", cb[:, c0:c1], u[:, c0:c1], st[:, j0 + 2:j0 + 3], g_, ALU.mult, ALU.mult)
            p = psT.next()
            for j in range(6):
                fw.tr(p[:, j, :], cb[:, j * 128:(j + 1) * 128], C.ident)
            fw.copy("dve", cT, p[:, 0:6, :])
            for cbk in range(3):
                pq = psm.next()
                for k in range(4):
                    fw.mm(pq, cT[:, k, :], wuq[:, k, cbk * 512:(cbk + 1) * 512], start=(k == 0), stop=(k == 3))
                fw.copy("act", qfull.re("p h d -> p (h d)")[:, cbk * 512:(cbk + 1) * 512], pq)
            for cbk in range(4):
                pk = psm.next()
                for k in range(2):
                    fw.mm(pk, cT[:, 4 + k, :], wukv[:, k, cbk * 512:(cbk + 1) * 512], start=(k == 0), stop=(k == 1))
                fw.copy("act", kvfull.re("p h d -> p (h d)")[:, cbk * 512:(cbk + 1) * 512], pk)
            cs32 = (C.cos32[:, n, :], C.sin32[:, n, :])
            head_norm(fw, qfull[:, :, 0:64], gqn, qf[:, :, 0:64], 16, 64)
            head_norm(fw, qfull[:, :, 64:96], gqr, tr_, 16, 32)
            rope(fw, tr_, qf[:, :, 64:96], cs32[0], cs32[1], 16, 32)
            head_norm(fw, kvfull[:, :, 0:64], gkn, kf[:, :, 0:64], 16, 64)
            head_norm(fw, u[:, 768:800].re("p (h d) -> p h d", h=1), gkr, tkr, 1, 32)
            rope(fw, tkr, kpe, cs32[0], cs32[1], 1, 32)
            fw.copy("dve", kf[:, :, 64:96], kpe.bc([128, 16, 32]))
            V1c = V1.next()
            fw.copy("act", V1c[:, :, 0:64], kvfull[:, :, 64:128])
            fw.dma("sp", V1S_blk[n], V1c.re("p h d -> p (h d)"))
            for (src, dstS, rot) in ((qf, qTS, qTt), (kf, kTS, kTt)):
                tt_ = rot.next()
                for h0 in (0, 8):
                    p = psT.next()
                    for h in range(8):
                        fw.tr(p[0:96, h, :], src[:, h0 + h, :], C.ident)
                    fw.copy("dve", tt_[:, h0:h0 + 8, :], p[0:96, :, :])
                fw.dma("act", dstS[:, :, n * 128:(n + 1) * 128].re("h d t -> d h t"), tt_)


def mla_attn_phase(fw, C, qTS, kTS, V1S, yA):
    with fw.phase():
        qT = Rot([fw.sb("ma_qT", [96, S], BF16) for _ in range(2)])
        kT = Rot([fw.sb("ma_kT", [96, S], BF16) for _ in range(2)])
        V1 = Rot([fw.sb("ma_V1", [128, NB, 66], BF16) for _ in range(2)])
        E = Rot([fw.sb("ma_E", [128, 512], BF16) for _ in range(3)])
        yo = Rot([fw.sb("ma_yo", [128, NB, 64], BF16) for _ in range(2)])
        rec = fw.sb("ma_rec", [128, 2], F32)
        pss = Rot([fw.ps("ma_pss", [128, 512], F32) for _ in range(3)])
        pso = Rot([fw.ps("ma_pso", [128, 512], F32) for _ in range(2)])
        heads = []
        for h in range(16):
            def load(h=h):
                q_ = qT.next()
                k_ = kT.next()
                v_ = V1.next()
                fw.dma("sp", q_, qTS[h])
                fw.dma("sp", k_, kTS[h])
                fw.dma("sp", v_, V1S[:, h * 66:(h + 1) * 66].re("(n p) d -> p n d", p=128))
                return (q_, k_, v_)
            st_ = {}

            def on_out(qi, po, h=h, st_=st_):
                if qi == 0:
                    st_["y"] = yo.next()
                y_ = st_["y"]
                fw.recip(rec[:, 0:1], po[:, 64:65])
                fw.ts("dve", y_[:, qi, :], po[:, 0:64], rec[:, 0:1], ALU.mult)
                if qi == NB - 1:
                    fw.dma("act", yA[:, h * 64:(h + 1) * 64].re("(n p) d -> p n d", p=128), y_)
            heads.append({"load": load, "on_out": on_out})
        causal_attn(fw, C, heads, 64, pss, pso, E, "pool")


def diff_prep_phase(fw, C, I, u_blk, qTS, kTS, V1S_blk):
    with fw.phase():
        gq = load_bc(fw, "df_gq", I["diff_q_norm"][0], 64, mul=0.125)
        gk = load_bc(fw, "df_gk", I["diff_k_norm"][0], 64)
        ut = Rot([fw.sb("df_u", [128, 3072], F32) for _ in range(2)])
        tn_r = Rot([fw.sb("df_tn", [128, 16, 64], F32) for _ in range(2)])
        qf_r = Rot([fw.sb("df_qf", [128, 16, 64], BF16) for _ in range(2)])
        kf_r = Rot([fw.sb("df_kf", [128, 16, 64], BF16) for _ in range(2)])
        V1 = Rot([fw.sb("df_V1", [128, 8, 130], BF16) for _ in range(2)])
        qTt = Rot([fw.sb("df_qTt", [64, 16, 128], BF16) for _ in range(2)])
        kTt = Rot([fw.sb("df_kTt", [64, 16, 128], BF16) for _ in range(2)])
        for v_ in V1.items:
            fw.memset("dve", v_, 1.0)
        psT = Rot([fw.ps("df_psT", [128, 8, 128], BF16) for _ in range(2)])
        for n in range(NB):
            u = ut.next()
            fw.dma("sp", u, u_blk[n][:, 800:3872])
            cs = (C.cos64[:, n, :], C.sin64[:, n, :])
            qf, kf = qf_r.next(), kf_r.next()
            tn = tn_r.next()
            head_norm(fw, u[:, 0:1024].re("p (h d) -> p h d", h=16), gq, tn, 16, 64)
            rope(fw, tn, qf, cs[0], cs[1], 16, 64, eng="dve")
            tn = tn_r.next()
            head_norm(fw, u[:, 1024:2048].re("p (h d) -> p h d", h=16), gk, tn, 16, 64)
            rope(fw, tn, kf, cs[0], cs[1], 16, 64)
            V1c = V1.next()
            fw.copy("act", V1c[:, :, 0:128], u[:, 2048:3072].re("p (h d) -> p h d", h=8))
            fw.dma("sp", V1S_blk[n], V1c.re("p h d -> p (h d)"))
            for (src, dstS, rot) in ((qf, qTS, qTt), (kf, kTS, kTt)):
                tt_ = rot.next()
                for h0 in (0, 8):
                    p = psT.next()
                    for h in range(8):
                        fw.tr(p[0:64, h, :], src[:, h0 + h, :], C.ident)
                    fw.copy("dve", tt_[:, h0:h0 + 8, :], p[0:64, :, :])
                fw.dma("act", dstS[:, :, n * 128:(n + 1) * 128].re("h d t -> d h t"), tt_)


def diff_attn_phase(fw, C, I, qTS, kTS, V1S, yA):
    lam_init = 0.8 - 0.6 * math.exp(-0.3 * 1)
    with fw.phase():
        lt = [load_bc(fw, "df_l%d" % j, I[nm][0], 64) for j, nm in enumerate(("diff_lq1", "diff_lk1", "diff_lq2", "diff_lk2"))]
        ls = fw.sb("df_ls", [128, 4], F32)
        fw.tt("dve", lt[0], lt[0], lt[1], ALU.mult)
        fw.tt("dve", lt[2], lt[2], lt[3], ALU.mult)
        fw.red("dve", ls[:, 0:1], lt[0])
        fw.red("dve", ls[:, 1:2], lt[2])
        fw.act(ls[:, 0:2], ls[:, 0:2], AF.Exp)
        fw.tt("dve", ls[:, 2:3], ls[:, 0:1], ls[:, 1:2], ALU.subtract)
        fw.ts("dve", ls[:, 3:4], ls[:, 2:3], lam_init, ALU.add, -1.0, ALU.mult)
        gsub = load_bc(fw, "df_gs", I["diff_subln"][0], 128, mul=(1.0 - lam_init))
        qT = Rot([fw.sb("da_qT", [64, S], BF16) for _ in range(2)])
        kT = Rot([fw.sb("da_kT", [64, S], BF16) for _ in range(2)])
        V1 = Rot([fw.sb("da_V1", [128, NB, 130], BF16) for _ in range(2)])
        E = Rot([fw.sb("da_E", [128, 512], BF16) for _ in range(3)])
        o1 = fw.sb("da_o1", [128, NB, 128], F32)
        o2 = fw.sb("da_o2", [128, 128], F32)
        sq = fw.sb("da_sq", [128, 128], F32)
        yo = Rot([fw.sb("da_yo", [128, NB, 128], BF16) for _ in range(2)])
        rec = fw.sb("da_rec", [128, 4], F32)
        pss = Rot([fw.ps("da_pss", [128, 512], F32) for _ in range(3)])
        pso = Rot([fw.ps("da_pso", [128, 512], F32) for _ in range(2)])
        heads = []
        vcur = {}
        for h in range(8):
            for m in range(2):
                def load(h=h, m=m):
                    if m == 0:
                        v_ = V1.next()
                        fw.dma("sp", v_, V1S[:, h * 130:(h + 1) * 130].re("(n p) d -> p n d", p=128))
                        vcur[h] = v_
                    q_ = qT.next()
                    k_ = kT.next()
                    fw.dma("sp", q_, qTS[2 * h + m])
                    fw.dma("sp", k_, kTS[2 * h + m])
                    return (q_, k_, vcur[h])
                st_ = vcur

                def on_out(qi, po, h=h, m=m):
                    fw.recip(rec[:, 0:1], po[:, 128:129])
                    if m == 0:
                        fw.ts("dve", o1[:, qi, :], po[:, 0:128], rec[:, 0:1], ALU.mult)
                    else:
                        if qi == 0:
                            vcur[("y", h)] = yo.next()
                        y_ = vcur[("y", h)]
                        fw.tt("dve", rec[:, 1:2], rec[:, 0:1], ls[:, 3:4], ALU.mult)
                        fw.stt("dve", o2, po[:, 0:128], rec[:, 1:2], o1[:, qi, :], ALU.mult, ALU.add)
                        fw.tt("dve", sq, o2, o2, ALU.mult)
                        fw.red("dve", rec[:, 2:3], sq)
                        rstd_from_ss(fw, rec[:, 2:3], 128, EPS, rec[:, 3:4], rec[:, 2:3])
                        fw.stt("dve", y_[:, qi, :], o2, rec[:, 2:3], gsub, ALU.mult, ALU.mult)
                        if qi == NB - 1:
                            fw.dma("act", yA[:, 1024 + h * 128:1024 + (h + 1) * 128].re("(n p) d -> p n d", p=128), y_)
                heads.append({"load": load, "on_out": on_out})
        causal_attn(fw, C, heads, 128, pss, pso, E, "pool")


WNAMES = ['ffn1_norm', 'ffn1_w_gate', 'ffn1_w_up', 'ffn1_w_down', 'mix_norm', 'ab_w_in', 'ab_w_out', 'swa_q_norm',
          'swa_k_norm', 'swa_sinks', 'rwkv_mu', 'rwkv_w0', 'rwkv_w2', 'rwkv_a0', 'rwkv_a2', 'rwkv_g2', 'rwkv_k_k',
          'rwkv_k_a', 'rwkv_r_k', 'rwkv_gn_g', 'rwkv_gn_b', 'cd_w_in', 'cd_w_out', 'mla_cq_norm', 'mla_ckv_norm',
          'mla_w_uq', 'mla_w_ukv', 'mla_q_nope_norm', 'mla_k_nope_norm', 'mla_q_rope_norm', 'mla_k_rope_norm',
          'diff_q_norm', 'diff_k_norm', 'diff_lq1', 'diff_lk1', 'diff_lq2', 'diff_lk2', 'diff_subln', 'memx_norm',
          'memx_w_q', 'memx_q_norm', 'memx_w_o', 'mem_norm', 'mem_w_kv', 'mem_k_norm', 'ffn2_norm', 'ffn2_w_gate',
          'ffn2_w_up', 'ffn2_w_down']


class Consts:
    pass


def build(shapes, phases=None, debug=()):
    nc = bass.Bass("TRN2", target_bir_lowering=False)
    fw = FW(nc)
    I = {}
    I["x"] = fw.dram("x", [S, D], F32, kind="ExternalInput")
    I["mem"] = fw.dram("mem", [256, D], F32, kind="ExternalInput")
    I["pos"] = fw.dram("pos", [128, NB], I32, kind="ExternalInput")
    for nm in WNAMES:
        I[nm] = fw.dram(nm, list(shapes[nm]), F32, kind="ExternalInput")
    I["c_ident"] = fw.dram("c_ident", [128, 128], BF16, kind="ExternalInput")
    I["c_masku"] = fw.dram("c_masku", [128, 128], BF16, kind="ExternalInput")
    I["c_maskl"] = fw.dram("c_maskl", [128, 128], BF16, kind="ExternalInput")
    I["c_inv64"] = fw.dram("c_inv64", [128, 32], F32, kind="ExternalInput")
    I["c_inv32"] = fw.dram("c_inv32", [128, 16], F32, kind="ExternalInput")
    out = fw.dram("out", [S, D], F32, kind="ExternalOutput")

    def scratch(name, shape, dt):
        return fw.dram(name, shape, dt, kind=("ExternalOutput" if name in debug else "Internal"))

    def blks(v, n=NB):
        return [V(Buf(v.buf.name + "_b%d" % i), v.ap[i * 128:(i + 1) * 128]) for i in range(n)]
    xs = scratch("xs", [S, D], F32)
    xs_blk = blks(xs)
    u = scratch("u", [S, 4896], F32)
    u_blk = blks(u)
    yA = scratch("yA", [S, D], BF16)
    yA_blk = blks(yA)
    rowsTM = scratch("rowsTM", [S, 6 * 1024], F32)
    yTM = scratch("yTM", [S, 1024], F32)
    gS = scratch("gS", [S, 1024], F32)
    bonS = scratch("bonS", [S, 1024], F32)
    qTS = scratch("qTS", [16, 96, S], BF16)
    kTS = scratch("kTS", [16, 96, S], BF16)
    mV1S = scratch("mV1S", [S, 16 * 66], BF16)
    dqTS = scratch("dqTS", [16, 64, S], BF16)
    dkTS = scratch("dkTS", [16, 64, S], BF16)
    dV1S = scratch("dV1S", [S, 8 * 130], BF16)
    C = Consts()
    ph = phases

    def on(p):
        return ph is None or p in ph

    setup_phase(fw, C, I)
    fw.memset("dve", C.cst[:, 3:4], -0.5)
    x_blk = blks(I["x"])
    for i in range(NB):
        fw.dma("sp" if i % 2 else "act", xs_blk[i], x_blk[i])
    for layer in range(2):
        L = "L%d" % layer
        if on(L + "ffn1"):
            ffn_phase(fw, C, xs_blk, I["ffn1_norm"][layer], I["ffn1_w_gate"][layer], I["ffn1_w_up"][layer], I["ffn1_w_down"][layer])
        if layer == 0:
            if on("L0win"):
                mk, ev = evac_store(u_blk)
                linear_phase(fw, C, xs_blk, D, I["mix_norm"][0], I["ab_w_in"][0], 4896, ev, extra=mk)
                fw.barrier()
            if on("L0swa"):
                swa_phase(fw, C, I, u_blk, yA_blk)
            if on("L0rwkv"):
                rwkv_prep_phase(fw, C, I, u, u_blk, blks(rowsTM), blks(gS), blks(bonS))
                rwkv_chunk_phase(fw, C, blks(rowsTM), blks(yTM))
                rwkv_post_phase(fw, C, I, blks(yTM), blks(gS), blks(bonS), yA_blk)
            if on("L0wout"):
                mk, ev = evac_resid(xs_blk)
                linear_phase(fw, C, yA_blk, D, None, I["ab_w_out"][0], D, ev, cast_only=True, src_bf16=True, extra=mk)
        else:
            if on("L1win"):
                mk, ev = evac_store(u_blk)
                linear_phase(fw, C, xs_blk, D, I["mix_norm"][1], I["cd_w_in"][0], 3872, ev, extra=mk)
            if on("L1mla"):
                mla_prep_phase(fw, C, I, u_blk, qTS, kTS, blks(mV1S))
                mla_attn_phase(fw, C, qTS, kTS, mV1S, yA)
            if on("L1diff"):
                diff_prep_phase(fw, C, I, u_blk, dqTS, dkTS, blks(dV1S))
                diff_attn_phase(fw, C, I, dqTS, dkTS, dV1S, yA)
            if on("L1wout"):
                mk, ev = evac_resid(xs_blk)
                linear_phase(fw, C, blks(yA), D, None, I["cd_w_out"][0], D, ev, cast_only=True, src_bf16=True, extra=mk)
        if on(L + "memx"):
            memx_phase(fw, C, I, layer, xs_blk)
        if on(L + "ffn2"):
            ffn_phase(fw, C, xs_blk, I["ffn2_norm"][layer], I["ffn2_w_gate"][layer], I["ffn2_w_up"][layer], I["ffn2_w_down"][layer])
    fw.barrier()
    toks = []
    for i in range(NB):
        toks.append(fw.dma("sp" if i % 2 else "act", V(Buf("out%d" % i), out.ap[i * 128:(i + 1) * 128]), xs_blk[i]))
    for t in toks:
        fw._wait("sp", t)
    fw.barrier()
    return nc


def host_consts():
    bf = ml_dtypes.bfloat16
    s = np.arange(128)[:, None]
    q = np.arange(128)[None, :]
    inv64 = (1.0 / (10000.0 ** (np.arange(0, 64, 2, dtype=np.float32) / 64))).astype(np.float32)
    inv32 = (1.0 / (10000.0 ** (np.arange(0, 32, 2, dtype=np.float32) / 32))).astype(np.float32)
    return {
        "c_ident": np.eye(128, dtype=np.float32).astype(bf),
        "c_masku": (q >= s).astype(np.float32).astype(bf),
        "c_maskl": (s > q).astype(np.float32).astype(bf),
        "c_inv64": np.ascontiguousarray(np.broadcast_to(inv64[None, :], (128, 32))),
        "c_inv32": np.ascontiguousarray(np.broadcast_to(inv32[None, :], (128, 16))),
    }


def make_in_maps(inputs, cores):
    cst = host_consts()
    maps = []
    for b in cores:
        m = {"x": np.ascontiguousarray(inputs["x"][b], dtype=np.float32),
             "mem": np.ascontiguousarray(inputs["mem"][b], dtype=np.float32),
             "pos": np.ascontiguousarray(np.asarray(inputs["positions"][b]).astype(np.int32).reshape(NB, 128).T)}
        for nm in WNAMES:
            m[nm] = np.ascontiguousarray(inputs[nm], dtype=np.float32)
        m.update(cst)
        maps.append(m)
    return maps


def kernel(**inputs):
    inputs = {k: np.asarray(v) for k, v in inputs.items()}
    shapes = {nm: inputs[nm].shape for nm in WNAMES}
    nc = build(shapes)
    maps = make_in_maps(inputs, list(range(8)))
    res = run_bass_kernel_spmd(nc, maps, core_ids=list(range(8)))
    return np.stack([np.asarray(r["out"], dtype=np.float32) for r in res.results], axis=0)
```

```python
import math
from contextlib import ExitStack
import numpy as np
import ml_dtypes
import concourse.bass as bass
import concourse.mybir as mybir
from concourse.bass_utils import run_bass_kernel_spmd

F32 = mybir.dt.float32
BF16 = mybir.dt.bfloat16
I32 = mybir.dt.int32
ALU = mybir.AluOpType
AF = mybir.ActivationFunctionType
AX = mybir.AxisListType

S = 2048
D = 2048
NB = S // 128
FF = 5632
EPS = 1e-6
PI = math.pi


class Buf:
    __slots__ = ("name", "w", "r", "excl")

    def __init__(self, name, excl=False):
        self.name = name
        self.w = None
        self.r = []
        self.excl = excl


class V:
    __slots__ = ("buf", "ap")

    def __init__(self, buf, ap):
        self.buf = buf
        self.ap = ap

    def __getitem__(self, k):
        return V(self.buf, self.ap[k])

    def re(self, s, **kw):
        return V(self.buf, self.ap.rearrange(s, **kw))

    def bc(self, shape):
        return V(self.buf, self.ap.to_broadcast(list(shape)))

    def un(self, axis):
        return V(self.buf, self.ap.unsqueeze(axis))

    def pb(self, n):
        return V(self.buf, self.ap.partition_broadcast(n))


class FW:
    NDMA = 6

    def __init__(self, nc):
        self.nc = nc
        self.E = {"pe": nc.tensor, "act": nc.scalar, "dve": nc.vector, "pool": nc.gpsimd, "sp": nc.sync}
        self.sems = {}
        self.cnt = {}
        for e in self.E:
            self.sems[e] = nc.alloc_semaphore("s_" + e)
            self.cnt[e] = 0
        self.dq = {}
        for q in ("sp", "act", "pool"):
            lst = []
            for i in range(self.NDMA):
                k = "d_%s%d" % (q, i)
                self.sems[k] = nc.alloc_semaphore(k)
                self.cnt[k] = 0
                lst.append(k)
            self.dq[q] = [lst, 0]
        self.waited = {e: {} for e in self.E}
        self.stack = None
        self.cache = {}
        self.uid = 0

    def phase(self):
        fw = self

        class P:
            def __enter__(s):
                fw.stack = ExitStack()
                fw.stack.__enter__()
                fw.cache = {}
                return fw

            def __exit__(s, *a):
                fw.barrier()
                fw.stack.close()
                fw.stack = None
                return False
        return P()

    def _nm(self, name):
        self.uid += 1
        return "%s_%d" % (name, self.uid)

    def sb(self, name, shape, dt, persistent=False):
        nm = self._nm(name)
        if persistent or self.stack is None:
            t = self.nc.alloc_sbuf_tensor(nm, list(shape), dt)
        else:
            t = self.stack.enter_context(self.nc.sbuf_tensor(nm, list(shape), dt))
        return V(Buf(nm), t.ap())

    def tmp(self, name, shape, dt):
        key = (name, tuple(shape), str(dt))
        if key not in self.cache:
            self.cache[key] = self.sb(name, shape, dt)
        return self.cache[key]

    def ps(self, name, shape, dt):
        nm = self._nm(name)
        t = self.stack.enter_context(self.nc.psum_tensor(nm, list(shape), dt))
        return V(Buf(nm), t.ap())

    def dram(self, name, shape, dt, kind="Internal"):
        t = self.nc.dram_tensor(name, list(shape), dt, kind=kind)
        return V(Buf(name), t.ap())

    def _wait(self, eng, tok):
        if tok is None:
            return
        k, v = tok
        w = self.waited[eng]
        if w.get(k, 0) >= v:
            return
        if k == eng and eng == "pe":
            return
        self.E[eng].wait_ge(self.sems[k], v)
        w[k] = v

    def _deps(self, eng, reads, writes):
        for x in reads:
            self._wait(eng, x.buf.w)
            if x.buf.excl:
                for t in x.buf.r:
                    self._wait(eng, t)
        for x in writes:
            b = x.buf
            self._wait(eng, b.w)
            for t in b.r:
                self._wait(eng, t)

    def _commit(self, tok, reads, writes):
        for x in reads:
            if x.buf.excl:
                x.buf.w = tok
                x.buf.r = []
                continue
            r = x.buf.r
            r.append(tok)
            if len(r) > 16:
                m = {}
                for k, v in r:
                    if m.get(k, 0) < v:
                        m[k] = v
                x.buf.r = list(m.items())
        for x in writes:
            x.buf.w = tok
            x.buf.r = []

    def op(self, eng, fn, reads, writes):
        self._deps(eng, reads, writes)
        ins = fn()
        self.cnt[eng] += 1
        ins.then_inc(self.sems[eng], 1)
        tok = (eng, self.cnt[eng])
        self._commit(tok, reads, writes)
        return tok

    def dma(self, q, out, in_, **kw):
        lst, i = self.dq[q]
        k = lst[i % len(lst)]
        self.dq[q][1] = i + 1
        self._wait(q, (k, self.cnt[k]))
        self._deps(q, [in_], [out])
        ins = self.E[q].dma_start(out=out.ap, in_=in_.ap, **kw)
        self.cnt[k] += 16
        ins.then_inc(self.sems[k], 16)
        tok = (k, self.cnt[k])
        self._commit(tok, [in_], [out])
        return tok

    def barrier(self):
        for e in self.E:
            for k in self.sems:
                if self.cnt[k] > 0:
                    self._wait(e, (k, self.cnt[k]))

    def mm(self, out, lhsT, rhs, start=True, stop=True):
        return self.op("pe", lambda: self.nc.tensor.matmul(out.ap, lhsT.ap, rhs.ap, start=start, stop=stop),
                       [lhsT, rhs], [out])

    def tr(self, out, in_, ident):
        return self.op("pe", lambda: self.nc.tensor.transpose(out.ap, in_.ap, ident.ap), [in_, ident], [out])

    def act(self, out, in_, func, bias=None, scale=1.0, accum=None):
        reads = [in_]
        kw = {}
        if isinstance(bias, V):
            reads.append(bias)
            kw["bias"] = bias.ap
        elif bias is not None:
            kw["bias"] = bias
        if isinstance(scale, V):
            reads.append(scale)
            kw["scale"] = scale.ap
        else:
            kw["scale"] = scale
        writes = [out]
        if accum is not None:
            kw["accum_out"] = accum.ap
            writes.append(accum)
        return self.op("act", lambda: self.nc.scalar.activation(out.ap, in_.ap, func, **kw), reads, writes)

    def tt(self, eng, out, a, b, op):
        e = self.E[eng]
        return self.op(eng, lambda: e.tensor_tensor(out.ap, a.ap, b.ap, op), [a, b], [out])

    def ts(self, eng, out, a, s1, op0, s2=None, op1=None):
        e = self.E[eng]
        reads = [a]
        if isinstance(s1, V):
            reads.append(s1)
            s1 = s1.ap
        if isinstance(s2, V):
            reads.append(s2)
            s2 = s2.ap
        if op1 is None:
            s2 = 0.0
            op1 = ALU.add
        return self.op(eng, lambda: e.tensor_scalar(out.ap, a.ap, s1, s2, op0, op1), reads, [out])

    def stt(self, eng, out, a, s, b, op0, op1):
        e = self.E[eng]
        reads = [a, b]
        if isinstance(s, V):
            reads.append(s)
            s = s.ap
        return self.op(eng, lambda: e.scalar_tensor_tensor(out.ap, a.ap, s, b.ap, op0, op1), reads, [out])

    def copy(self, eng, out, in_):
        if eng == "act":
            return self.op("act", lambda: self.nc.scalar.copy(out.ap, in_.ap), [in_], [out])
        e = self.E[eng]
        return self.op(eng, lambda: e.tensor_copy(out.ap, in_.ap), [in_], [out])

    def red(self, eng, out, in_, op=ALU.add, axis=AX.X):
        e = self.E[eng]
        return self.op(eng, lambda: e.tensor_reduce(out.ap, in_.ap, axis, op), [in_], [out])

    def memset(self, eng, out, val):
        e = self.E[eng]
        return self.op(eng, lambda: e.memset(out.ap, val), [], [out])

    def recip(self, out, in_):
        return self.op("dve", lambda: self.nc.vector.reciprocal(out.ap, in_.ap), [in_], [out])


class Rot:
    def __init__(self, items):
        self.items = items
        self.i = 0

    def next(self):
        x = self.items[self.i % len(self.items)]
        self.i += 1
        return x


def wview(W, c0, c1, r0=0, r1=None):
    ap = W.ap
    if r1 is None:
        r1 = ap.shape[0]
    return V(W.buf, ap[r0:r1, c0:c1].rearrange("(k p) c -> p k c", p=128))


def rstd_from_ss(fw, ss, n, eps, tmp, out):
    fw.act(tmp, ss, AF.Ln, bias=fw.eps_tiles[eps], scale=1.0 / n)
    fw.act(out, tmp, AF.Exp, scale=-0.5)


def norm_T(fw, C, src_blks, K, gain_bc, dstT, psT, nblk=None, dst_off=0, cast_only=False, src_bf16=False):
    nblk = len(src_blks) if nblk is None else nblk
    KC = K // 128
    xt = Rot([fw.sb("nt_x", [128, K], BF16 if src_bf16 else F32) for _ in range(2)])
    hb = Rot([fw.sb("nt_h", [128, K], BF16) for _ in range(2)])
    junk = fw.sb("nt_j", [128, K], BF16)
    st = Rot([fw.sb("nt_s", [128, 4], F32) for _ in range(2)])
    ev = Rot(["dve", "act"])
    for i in range(nblk):
        x = xt.next()
        fw.dma("sp", x, src_blks[i])
        if cast_only:
            h = x if src_bf16 else hb.next()
            if not src_bf16:
                fw.copy("dve", h, x)
        else:
            s = st.next()
            h = hb.next()
            fw.act(junk, x, AF.Square, accum=s[:, 0:1])
            rstd_from_ss(fw, s[:, 0:1], K, EPS, s[:, 1:2], s[:, 2:3])
            fw.stt("dve", h, x, s[:, 2:3], gain_bc, ALU.mult, ALU.mult)
        for k0 in range(0, KC, 8):
            n = min(8, KC - k0)
            p = psT.next()
            for j in range(n):
                fw.tr(p[:, j, :], h[:, (k0 + j) * 128:(k0 + j + 1) * 128], C.ident)
            fw.copy(ev.next(), dstT[:, k0:k0 + n, dst_off + i * 128: dst_off + (i + 1) * 128], p[:, 0:n, :])


def load_bc(fw, name, vec, n, q="sp", mul=None):
    t = fw.sb(name, [128, n], F32)
    fw.dma(q, t, vec.pb(128))
    if mul is not None:
        fw.ts("dve", t, t, float(mul), ALU.mult)
    return t


def head_norm(fw, src, gain_bc, out, H, d, eng="dve", eps=EPS):
    sq = fw.tmp("hn_sq", [128, H, d], F32)
    st = fw.tmp("hn_st", [128, 3, H], F32)
    fw.act(sq, src, AF.Square)
    fw.red("dve", st[:, 0, :], sq)
    rstd_from_ss(fw, st[:, 0, :], d, eps, st[:, 1, :], st[:, 2, :])
    fw.tt(eng, sq, src, st[:, 2, :].un(2).bc([128, H, d]), ALU.mult)
    fw.tt(eng, out, sq, gain_bc.un(1).bc([128, H, d]), ALU.mult)


def rope(fw, x, out, cos, sin, H, d, eng="pool"):
    h = d // 2
    t1 = fw.tmp("rp_1" + eng, [128, H, h], F32)
    t2 = fw.tmp("rp_2" + eng, [128, H, h], F32)
    cb = cos.un(1).bc([128, H, h])
    sbb = sin.un(1).bc([128, H, h])
    x1 = x[:, :, 0:h]
    x2 = x[:, :, h:d]
    fw.tt(eng, t1, x1, cb, ALU.mult)
    fw.tt(eng, t2, x2, sbb, ALU.mult)
    fw.tt(eng, out[:, :, 0:h], t1, t2, ALU.subtract)
    fw.tt(eng, t1, x1, sbb, ALU.mult)
    fw.tt(eng, t2, x2, cb, ALU.mult)
    fw.tt(eng, out[:, :, h:d], t1, t2, ALU.add)


def ffn_phase(fw, C, xs_blk, norm_g, wg, wu, wd, T=1024, NF=4):
    FC = FF // 128
    FG = FC // NF
    with fw.phase():
        gain = load_bc(fw, "ffn_g", norm_g, D)
        hT = fw.sb("hT", [128, 16, T], BF16)
        actT = fw.sb("actT", [128, FG, T], BF16)
        wgb = Rot([fw.sb("wgb", [128, 16, 256], BF16) for _ in range(2)])
        wub = Rot([fw.sb("wub", [128, 16, 256], BF16) for _ in range(2)])
        wdb = Rot([fw.sb("wdb", [128, FG, 512], BF16) for _ in range(2)])
        sg = Rot([fw.sb("sg", [128, 512], F32) for _ in range(2)])
        xo = Rot([fw.sb("xo", [128, 512], F32) for _ in range(4)])
        xn = Rot([fw.sb("xn", [128, 512], F32) for _ in range(4)])
        psT = Rot([fw.ps("psT", [128, 8, 128], BF16) for _ in range(1)])
        psg = Rot([fw.ps("psg", [128, 512], F32) for _ in range(2)])
        psu = Rot([fw.ps("psu", [128, 512], F32) for _ in range(2)])
        pso = Rot([fw.ps("pso", [128, 512], F32) for _ in range(3)])
        for tt in range(S // T):
            nsub = T // 128
            norm_T(fw, C, xs_blk[tt * nsub:(tt + 1) * nsub], D, gain, hT, psT)
            for fg in range(NF):
                for fp in range(FG // 2 + FG % 2):
                    nj = min(2, FG - fp * 2)
                    c0 = (fg * FG + fp * 2) * 128
                    g_ = wgb.next()
                    u_ = wub.next()
                    fw.dma("pool", g_[:, :, 0:nj * 128], wview(wg, c0, c0 + nj * 128))
                    fw.dma("pool", u_[:, :, 0:nj * 128], wview(wu, c0, c0 + nj * 128))
                    for j in range(nj):
                        fl = fp * 2 + j
                        for th in range(T // 512):
                            pg = psg.next()
                            pu = psu.next()
                            tk = slice(th * 512, (th + 1) * 512)
                            for k in range(16):
                                fw.mm(pg, g_[:, k, j * 128:(j + 1) * 128], hT[:, k, tk], start=(k == 0), stop=(k == 15))
                            for k in range(16):
                                fw.mm(pu, u_[:, k, j * 128:(j + 1) * 128], hT[:, k, tk], start=(k == 0), stop=(k == 15))
                            s_ = sg.next()
                            fw.act(s_, pg, AF.Silu)
                            fw.tt("dve", actT[:, fl, tk], s_, pu, ALU.mult)
                for db in range(4):
                    w_ = wdb.next()
                    fw.dma("pool", w_, wview(wd, db * 512, (db + 1) * 512, fg * FG * 128, (fg + 1) * FG * 128))
                    for sub in range(nsub):
                        blk = xs_blk[tt * nsub + sub]
                        po = pso.next()
                        for f in range(FG):
                            fw.mm(po, actT[:, f, sub * 128:(sub + 1) * 128], w_[:, f, :], start=(f == 0), stop=(f == FG - 1))
                        o_ = xo.next()
                        n_ = xn.next()
                        fw.dma("sp", o_, blk[:, db * 512:(db + 1) * 512])
                        fw.stt("dve", n_, po, 0.5, o_, ALU.mult, ALU.add)
                        fw.dma("act", blk[:, db * 512:(db + 1) * 512], n_)


def linear_phase(fw, C, src_blks, K, norm_g, W, ncols, evac, cast_only=False, src_bf16=False, extra=None):
    KC = K // 128
    with fw.phase():
        gain = None if cast_only else load_bc(fw, "lin_g", norm_g, K)
        hT = fw.sb("lhT", [128, KC, S], BF16)
        psT = Rot([fw.ps("lpsT", [128, 8, 128], BF16) for _ in range(2)])
        pso = Rot([fw.ps("lpso", [128, 512], F32) for _ in range(4)])
        wb = Rot([fw.sb("lwb", [128, KC, 512], BF16) for _ in range(2)])
        ctx = extra(fw) if extra is not None else None
        norm_T(fw, C, src_blks, K, gain, hT, psT, cast_only=cast_only, src_bf16=src_bf16)
        for c0 in range(0, ncols, 512):
            c1 = min(ncols, c0 + 512)
            w_ = wb.next()
            fw.dma("pool", w_[:, :, 0:c1 - c0], wview(W, c0, c1))
            for i in range(NB):
                po = pso.next()
                for k in range(KC):
                    fw.mm(po[:, 0:c1 - c0], hT[:, k, i * 128:(i + 1) * 128], w_[:, k, 0:c1 - c0], start=(k == 0), stop=(k == KC - 1))
                evac(fw, i, c0, c1, po[:, 0:c1 - c0], ctx)


def evac_store(dst_blks):
    def mk(fw):
        return {"t": Rot([fw.sb("ev_t", [128, 512], F32) for _ in range(3)]), "e": Rot(["act", "dve"])}

    def ev(fw, i, c0, c1, po, ctx):
        t = ctx["t"].next()
        fw.copy(ctx["e"].next(), t[:, 0:c1 - c0], po)
        fw.dma("sp", dst_blks[i][:, c0:c1], t[:, 0:c1 - c0])
    return mk, ev


def evac_resid(xs_blk):
    def mk(fw):
        return {"o": Rot([fw.sb("er_o", [128, 512], F32) for _ in range(3)]),
                "n": Rot([fw.sb("er_n", [128, 512], F32) for _ in range(3)])}

    def ev(fw, i, c0, c1, po, ctx):
        o_ = ctx["o"].next()
        n_ = ctx["n"].next()
        fw.dma("sp", o_, xs_blk[i][:, c0:c1])
        fw.tt("dve", n_, po, o_, ALU.add)
        fw.dma("act", xs_blk[i][:, c0:c1], n_)
    return mk, ev


def setup_phase(fw, C, I):
    C.ident = fw.sb("ident", [128, 128], BF16, persistent=True)
    C.maskU = fw.sb("maskU", [128, 128], BF16, persistent=True)
    C.maskL = fw.sb("maskL", [128, 128], BF16, persistent=True)
    C.cos64 = fw.sb("cos64", [128, NB, 32], F32, persistent=True)
    C.sin64 = fw.sb("sin64", [128, NB, 32], F32, persistent=True)
    C.cos32 = fw.sb("cos32", [128, NB, 16], F32, persistent=True)
    C.sin32 = fw.sb("sin32", [128, NB, 16], F32, persistent=True)
    C.memKT = fw.sb("memKT", [128, 4, 256], BF16, persistent=True)
    C.memV1 = fw.sb("memV1", [128, 2, 4, 130], BF16, persistent=True)
    C.cst = fw.sb("cst", [128, 8], F32, persistent=True)
    fw.eps_tiles = {}
    for j, e in enumerate([EPS, 64e-5, 0.0]):
        fw.memset("dve", C.cst[:, j:j + 1], e)
        fw.eps_tiles[e] = C.cst[:, j:j + 1]
    fw.dma("sp", C.ident, I["c_ident"])
    fw.dma("sp", C.maskU, I["c_masku"])
    fw.dma("sp", C.maskL, I["c_maskl"])
    C.negU4 = fw.sb("negU4", [128, 4, 128], BF16, persistent=True)
    C.negL4 = fw.sb("negL4", [128, 4, 128], BF16, persistent=True)
    fw.ts("dve", C.negU4, C.maskU.un(1).bc([128, 4, 128]), -1.0, ALU.add, 200.0, ALU.mult)
    fw.ts("dve", C.negL4, C.maskL.un(1).bc([128, 4, 128]), -1.0, ALU.add, 200.0, ALU.mult)
    with fw.phase():
        posi = fw.sb("posi", [128, NB], I32)
        posf = fw.sb("posf", [128, NB], F32)
        fw.dma("sp", posi, I["pos"])
        fw.copy("dve", posf, posi)
        for (d2, invn, ct, st_) in ((32, "c_inv64", C.cos64, C.sin64), (16, "c_inv32", C.cos32, C.sin32)):
            inv = fw.sb("inv", [128, d2], F32)
            fw.dma("sp", inv, I[invn])
            ang = fw.sb("ang", [128, NB, d2], F32)
            a2 = fw.sb("ang2", [128, NB, d2], F32)
            ni = fw.sb("angi", [128, NB, d2], I32)
            nf = fw.sb("angf", [128, NB, d2], F32)
            fw.tt("dve", ang, posf.un(2).bc([128, NB, d2]), inv.un(1).bc([128, NB, d2]), ALU.mult)
            for (off, dst) in ((0.0, st_), (PI / 2, ct)):
                fw.ts("dve", a2, ang, off, ALU.add)
                fw.ts("dve", nf, a2, 1.0 / (2 * PI), ALU.mult)
                fw.copy("dve", ni, nf)
                fw.copy("dve", nf, ni)
                fw.stt("dve", a2, nf, -2 * PI, a2, ALU.mult, ALU.add)
                fw.ts("dve", a2, a2, 3.1415925, ALU.min, -3.1415925, ALU.max)
                fw.act(dst, a2, AF.Sin)
    with fw.phase():
        gain = load_bc(fw, "mem_g", I["mem_norm"], D)
        gk = load_bc(fw, "mem_gk", I["mem_k_norm"], 128)
        memT = fw.sb("memT", [128, 16, 256], BF16)
        wkv = fw.sb("wkv", [128, 16, 1024], BF16)
        psT = Rot([fw.ps("mpsT", [128, 8, 128], BF16) for _ in range(2)])
        pso = Rot([fw.ps("mpso", [128, 512], F32) for _ in range(2)])
        kf = fw.sb("mkf", [128, 4, 128], BF16)
        mem_blks = [I["mem"][i * 128:(i + 1) * 128, :] for i in range(2)]
        fw.dma("pool", wkv, wview(I["mem_w_kv"], 0, 1024))
        fw.memset("dve", C.memV1, 1.0)
        norm_T(fw, C, mem_blks, D, gain, memT, psT)
        for mt in range(2):
            pk = pso.next()
            for k in range(16):
                fw.mm(pk, memT[:, k, mt * 128:(mt + 1) * 128], wkv[:, k, 0:512], start=(k == 0), stop=(k == 15))
            head_norm(fw, pk.re("p (h d) -> p h d", h=4), gk, kf, 4, 128)
            p = psT.next()
            for h in range(4):
                fw.tr(p[:, h, :], kf[:, h, :], C.ident)
            fw.copy("dve", C.memKT[:, :, mt * 128:(mt + 1) * 128], p[:, 0:4, :])
            pv = pso.next()
            for k in range(16):
                fw.mm(pv, memT[:, k, mt * 128:(mt + 1) * 128], wkv[:, k, 512:1024], start=(k == 0), stop=(k == 15))
            fw.copy("dve", C.memV1[:, mt, :, 0:128], pv.re("p (h d) -> p h d", h=4))


def memx_phase(fw, C, I, layer, xs_blk):
    with fw.phase():
        gain = load_bc(fw, "mx_g", I["memx_norm"][layer], D)
        gq = load_bc(fw, "mx_gq", I["memx_q_norm"][layer], 128, mul=128 ** -0.5)
        hT = fw.sb("mx_hT", [128, 16, S], BF16)
        wq = fw.sb("mx_wq", [128, 16, 512], BF16)
        wo = fw.sb("mx_wo", [128, 4, 2048], BF16)
        fw.dma("pool", wq, wview(I["memx_w_q"][layer], 0, 512))
        fw.dma("pool", wo, wview(I["memx_w_o"][layer], 0, 2048))
        psT = Rot([fw.ps("mx_psT", [128, 8, 128], BF16) for _ in range(1)])
        psq = Rot([fw.ps("mx_psq", [128, 512], F32) for _ in range(1)])
        pssc = Rot([fw.ps("mx_pssc", [128, 512], F32) for _ in range(2)])
        pss = Rot([fw.ps("mx_pss", [128, 2, 256], F32) for _ in range(2)])
        pso = Rot([fw.ps("mx_pso", [128, 512], F32) for _ in range(2)])
        norm_T(fw, C, xs_blk, D, gain, hT, psT)
        qf_r = Rot([fw.sb("mx_qf", [128, 4, 128], BF16) for _ in range(2)])
        qT_r = Rot([fw.sb("mx_qT", [128, 4, 128], BF16) for _ in range(2)])
        E = Rot([fw.sb("mx_E", [128, 2, 128], BF16) for _ in range(3)])
        rec_r = Rot([fw.sb("mx_rec", [128, 4], F32) for _ in range(2)])
        ob_r = Rot([fw.sb("mx_ob", [128, 4, 128], BF16) for _ in range(2)])
        oT_r = Rot([fw.sb("mx_oT", [128, 4, 128], BF16) for _ in range(2)])
        xo = Rot([fw.sb("mx_xo", [128, 2048], F32) for _ in range(2)])

        def prep(i):
            qf, qT = qf_r.next(), qT_r.next()
            pq = psq.next()
            for k in range(16):
                fw.mm(pq, hT[:, k, i * 128:(i + 1) * 128], wq[:, k, :], start=(k == 0), stop=(k == 15))
            head_norm(fw, pq.re("p (h d) -> p h d", h=4), gq, qf, 4, 128)
            p = psT.next()
            for h in range(4):
                fw.tr(p[:, h, :], qf[:, h, :], C.ident)
            fw.copy("dve", qT, p[:, 0:4, :])
            return qT

        def attn(i, qT):
            rec, ob, oT = rec_r.next(), ob_r.next(), oT_r.next()
            o_ = xo.next()
            fw.dma("sp", o_, xs_blk[i])
            pa = pss.next()
            pb = pss.next()

            def scores(h):
                ps_ = pssc.next()
                for mt in range(2):
                    fw.mm(ps_[:, mt * 128:(mt + 1) * 128], C.memKT[:, h, mt * 128:(mt + 1) * 128], qT[:, h, :])
                return ps_
            pend = scores(0)
            for h in range(4):
                ps_ = pend
                if h + 1 < 4:
                    pend = scores(h + 1)
                e_ = E.next()
                fw.act(e_.re("p a b -> p (a b)"), ps_[:, 0:256], AF.Exp)
                pacc = (pa if h < 2 else pb)[:, h % 2, 0:129]
                for mt in range(2):
                    fw.mm(pacc, e_[:, mt, :], C.memV1[:, mt, h, 0:129], start=(mt == 0), stop=(mt == 1))
            for (pp_, h0) in ((pa, 0), (pb, 2)):
                fw.recip(rec[:, h0:h0 + 2], pp_[:, :, 128])
                fw.tt("dve", ob[:, h0:h0 + 2, :], pp_[:, :, 0:128], rec[:, h0:h0 + 2].un(2).bc([128, 2, 128]), ALU.mult)
            p = psT.next()
            for h in range(4):
                fw.tr(p[:, h, :], ob[:, h, :], C.ident)
            fw.copy("act", oT, p[:, 0:4, :])
            for db in range(4):
                po = pso.next()
                for k in range(4):
                    fw.mm(po, oT[:, k, :], wo[:, k, db * 512:(db + 1) * 512], start=(k == 0), stop=(k == 3))
                fw.tt("dve", o_[:, db * 512:(db + 1) * 512], po, o_[:, db * 512:(db + 1) * 512], ALU.add)
            fw.dma("act", xs_blk[i], o_)

        cur = prep(0)
        for i in range(NB):
            nxt = prep(i + 1) if i + 1 < NB else None
            attn(i, cur)
            cur = nxt


def swa_phase(fw, C, I, u_blk, yA_blk):
    with fw.phase():
        gq = load_bc(fw, "sw_gq", I["swa_q_norm"][0], 64, mul=0.125)
        gk = load_bc(fw, "sw_gk", I["swa_k_norm"][0], 64)
        esink = load_bc(fw, "sw_sink", I["swa_sinks"][0], 16)
        fw.act(esink, esink, AF.Exp)
        ut = Rot([fw.sb("sw_u", [128, 1536], F32) for _ in range(2)])
        qn_r = Rot([fw.sb("sw_qn", [128, 16, 64], F32) for _ in range(2)])
        kn_r = Rot([fw.sb("sw_kn", [128, 4, 64], F32) for _ in range(2)])
        qf_r = Rot([fw.sb("sw_qf", [128, 16, 64], BF16) for _ in range(2)])
        kf_r = Rot([fw.sb("sw_kf", [128, 4, 64], BF16) for _ in range(2)])
        qT_r = Rot([fw.sb("sw_qT", [64, 16, 128], BF16) for _ in range(2)])
        kT = Rot([fw.sb("sw_kT", [64, 4, 128], BF16) for _ in range(3)])
        V1 = Rot([fw.sb("sw_V1", [128, 4, 66], BF16) for _ in range(3)])
        E = Rot([fw.sb("sw_E", [128, 4, 128], BF16) for _ in range(4)])
        den = fw.sb("sw_den", [128, 4], F32)
        yt = Rot([fw.sb("sw_y", [128, 16, 64], BF16) for _ in range(2)])
        psT = Rot([fw.ps("sw_psT", [128, 8, 128], BF16) for _ in range(2)])
        pss = Rot([fw.ps("sw_pss", [128, 512], F32) for _ in range(3)])
        pso = Rot([fw.ps("sw_pso", [128, 4, 128], F32) for _ in range(2)])
        for v_ in V1.items:
            fw.memset("dve", v_, 1.0)

        def prep(n):
            u = ut.next()
            qn, kn, qf, kf, qT = qn_r.next(), kn_r.next(), qf_r.next(), kf_r.next(), qT_r.next()
            fw.dma("sp", u, u_blk[n][:, 0:1536])
            head_norm(fw, u[:, 0:1024].re("p (h d) -> p h d", h=16), gq, qn, 16, 64)
            rope(fw, qn, qf, C.cos64[:, n, :], C.sin64[:, n, :], 16, 64, eng="dve")
            head_norm(fw, u[:, 1024:1280].re("p (h d) -> p h d", h=4), gk, kn, 4, 64)
            rope(fw, kn, kf, C.cos64[:, n, :], C.sin64[:, n, :], 4, 64)
            V1c = V1.next()
            fw.copy("act", V1c[:, :, 0:64], u[:, 1280:1536].re("p (h d) -> p h d", h=4))
            for h0 in (0, 8):
                p = psT.next()
                for h in range(8):
                    fw.tr(p[0:64, h, :], qf[:, h0 + h, :], C.ident)
                fw.copy("dve", qT[:, h0:h0 + 8, :], p[0:64, :, :])
            kTc = kT.next()
            p = psT.next()
            for h in range(4):
                fw.tr(p[0:64, h, :], kf[:, h, :], C.ident)
            fw.copy("dve", kTc, p[0:64, 0:4, :])
            return (qT, kTc, V1c)

        def attn(n, cur, prev):
            qT, kTc, V1c = cur
            y = yt.next()
            items = []
            for g in range(4):
                if prev is not None:
                    items.append((g, prev[1], prev[2], C.negL4, False))
                items.append((g, kTc, V1c, C.negU4, True))

            def scores(it):
                g, kt_, v1_, neg, last = it
                ps_ = pss.next()
                fw.mm(ps_, kt_[:, g, :], qT[:, 4 * g:4 * g + 4, :].re("p a b -> p (a b)"), start=True, stop=False)
                fw.mm(ps_, C.ident, neg.re("p a b -> p (a b)"), start=False, stop=True)
                return ps_
            pend = scores(items[0])
            es = []
            for i, it in enumerate(items):
                ps_ = pend
                if i + 1 < len(items):
                    pend = scores(items[i + 1])
                g, kt_, v1_, neg, last = it
                e_ = E.next()
                fw.act(e_.re("p a b -> p (a b)"), ps_, AF.Exp)
                es.append((e_, v1_))
                if last:
                    po = pso.next()
                    for hq in range(4):
                        for j, (ee, vv) in enumerate(es):
                            fw.mm(po[:, hq, 0:65], ee[:, hq, :], vv[:, g, 0:65], start=(j == 0), stop=(j == len(es) - 1))
                    fw.tt("dve", den, po[:, :, 64], esink[:, 4 * g:4 * g + 4], ALU.add)
                    fw.recip(den, den)
                    fw.tt("dve", y[:, 4 * g:4 * g + 4, :], po[:, :, 0:64], den.un(2).bc([128, 4, 64]), ALU.mult)
                    es = []
            fw.dma("act", yA_blk[n][:, 0:1024], y.re("p h d -> p (h d)"))

        cur = prep(0)
        prev = None
        for n in range(NB):
            nxt = prep(n + 1) if n + 1 < NB else None
            attn(n, cur, prev)
            prev, cur = cur, nxt


def rwkv_prep_phase(fw, C, I, u, u_blk, rowsTM_blk, gS_blk, bonS_blk):
    RW0 = 1536
    RWW = 3360
    with fw.phase():
        mu = load_bc(fw, "rk_mu", I["rwkv_mu"][0], RWW)
        w0 = load_bc(fw, "rk_w0", I["rwkv_w0"][0], 1024)
        a0 = load_bc(fw, "rk_a0", I["rwkv_a0"][0], 1024)
        kkw = load_bc(fw, "rk_kk", I["rwkv_k_k"][0], 1024)
        kaw = load_bc(fw, "rk_ka", I["rwkv_k_a"][0], 1024)
        rkw = load_bc(fw, "rk_rk", I["rwkv_r_k"][0], 1024)
        w2 = fw.sb("rk_w2", [64, 1024], BF16)
        a2 = fw.sb("rk_a2", [64, 1024], BF16)
        g2a = fw.sb("rk_g2a", [128, 1024], BF16)
        g2b = fw.sb("rk_g2b", [32, 1024], BF16)
        fw.dma("pool", w2, I["rwkv_w2"][0])
        fw.dma("pool", a2, I["rwkv_a2"][0])
        fw.dma("pool", g2a, I["rwkv_g2"][0][0:128, :])
        fw.dma("pool", g2b, I["rwkv_g2"][0][128:160, :])
        pt = Rot([fw.sb("rk_p", [128, RWW], F32) for _ in range(2)])
        pv = Rot([fw.sb("rk_pv", [128, RWW], F32) for _ in range(1)])
        xs = fw.sb("rk_xs", [128, RWW], F32)
        lo = fw.sb("rk_lo", [128, 288], BF16)
        loT = fw.sb("rk_loT", [128, 4, 128], BF16)
        t1 = fw.sb("rk_t1", [128, 1024], F32)
        t2 = fw.sb("rk_t2", [128, 1024], F32)
        aa = fw.sb("rk_a", [128, 1024], F32)
        kk = fw.sb("rk_kkn", [128, 1024], F32)
        st = fw.sb("rk_st", [128, 4, 16], F32)
        rows = Rot([fw.sb("rk_rows", [128, 6, 1024], F32) for _ in range(1)])
        gg = Rot([fw.sb("rk_g", [128, 1024], F32) for _ in range(1)])
        bon = Rot([fw.sb("rk_bon", [128, 1024], F32) for _ in range(1)])
        psT = Rot([fw.ps("rk_psT", [128, 8, 128], BF16) for _ in range(1)])
        psm = Rot([fw.ps("rk_psm", [128, 512], F32) for _ in range(4)])
        H3 = lambda v: v.re("p (h d) -> p h d", h=16)
        for n in range(NB):
            p = pt.next()
            pr = pv.next()
            fw.dma("sp", p, u_blk[n][:, RW0:RW0 + RWW])
            if n == 0:
                fw.memset("dve", pr, 0.0)
                fw.dma("act", pr[1:128, :], u[0:127, RW0:RW0 + RWW])
            else:
                fw.dma("act", pr, u[n * 128 - 1:n * 128 + 127, RW0:RW0 + RWW])
            fw.tt("dve", pr, pr, p, ALU.subtract)
            fw.tt("pool", pr, pr, mu, ALU.mult)
            fw.tt("dve", xs, pr, p, ALU.add)
            r_ = xs[:, 0:1024]
            k_ = xs[:, 1024:2048]
            v_ = xs[:, 2048:3072]
            fw.act(lo[:, 0:64], xs[:, 3072:3136], AF.Tanh)
            fw.copy("dve", lo[:, 64:128], xs[:, 3136:3200])
            fw.act(lo[:, 128:288], xs[:, 3200:3360], AF.Sigmoid)
            ptt = psT.next()
            fw.tr(ptt[0:64, 0, :], lo[:, 0:64], C.ident)
            fw.tr(ptt[0:64, 1, :], lo[:, 64:128], C.ident)
            fw.tr(ptt[:, 2, :], lo[:, 128:256], C.ident)
            fw.tr(ptt[0:32, 3, :], lo[:, 256:288], C.ident)
            fw.copy("dve", loT[0:64, 0:2, :], ptt[0:64, 0:2, :])
            fw.copy("dve", loT[:, 2, :], ptt[:, 2, :])
            fw.copy("dve", loT[0:32, 3, :], ptt[0:32, 3, :])
            R = rows.next()
            g_ = gg.next()
            for cb in range(2):
                cs = slice(cb * 512, (cb + 1) * 512)
                pw = psm.next()
                fw.mm(pw, loT[0:64, 0, :], w2[:, cs])
                pa = psm.next()
                fw.mm(pa, loT[0:64, 1, :], a2[:, cs])
                pg = psm.next()
                fw.mm(pg, loT[:, 2, :], g2a[:, cs], start=True, stop=False)
                fw.mm(pg, loT[0:32, 3, :], g2b[:, cs], start=False, stop=True)
                fw.copy("act", g_[:, cs], pg)
                fw.tt("dve", t1[:, cs], pw, w0[:, cs], ALU.add)
                fw.tt("dve", aa[:, cs], pa, a0[:, cs], ALU.add)
            fw.act(t1, t1, AF.Exp, scale=-1.0)
            fw.act(t1, t1, AF.Ln, bias=1.0)
            fw.act(t1, t1, AF.Exp, scale=-1.0, bias=C.cst[:, 3:4])
            fw.copy("act", R[:, 1, :], t1)
            fw.act(aa, aa, AF.Sigmoid)
            fw.tt("pool", kk, k_, kkw, ALU.mult)
            fw.act(t2, kk, AF.Square)
            fw.red("dve", st[:, 0, :], H3(t2))
            fw.act(st[:, 1, :], st[:, 0, :], AF.Sqrt)
            fw.ts("dve", st[:, 1, :], st[:, 1, :], 1e-12, ALU.max)
            fw.recip(st[:, 2, :], st[:, 1, :])
            fw.tt("dve", H3(kk), H3(kk), st[:, 2, :].un(2).bc([128, 16, 64]), ALU.mult)
            fw.op("act", lambda: fw.nc.scalar.mul(R[:, 0, :].ap, kk.ap, -1.0), [kk], [R[:, 0, :]])
            fw.tt("pool", R[:, 2, :], kk, aa, ALU.mult)
            fw.ts("dve", t2, aa, -1.0, ALU.add)
            fw.tt("dve", t2, t2, kaw, ALU.mult)
            fw.stt("dve", R[:, 3, :], t2, 1.0, k_, ALU.add, ALU.mult)
            fw.copy("act", R[:, 4, :], r_)
            fw.tt("dve", t2, r_, rkw, ALU.mult)
            fw.tt("dve", t2, t2, R[:, 3, :], ALU.mult)
            fw.red("dve", st[:, 3, :], H3(t2))
            b_ = bon.next()
            fw.tt("dve", H3(b_), H3(v_), st[:, 3, :].un(2).bc([128, 16, 64]), ALU.mult)
            fw.dma("sp", gS_blk[n], g_)
            fw.dma("sp", bonS_blk[n], b_)
            fw.copy("act", R[:, 5, :], v_)
            fw.dma("act", rowsTM_blk[n], R.re("p a c -> p (a c)"))


def rwkv_scan_phase(fw, C, rowS, vS, yS):
    TB = 32
    with fw.phase():
        St = [fw.sb("sc_S", [128, 8, 64], F32) for _ in range(2)]
        A = fw.sb("sc_A", [128, 8, 64], F32)
        B = fw.sb("sc_B", [128, 8, 64], F32)
        Cc = Rot([fw.sb("sc_C", [128, 8, 64], F32) for _ in range(2)])
        Ee_r = Rot([fw.sb("sc_E", [128, 8, 64], F32) for _ in range(2)])
        sa = fw.sb("sc_sa", [128, 8], F32)
        rb = Rot([fw.sb("sc_rows", [128, 5, TB, 64], F32) for _ in range(2)])
        vb = Rot([fw.sb("sc_v", [128, TB, 8], F32) for _ in range(2)])
        yb = Rot([fw.sb("sc_y", [128, TB, 8], F32) for _ in range(2)])
        fw.memset("dve", St[0], 0.0)
        cur = 0
        qs = Rot(["sp", "act"])
        for b in range(S // TB):
            t0 = b * TB
            R = rb.next()
            vv = vb.next()
            yy = yb.next()
            for ty in range(5):
                for h in range(16):
                    fw.dma(qs.next(), R[h * 8:(h + 1) * 8, ty, :, :], rowS[ty, h, t0:t0 + TB, :].pb(8))
            fw.dma(qs.next(), vv, vS[:, t0:t0 + TB, :])
            for t in range(TB):
                S0 = St[cur]
                S1 = St[1 - cur]
                rowb = lambda ty: R[:, ty, t, :].un(1).bc([128, 8, 64])
                c_ = Cc.next()
                fw.tt("dve", c_, rowb(3), vv[:, t, :].un(2).bc([128, 8, 64]), ALU.mult)
                fw.tt("dve", A, S0, rowb(0), ALU.mult)
                fw.red("dve", sa, A)
                fw.tt("dve", S1, S0, rowb(1), ALU.mult)
                fw.tt("dve", B, rowb(2), sa.un(2).bc([128, 8, 64]), ALU.mult)
                fw.tt("dve", S1, S1, B, ALU.add)
                fw.tt("dve", S1, S1, c_, ALU.add)
                Ee = Ee_r.next()
                fw.tt("dve", Ee, S1, rowb(4), ALU.mult)
                fw.red("dve", yy[:, t, :], Ee)
                cur = 1 - cur
            fw.dma(qs.next(), yS[:, t0:t0 + TB, :], yy)


def rwkv_chunk_phase(fw, C, rowsTM_blk, yTM_blk):
    HG = 8
    with fw.phase():
        triU = fw.sb("ck_tri", [128, 128], F32)
        mk2 = fw.sb("ck_mk2", [128, 2, 128], F32)
        mSL = fw.sb("ck_msl", [128, 128], F32)
        ones = fw.sb("ck_ones", [128, 1], F32)
        fw.copy("dve", triU, C.maskU)
        fw.copy("dve", mk2[:, 1, :], C.maskU)
        fw.tt("dve", mk2[:, 0, :], C.maskU, C.ident, ALU.subtract)
        fw.copy("dve", mSL, C.maskL)
        fw.memset("dve", ones, 1.0)
        H32 = [fw.sb("ck_H32", [64, 64], F32) for _ in range(16)]
        Hb = [fw.sb("ck_Hb", [64, 64], BF16) for _ in range(16)]
        for h in range(16):
            fw.memset("dve", H32[h], 0.0)
            fw.memset("dve", Hb[h], 0.0)
        inb = Rot([fw.sb("ck_in", [128, 6, 1024], F32) for _ in range(2)])
        Lt = fw.sb("ck_L", [128, 1024], F32)
        Pt = fw.sb("ck_Pt", [128, 1024], F32)
        Pinv = fw.sb("ck_Pinv", [128, 1024], F32)
        Pm1 = fw.sb("ck_Pm1", [128, 1024], F32)
        At = fw.sb("ck_At", [128, 1024], BF16)
        Bt = fw.sb("ck_Bt", [128, 1024], BF16)
        Kt = fw.sb("ck_Kt", [128, 1024], BF16)
        Rt = fw.sb("ck_Rt", [128, 1024], BF16)
        Vt = fw.sb("ck_Vt", [128, 1024], BF16)
        PC = fw.sb("ck_PC", [64, 16], F32)
        ytile = Rot([fw.sb("ck_y", [128, 1024], F32) for _ in range(2)])
        T4s = [fw.sb("ck_T4", [64, 4, 128], BF16) for _ in range(HG)]
        G1s = [fw.sb("ck_G1", [128, 2, 128], BF16) for _ in range(HG)]
        G2s = [fw.sb("ck_G2", [128, 2, 128], BF16) for _ in range(HG)]
        NMs = [[fw.sb("ck_NM", [128, 2, 128], BF16) for _ in range(HG)] for _ in range(2)]
        Ms = [[fw.sb("ck_M", [128, 128], BF16) for _ in range(HG)] for _ in range(1)]
        Zs = [[fw.sb("ck_Z", [128, 128], BF16) for _ in range(HG)] for _ in range(2)]
        Yvs = [fw.sb("ck_Yv", [128, 64], F32) for _ in range(HG)]
        Gps = [fw.sb("ck_Gp", [64, 64], F32) for _ in range(HG)]
        WTs = [fw.sb("ck_WT", [64, 128], BF16) for _ in range(HG)]
        Us = [fw.sb("ck_U", [128, 64], BF16) for _ in range(HG)]
        tHs = [fw.sb("ck_tH", [64, 64], F32) for _ in range(HG)]
        banks = [fw.ps("ck_pp", [128, 512], F32) for _ in range(7)]
        for b_ in banks:
            b_.buf.excl = True
        pp = Rot([V(b_.buf, b_.ap[:, hh * 256:(hh + 1) * 256]) for hh in range(2) for b_ in banks])
        bankb = fw.ps("ck_pb", [128, 1024], BF16)
        bankb.buf.excl = True
        pb = Rot([V(bankb.buf, bankb.ap[:, hh * 512:(hh + 1) * 512]) for hh in range(2)])
        ev = Rot(["act", "dve"])
        for n in range(NB):
            X = inb.next()
            fw.dma("sp", X.re("p a c -> p (a c)"), rowsTM_blk[n])
            nkk, ew, kka, km, r_, v_ = (X[:, j, :] for j in range(6))
            for c4 in range(4):
                cs = slice(c4 * 256, (c4 + 1) * 256)
                p = pp.next()
                fw.mm(p, triU, ew[:, cs])
                fw.act(Pinv[:, cs], p, AF.Exp)
                fw.act(Pt[:, cs], p, AF.Exp, scale=-1.0)
                fw.tt("dve", Lt[:, cs], ew[:, cs], p, ALU.subtract)
                fw.act(Pm1[:, cs], Lt[:, cs], AF.Exp)
            pc = pp.next()
            for h in range(16):
                fw.mm(pc[0:64, h:h + 1], ew[:, h * 64:(h + 1) * 64], ones)
            fw.act(PC, pc[0:64, 0:16], AF.Exp, scale=-1.0)
            fw.tt("dve", At, nkk, Pm1, ALU.mult)
            fw.tt("pool", Bt, kka, Pinv, ALU.mult)
            fw.tt("pool", Kt, km, Pinv, ALU.mult)
            fw.tt("dve", Rt, r_, Pt, ALU.mult)
            fw.copy("act", Vt, v_)
            y = ytile.next()
            for g0 in range(0, 16, HG):
                hs = list(range(g0, g0 + HG))
                for h in hs:
                    i = h % HG
                    sl = slice(h * 64, (h + 1) * 64)
                    pT = pb.next()
                    pv = pT[0:64, :].re("p (a t) -> p a t", a=4)
                    for j, src in enumerate((At, Rt, Bt, Kt)):
                        fw.tr(pv[:, j, :], src[:, sl], C.ident)
                    T4 = T4s[i]
                    fw.copy("act", T4, pv)
                    ar = T4[:, 0:2, :].re("p a t -> p (a t)")
                    p1 = pp.next()
                    fw.mm(p1, T4[:, 2, :], ar)
                    p2 = pp.next()
                    fw.mm(p2, T4[:, 3, :], ar)
                    p3 = pp.next()
                    fw.mm(p3[:, 0:128], T4[:, 0, :], T4[:, 2, :])
                    fw.tt("dve", G1s[i], p1.re("p (a t) -> p a t", a=2), mk2, ALU.mult)
                    fw.tt("dve", G2s[i], p2.re("p (a t) -> p a t", a=2), mk2, ALU.mult)
                    fw.tt("dve", Ms[0][i], p3[:, 0:128], mSL, ALU.mult)
                    p4 = pp.next()
                    fw.mm(p4[:, 0:64], G2s[i][:, 0, :], Vt[:, sl])
                    fw.mm(p4[:, 64:128], G2s[i][:, 1, :], Vt[:, sl])
                    fw.copy("act", Zs[0][i][:, 0:64], At[:, sl])
                    fw.copy("act", Zs[0][i][:, 64:128], p4[:, 0:64])
                    fw.copy("dve", Yvs[i], p4[:, 64:128])
                    p5 = pp.next()
                    fw.mm(p5[0:64, 0:64], Kt[:, sl], Vt[:, sl])
                    fw.ts("dve", Gps[i], p5[0:64, 0:64], PC[:, h:h + 1], ALU.mult)
                for k in range(7):
                    for h in hs:
                        i = h % HG
                        Nk = G1s[i][:, 0, :] if k == 0 else NMs[k % 2][i][:, 0, :]
                        Mk = Ms[0][i] if k == 0 else NMs[k % 2][i][:, 1, :]
                        Zk = Zs[k % 2][i]
                        pz = pp.next()
                        fw.mm(pz[:, 0:128], Nk, Zk)
                        fw.tt("dve", Zs[(k + 1) % 2][i], Zk, pz[:, 0:128], ALU.add)
                        if k < 6:
                            pn = pp.next()
                            fw.mm(pn[:, 0:128], Mk, Nk)
                            fw.mm(pn[:, 128:256], Nk, Mk)
                            fw.copy("act" if (h % 4) else "dve", NMs[(k + 1) % 2][i].re("p a t -> p (a t)"), pn)
                for h in hs:
                    i = h % HG
                    sl = slice(h * 64, (h + 1) * 64)
                    Zf = Zs[1][i]
                    pT = pb.next()
                    fw.tr(pT[0:64, 0:128], Zf[:, 0:64], C.ident)
                    fw.copy("act", WTs[i], pT[0:64, 0:128])
                    pu = pp.next()
                    fw.mm(pu[:, 0:64], WTs[i], Hb[h])
                    fw.tt("dve", Us[i], pu[:, 0:64], Zf[:, 64:128], ALU.add)
                    py = pp.next()
                    fw.mm(py[:, 0:64], T4s[i][:, 1, :], Hb[h], start=True, stop=False)
                    fw.mm(py[:, 0:64], G1s[i][:, 1, :], Us[i], start=False, stop=True)
                    fw.tt("dve", y[:, sl], py[:, 0:64], Yvs[i], ALU.add)
                    ph = pp.next()
                    fw.mm(ph[0:64, 0:64], Bt[:, sl], Us[i])
                    fw.tt("dve", tHs[i], ph[0:64, 0:64], H32[h], ALU.add)
                    fw.stt("dve", H32[h], tHs[i], PC[:, h:h + 1], Gps[i], ALU.mult, ALU.add)
                    fw.copy("act", Hb[h], H32[h])
            fw.dma("sp", yTM_blk[n], y)


def rwkv_post_phase(fw, C, I, yTM_blk, gS_blk, bonS_blk, yA_blk):
    with fw.phase():
        gng = load_bc(fw, "rp_g", I["rwkv_gn_g"][0], 1024)
        gnb = load_bc(fw, "rp_b", I["rwkv_gn_b"][0], 1024)
        yt = Rot([fw.sb("rp_y", [128, 16, 64], F32) for _ in range(2)])
        gt = Rot([fw.sb("rp_gt", [128, 1024], F32) for _ in range(2)])
        bt = Rot([fw.sb("rp_bt", [128, 1024], F32) for _ in range(2)])
        sq = fw.sb("rp_sq", [128, 16, 64], F32)
        st = fw.sb("rp_st", [128, 4, 16], F32)
        ob = Rot([fw.sb("rp_o", [128, 1024], BF16) for _ in range(2)])
        F2 = lambda v: v.re("p h d -> p (h d)")
        for n in range(NB):
            y = yt.next()
            g_ = gt.next()
            b_ = bt.next()
            fw.dma("sp", y.re("p h d -> p (h d)"), yTM_blk[n])
            fw.dma("act", g_, gS_blk[n])
            fw.dma("act", b_, bonS_blk[n])
            fw.red("dve", st[:, 0, :], y)
            fw.ts("dve", st[:, 0, :], st[:, 0, :], 1.0 / 64, ALU.mult)
            fw.tt("dve", y, y, st[:, 0, :].un(2).bc([128, 16, 64]), ALU.subtract)
            fw.tt("pool", sq, y, y, ALU.mult)
            fw.red("dve", st[:, 1, :], sq)
            rstd_from_ss(fw, st[:, 1, :], 64, 64e-5, st[:, 2, :], st[:, 3, :])
            fw.tt("dve", y, y, st[:, 3, :].un(2).bc([128, 16, 64]), ALU.mult)
            fw.tt("pool", F2(y), F2(y), gng, ALU.mult)
            fw.tt("pool", F2(y), F2(y), gnb, ALU.add)
            fw.tt("dve", F2(y), F2(y), b_, ALU.add)
            o_ = ob.next()
            fw.tt("dve", o_, F2(y), g_, ALU.mult)
            fw.dma("sp", yA_blk[n][:, 1024:2048], o_)


def causal_attn(fw, C, heads, dv, pss, pso, E, masks_eng):
    groups = []
    for hi in range(len(heads)):
        for qi in range(NB):
            for sg in range(0, qi + 1, 4):
                groups.append((hi, qi, sg, min(4, qi + 1 - sg)))
    loaded = {}

    def qk(g):
        hi, qi, sg, n = g
        if hi not in loaded:
            loaded[hi] = heads[hi]["load"]()
        qT, kT, V1 = loaded[hi]
        ps_ = pss.next()
        for j in range(n):
            dg = (sg + j == qi)
            fw.mm(ps_[:, j * 128:(j + 1) * 128], kT[:, (sg + j) * 128:(sg + j + 1) * 128], qT[:, qi * 128:(qi + 1) * 128],
                  start=True, stop=not dg)
            if dg:
                fw.mm(ps_[:, j * 128:(j + 1) * 128], C.ident, C.negU4[:, 0, :], start=False, stop=True)
        return ps_
    pend = qk(groups[0])
    po = None
    for gi, g in enumerate(groups):
        ps_ = pend
        if gi + 1 < len(groups):
            pend = qk(groups[gi + 1])
        hi, qi, sg, n = g
        qT, kT, V1 = loaded[hi]
        if sg == 0:
            po = pso.next()
        e_ = E.next()
        fw.act(e_[:, 0:n * 128], ps_[:, 0:n * 128], AF.Exp)
        for j in range(n):
            fw.mm(po[:, 0:dv + 1], e_[:, j * 128:(j + 1) * 128], V1[:, sg + j, 0:dv + 1], start=(sg + j == 0), stop=(sg + j == qi))
        if sg + n - 1 == qi:
            heads[hi]["on_out"](qi, po)


def mla_prep_phase(fw, C, I, u_blk, qTS, kTS, V1S_blk):
    sc = 96 ** -0.5
    with fw.phase():
        gcq = load_bc(fw, "ml_gcq", I["mla_cq_norm"][0], 512)
        gckv = load_bc(fw, "ml_gckv", I["mla_ckv_norm"][0], 256)
        gqn = load_bc(fw, "ml_gqn", I["mla_q_nope_norm"][0], 64, mul=sc)
        gkn = load_bc(fw, "ml_gkn", I["mla_k_nope_norm"][0], 64)
        gqr = load_bc(fw, "ml_gqr", I["mla_q_rope_norm"][0], 32, mul=sc)
        gkr = load_bc(fw, "ml_gkr", I["mla_k_rope_norm"][0], 32)
        wuq = fw.sb("ml_wuq", [128, 4, 1536], BF16)
        wukv = fw.sb("ml_wukv", [128, 2, 2048], BF16)
        fw.dma("pool", wuq, wview(I["mla_w_uq"][0], 0, 1536))
        fw.dma("pool", wukv, wview(I["mla_w_ukv"][0], 0, 2048))
        ut = Rot([fw.sb("ml_u", [128, 800], F32) for _ in range(2)])
        cb = fw.sb("ml_cb", [128, 768], BF16)
        junk = fw.sb("ml_junk", [128, 512], BF16)
        st = fw.sb("ml_st", [128, 8], F32)
        cT = fw.sb("ml_cT", [128, 6, 128], BF16)
        qfull = fw.sb("ml_qfull", [128, 16, 96], F32)
        kvfull = fw.sb("ml_kvfull", [128, 16, 128], F32)
        tq = fw.sb("ml_tq", [128, 16, 64], F32)
        tr_ = fw.sb("ml_tr", [128, 16, 32], F32)
        tkr = fw.sb("ml_tkr", [128, 1, 32], F32)
        kpe = fw.sb("ml_kpe", [128, 1, 32], BF16)
        qf = fw.sb("ml_qf", [128, 16, 96], BF16)
        kf = fw.sb("ml_kf", [128, 16, 96], BF16)
        V1 = Rot([fw.sb("ml_V1", [128, 16, 66], BF16) for _ in range(2)])
        qTt = Rot([fw.sb("ml_qTt", [96, 16, 128], BF16) for _ in range(2)])
        kTt = Rot([fw.sb("ml_kTt", [96, 16, 128], BF16) for _ in range(2)])
        for v_ in V1.items:
            fw.memset("dve", v_, 1.0)
        psT = Rot([fw.ps("ml_psT", [128, 8, 128], BF16) for _ in range(2)])
        psm = Rot([fw.ps("ml_psm", [128, 512], F32) for _ in range(4)])
        for n in range(NB):
            u = ut.next()
            fw.dma("sp", u, u_blk[n][:, 0:800])
            for (c0, c1, g_, j0) in ((0, 512, gcq, 0), (512, 768, gckv, 3)):
                w = c1 - c0
                fw.act(junk[:, 0:w], u[:, c0:c1], AF.Square, accum=st[:, j0:j0 + 1])
                rstd_from_ss(fw, st[:, j0:j0 + 1], w, EPS, st[:, j0 + 1:j0 + 2], st[:, j0 + 2:j0 + 3])
                fw.stt("dve", cb[:, c0:c1], u[:, c0:c1], st[:, j0 + 2:j0 + 3], g_, ALU.mult, ALU.mult)
            p = psT.next()
            for j in range(6):
                fw.tr(p[:, j, :], cb[:, j * 128:(j + 1) * 128], C.ident)
            fw.copy("dve", cT, p[:, 0:6, :])
            for cbk in range(3):
                pq = psm.next()
                for k in range(4):
                    fw.mm(pq, cT[:, k, :], wuq[:, k, cbk * 512:(cbk + 1) * 512], start=(k == 0), stop=(k == 3))
                fw.copy("act", qfull.re("p h d -> p (h d)")[:, cbk * 512:(cbk + 1) * 512], pq)
            for cbk in range(4):
                pk = psm.next()
                for k in range(2):
                    fw.mm(pk, cT[:, 4 + k, :], wukv[:, k, cbk * 512:(cbk + 1) * 512], start=(k == 0), stop=(k == 1))
                fw.copy("act", kvfull.re("p h d -> p (h d)")[:, cbk * 512:(cbk + 1) * 512], pk)
            cs32 = (C.cos32[:, n, :], C.sin32[:, n, :])
            head_norm(fw, qfull[:, :, 0:64], gqn, qf[:, :, 0:64], 16, 64)
            head_norm(fw, qfull[:, :, 64:96], gqr, tr_, 16, 32)
            rope(fw, tr_, qf[:, :, 64:96], cs32[0], cs32[1], 16, 32)
            head_norm(fw, kvfull[:, :, 0:64], gkn, kf[:, :, 0:64], 16, 64)
            head_norm(fw, u[:, 768:800].re("p (h d) -> p h d", h=1), gkr, tkr, 1, 32)
            rope(fw, tkr, kpe, cs32[0], cs32[1], 1, 32)
            fw.copy("dve", kf[:, :, 64:96], kpe.bc([128, 16, 32]))
            V1c = V1.next()
            fw.copy("act", V1c[:, :, 0:64], kvfull[:, :, 64:128])
            fw.dma("sp", V1S_blk[n], V1c.re("p h d -> p (h d)"))
            for (src, dstS, rot) in ((qf, qTS, qTt), (kf, kTS, kTt)):
                tt_ = rot.next()
                for h0 in (0, 8):
                    p = psT.next()
                    for h in range(8):
                        fw.tr(p[0:96, h, :], src[:, h0 + h, :], C.ident)
                    fw.copy("dve", tt_[:, h0:h0 + 8, :], p[0:96, :, :])
                fw.dma("act", dstS[:, :, n * 128:(n + 1) * 128].re("h d t -> d h t"), tt_)


def mla_attn_phase(fw, C, qTS, kTS, V1S, yA):
    with fw.phase():
        qT = Rot([fw.sb("ma_qT", [96, S], BF16) for _ in range(2)])
        kT = Rot([fw.sb("ma_kT", [96, S], BF16) for _ in range(2)])
        V1 = Rot([fw.sb("ma_V1", [128, NB, 66], BF16) for _ in range(2)])
        E = Rot([fw.sb("ma_E", [128, 512], BF16) for _ in range(5)])
        yo = Rot([fw.sb("ma_yo", [128, NB, 64], BF16) for _ in range(2)])
        rec = fw.sb("ma_rec", [128, 2], F32)
        pss = Rot([fw.ps("ma_pss", [128, 512], F32) for _ in range(4)])
        pso = Rot([fw.ps("ma_pso", [128, 512], F32) for _ in range(3)])
        heads = []
        for h in range(16):
            def load(h=h):
                q_ = qT.next()
                k_ = kT.next()
                v_ = V1.next()
                fw.dma("sp", q_, qTS[h])
                fw.dma("sp", k_, kTS[h])
                fw.dma("sp", v_, V1S[:, h * 66:(h + 1) * 66].re("(n p) d -> p n d", p=128))
                return (q_, k_, v_)
            st_ = {}

            def on_out(qi, po, h=h, st_=st_):
                if qi == 0:
                    st_["y"] = yo.next()
                y_ = st_["y"]
                fw.recip(rec[:, 0:1], po[:, 64:65])
                fw.ts("dve", y_[:, qi, :], po[:, 0:64], rec[:, 0:1], ALU.mult)
                if qi == NB - 1:
                    fw.dma("act", yA[:, h * 64:(h + 1) * 64].re("(n p) d -> p n d", p=128), y_)
            heads.append({"load": load, "on_out": on_out})
        causal_attn(fw, C, heads, 64, pss, pso, E, "pool")


def diff_prep_phase(fw, C, I, u_blk, qTS, kTS, V1S_blk):
    with fw.phase():
        gq = load_bc(fw, "df_gq", I["diff_q_norm"][0], 64, mul=0.125)
        gk = load_bc(fw, "df_gk", I["diff_k_norm"][0], 64)
        ut = Rot([fw.sb("df_u", [128, 3072], F32) for _ in range(2)])
        tn_r = Rot([fw.sb("df_tn", [128, 16, 64], F32) for _ in range(2)])
        qf_r = Rot([fw.sb("df_qf", [128, 16, 64], BF16) for _ in range(2)])
        kf_r = Rot([fw.sb("df_kf", [128, 16, 64], BF16) for _ in range(2)])
        V1 = Rot([fw.sb("df_V1", [128, 8, 130], BF16) for _ in range(2)])
        qTt = Rot([fw.sb("df_qTt", [64, 16, 128], BF16) for _ in range(2)])
        kTt = Rot([fw.sb("df_kTt", [64, 16, 128], BF16) for _ in range(2)])
        for v_ in V1.items:
            fw.memset("dve", v_, 1.0)
        psT = Rot([fw.ps("df_psT", [128, 8, 128], BF16) for _ in range(2)])
        for n in range(NB):
            u = ut.next()
            fw.dma("sp", u, u_blk[n][:, 800:3872])
            cs = (C.cos64[:, n, :], C.sin64[:, n, :])
            qf, kf = qf_r.next(), kf_r.next()
            tn = tn_r.next()
            head_norm(fw, u[:, 0:1024].re("p (h d) -> p h d", h=16), gq, tn, 16, 64)
            rope(fw, tn, qf, cs[0], cs[1], 16, 64, eng="dve")
            tn = tn_r.next()
            head_norm(fw, u[:, 1024:2048].re("p (h d) -> p h d", h=16), gk, tn, 16, 64)
            rope(fw, tn, kf, cs[0], cs[1], 16, 64)
            V1c = V1.next()
            fw.copy("act", V1c[:, :, 0:128], u[:, 2048:3072].re("p (h d) -> p h d", h=8))
            fw.dma("sp", V1S_blk[n], V1c.re("p h d -> p (h d)"))
            for (src, dstS, rot) in ((qf, qTS, qTt), (kf, kTS, kTt)):
                tt_ = rot.next()
                for h0 in (0, 8):
                    p = psT.next()
                    for h in range(8):
                        fw.tr(p[0:64, h, :], src[:, h0 + h, :], C.ident)
                    fw.copy("dve", tt_[:, h0:h0 + 8, :], p[0:64, :, :])
                fw.dma("act", dstS[:, :, n * 128:(n + 1) * 128].re("h d t -> d h t"), tt_)


def diff_attn_phase(fw, C, I, qTS, kTS, V1S, yA):
    lam_init = 0.8 - 0.6 * math.exp(-0.3 * 1)
    with fw.phase():
        lt = [load_bc(fw, "df_l%d" % j, I[nm][0], 64) for j, nm in enumerate(("diff_lq1", "diff_lk1", "diff_lq2", "diff_lk2"))]
        ls = fw.sb("df_ls", [128, 4], F32)
        fw.tt("dve", lt[0], lt[0], lt[1], ALU.mult)
        fw.tt("dve", lt[2], lt[2], lt[3], ALU.mult)
        fw.red("dve", ls[:, 0:1], lt[0])
        fw.red("dve", ls[:, 1:2], lt[2])
        fw.act(ls[:, 0:2], ls[:, 0:2], AF.Exp)
        fw.tt("dve", ls[:, 2:3], ls[:, 0:1], ls[:, 1:2], ALU.subtract)
        fw.ts("dve", ls[:, 3:4], ls[:, 2:3], lam_init, ALU.add, -1.0, ALU.mult)
        gsub = load_bc(fw, "df_gs", I["diff_subln"][0], 128, mul=(1.0 - lam_init))
        qT = Rot([fw.sb("da_qT", [64, S], BF16) for _ in range(2)])
        kT = Rot([fw.sb("da_kT", [64, S], BF16) for _ in range(2)])
        V1 = Rot([fw.sb("da_V1", [128, NB, 130], BF16) for _ in range(2)])
        E = Rot([fw.sb("da_E", [128, 512], BF16) for _ in range(5)])
        o1 = fw.sb("da_o1", [128, NB, 128], F32)
        o2 = fw.sb("da_o2", [128, 128], F32)
        sq = fw.sb("da_sq", [128, 128], F32)
        yo = Rot([fw.sb("da_yo", [128, NB, 128], BF16) for _ in range(2)])
        rec = fw.sb("da_rec", [128, 4], F32)
        pss = Rot([fw.ps("da_pss", [128, 512], F32) for _ in range(4)])
        pso = Rot([fw.ps("da_pso", [128, 512], F32) for _ in range(3)])
        heads = []
        vcur = {}
        for h in range(8):
            for m in range(2):
                def load(h=h, m=m):
                    if m == 0:
                        v_ = V1.next()
                        fw.dma("sp", v_, V1S[:, h * 130:(h + 1) * 130].re("(n p) d -> p n d", p=128))
                        vcur[h] = v_
                    q_ = qT.next()
                    k_ = kT.next()
                    fw.dma("sp", q_, qTS[2 * h + m])
                    fw.dma("sp", k_, kTS[2 * h + m])
                    return (q_, k_, vcur[h])
                st_ = vcur

                def on_out(qi, po, h=h, m=m):
                    fw.recip(rec[:, 0:1], po[:, 128:129])
                    if m == 0:
                        fw.ts("dve", o1[:, qi, :], po[:, 0:128], rec[:, 0:1], ALU.mult)
                    else:
                        if qi == 0:
                            vcur[("y", h)] = yo.next()
                        y_ = vcur[("y", h)]
                        fw.tt("dve", rec[:, 1:2], rec[:, 0:1], ls[:, 3:4], ALU.mult)
                        fw.stt("dve", o2, po[:, 0:128], rec[:, 1:2], o1[:, qi, :], ALU.mult, ALU.add)
                        fw.tt("dve", sq, o2, o2, ALU.mult)
                        fw.red("dve", rec[:, 2:3], sq)
                        rstd_from_ss(fw, rec[:, 2:3], 128, EPS, rec[:, 3:4], rec[:, 2:3])
                        fw.stt("dve", y_[:, qi, :], o2, rec[:, 2:3], gsub, ALU.mult, ALU.mult)
                        if qi == NB - 1:
                            fw.dma("act", yA[:, 1024 + h * 128:1024 + (h + 1) * 128].re("(n p) d -> p n d", p=128), y_)
                heads.append({"load": load, "on_out": on_out})
        causal_attn(fw, C, heads, 128, pss, pso, E, "pool")


WNAMES = ['ffn1_norm', 'ffn1_w_gate', 'ffn1_w_up', 'ffn1_w_down', 'mix_norm', 'ab_w_in', 'ab_w_out', 'swa_q_norm',
          'swa_k_norm', 'swa_sinks', 'rwkv_mu', 'rwkv_w0', 'rwkv_w2', 'rwkv_a0', 'rwkv_a2', 'rwkv_g2', 'rwkv_k_k',
          'rwkv_k_a', 'rwkv_r_k', 'rwkv_gn_g', 'rwkv_gn_b', 'cd_w_in', 'cd_w_out', 'mla_cq_norm', 'mla_ckv_norm',
          'mla_w_uq', 'mla_w_ukv', 'mla_q_nope_norm', 'mla_k_nope_norm', 'mla_q_rope_norm', 'mla_k_rope_norm',
          'diff_q_norm', 'diff_k_norm', 'diff_lq1', 'diff_lk1', 'diff_lq2', 'diff_lk2', 'diff_subln', 'memx_norm',
          'memx_w_q', 'memx_q_norm', 'memx_w_o', 'mem_norm', 'mem_w_kv', 'mem_k_norm', 'ffn2_norm', 'ffn2_w_gate',
          'ffn2_w_up', 'ffn2_w_down']


class Consts:
    pass


def build(shapes, phases=None, debug=()):
    nc = bass.Bass("TRN2", target_bir_lowering=False)
    fw = FW(nc)
    I = {}
    I["x"] = fw.dram("x", [S, D], F32, kind="ExternalInput")
    I["mem"] = fw.dram("mem", [256, D], F32, kind="ExternalInput")
    I["pos"] = fw.dram("pos", [128, NB], I32, kind="ExternalInput")
    for nm in WNAMES:
        I[nm] = fw.dram(nm, list(shapes[nm]), F32, kind="ExternalInput")
    I["c_ident"] = fw.dram("c_ident", [128, 128], BF16, kind="ExternalInput")
    I["c_masku"] = fw.dram("c_masku", [128, 128], BF16, kind="ExternalInput")
    I["c_maskl"] = fw.dram("c_maskl", [128, 128], BF16, kind="ExternalInput")
    I["c_inv64"] = fw.dram("c_inv64", [128, 32], F32, kind="ExternalInput")
    I["c_inv32"] = fw.dram("c_inv32", [128, 16], F32, kind="ExternalInput")
    out = fw.dram("out", [S, D], F32, kind="ExternalOutput")

    def scratch(name, shape, dt):
        return fw.dram(name, shape, dt, kind=("ExternalOutput" if name in debug else "Internal"))

    def blks(v, n=NB):
        return [V(Buf(v.buf.name + "_b%d" % i), v.ap[i * 128:(i + 1) * 128]) for i in range(n)]
    xs = scratch("xs", [S, D], F32)
    xs_blk = blks(xs)
    u = scratch("u", [S, 4896], F32)
    u_blk = blks(u)
    yA = scratch("yA", [S, D], BF16)
    yA_blk = blks(yA)
    rowsTM = scratch("rowsTM", [S, 6 * 1024], F32)
    yTM = scratch("yTM", [S, 1024], F32)
    gS = scratch("gS", [S, 1024], F32)
    bonS = scratch("bonS", [S, 1024], F32)
    qTS = scratch("qTS", [16, 96, S], BF16)
    kTS = scratch("kTS", [16, 96, S], BF16)
    mV1S = scratch("mV1S", [S, 16 * 66], BF16)
    dqTS = scratch("dqTS", [16, 64, S], BF16)
    dkTS = scratch("dkTS", [16, 64, S], BF16)
    dV1S = scratch("dV1S", [S, 8 * 130], BF16)
    C = Consts()
    ph = phases

    def on(p):
        return ph is None or p in ph

    setup_phase(fw, C, I)
    fw.memset("dve", C.cst[:, 3:4], -0.5)
    x_blk = blks(I["x"])
    for i in range(NB):
        fw.dma("sp" if i % 2 else "act", xs_blk[i], x_blk[i])
    for layer in range(2):
        L = "L%d" % layer
        if on(L + "ffn1"):
            ffn_phase(fw, C, xs_blk, I["ffn1_norm"][layer], I["ffn1_w_gate"][layer], I["ffn1_w_up"][layer], I["ffn1_w_down"][layer])
        if layer == 0:
            if on("L0win"):
                mk, ev = evac_store(u_blk)
                linear_phase(fw, C, xs_blk, D, I["mix_norm"][0], I["ab_w_in"][0], 4896, ev, extra=mk)
                fw.barrier()
            if on("L0swa"):
                swa_phase(fw, C, I, u_blk, yA_blk)
            if on("L0rwkv"):
                rwkv_prep_phase(fw, C, I, u, u_blk, blks(rowsTM), blks(gS), blks(bonS))
                rwkv_chunk_phase(fw, C, blks(rowsTM), blks(yTM))
                rwkv_post_phase(fw, C, I, blks(yTM), blks(gS), blks(bonS), yA_blk)
            if on("L0wout"):
                mk, ev = evac_resid(xs_blk)
                linear_phase(fw, C, yA_blk, D, None, I["ab_w_out"][0], D, ev, cast_only=True, src_bf16=True, extra=mk)
        else:
            if on("L1win"):
                mk, ev = evac_store(u_blk)
                linear_phase(fw, C, xs_blk, D, I["mix_norm"][1], I["cd_w_in"][0], 3872, ev, extra=mk)
            if on("L1mla"):
                mla_prep_phase(fw, C, I, u_blk, qTS, kTS, blks(mV1S))
                mla_attn_phase(fw, C, qTS, kTS, mV1S, yA)
            if on("L1diff"):
                diff_prep_phase(fw, C, I, u_blk, dqTS, dkTS, blks(dV1S))
                diff_attn_phase(fw, C, I, dqTS, dkTS, dV1S, yA)
            if on("L1wout"):
                mk, ev = evac_resid(xs_blk)
                linear_phase(fw, C, blks(yA), D, None, I["cd_w_out"][0], D, ev, cast_only=True, src_bf16=True, extra=mk)
        if on(L + "memx"):
            memx_phase(fw, C, I, layer, xs_blk)
        if on(L + "ffn2"):
            ffn_phase(fw, C, xs_blk, I["ffn2_norm"][layer], I["ffn2_w_gate"][layer], I["ffn2_w_up"][layer], I["ffn2_w_down"][layer])
    fw.barrier()
    toks = []
    for i in range(NB):
        toks.append(fw.dma("sp" if i % 2 else "act", V(Buf("out%d" % i), out.ap[i * 128:(i + 1) * 128]), xs_blk[i]))
    for t in toks:
        fw._wait("sp", t)
    fw.barrier()
    return nc


def host_consts():
    bf = ml_dtypes.bfloat16
    s = np.arange(128)[:, None]
    q = np.arange(128)[None, :]
    inv64 = (1.0 / (10000.0 ** (np.arange(0, 64, 2, dtype=np.float32) / 64))).astype(np.float32)
    inv32 = (1.0 / (10000.0 ** (np.arange(0, 32, 2, dtype=np.float32) / 32))).astype(np.float32)
    return {
        "c_ident": np.eye(128, dtype=np.float32).astype(bf),
        "c_masku": (q >= s).astype(np.float32).astype(bf),
        "c_maskl": (s > q).astype(np.float32).astype(bf),
        "c_inv64": np.ascontiguousarray(np.broadcast_to(inv64[None, :], (128, 32))),
        "c_inv32": np.ascontiguousarray(np.broadcast_to(inv32[None, :], (128, 16))),
    }


def make_in_maps(inputs, cores):
    cst = host_consts()
    maps = []
    for b in cores:
        m = {"x": np.ascontiguousarray(inputs["x"][b], dtype=np.float32),
             "mem": np.ascontiguousarray(inputs["mem"][b], dtype=np.float32),
             "pos": np.ascontiguousarray(np.asarray(inputs["positions"][b]).astype(np.int32).reshape(NB, 128).T)}
        for nm in WNAMES:
            m[nm] = np.ascontiguousarray(inputs[nm], dtype=np.float32)
        m.update(cst)
        maps.append(m)
    return maps


def kernel(**inputs):
    inputs = {k: np.asarray(v) for k, v in inputs.items()}
    shapes = {nm: inputs[nm].shape for nm in WNAMES}
    nc = build(shapes)
    maps = make_in_maps(inputs, list(range(8)))
    res = run_bass_kernel_spmd(nc, maps, core_ids=list(range(8)))
    return np.stack([np.asarray(r["out"], dtype=np.float32) for r in res.results], axis=0)
```
